# Optimizing a Trainium2 kernel written in Bass

```python
import math
import jax, jax.numpy as jnp
from jax import lax
import numpy as np

D_MODEL = 2048
BATCH = 8
SEQ = 2048
DEPTH = 2
DEC_BATCH = 128
DEC_SEQ = 8
PAST_LEN = 8192
PAGE_SIZE = 128

H_A = 4
DK_A = 128
DV_A = 256
W_A = H_A * DV_A
RET_CHUNK = 128
ROPE_BASE = 10000.0
N_POOL_GROUPS = 4
W_B = 1024
GW_B = W_B // N_POOL_GROUPS
POOL_WINDOWS = (2, 4, 8, 16)
POOL_BUF = 15
H_C = 16
KV_C = 4
G_C = H_C // KV_C
HD_C = 64
W_C = H_C * HD_C
WINDOW = 128
ATT_BLOCK = 128
NUM_BUCKETS = 32
MAX_DISTANCE = 128
SPLITS = (H_A * DK_A, H_A * DK_A, W_A, W_A,
          W_B, W_B,
          H_C * HD_C, KV_C * HD_C, KV_C * HD_C, W_C,
          D_MODEL, D_MODEL, D_MODEL)
N_IN = 13824
LN_EPS = 1e-5
RMS_EPS = 1e-6

kernel_name = 'hybrid_retention_pool_swa_deepnorm_step'


def layer_norm(x, g, b):
    xf = x.astype(jnp.float32)
    mu = jnp.mean(xf, -1, keepdims=True)
    var = jnp.mean(jnp.square(xf - mu), -1, keepdims=True)
    return ((xf - mu) * lax.rsqrt(var + LN_EPS) * g.astype(jnp.float32) + b.astype(jnp.float32)).astype(x.dtype)


def rotary(x, pos):
    half = x.shape[-1] // 2
    inv = ROPE_BASE ** (-jnp.arange(half, dtype=jnp.float32) / half)
    ang = pos[:, None] * inv[None, :]
    cos = jnp.cos(ang)[None, :, None, :]
    sin = jnp.sin(ang)[None, :, None, :]
    x1, x2 = x[..., :half], x[..., half:]
    return jnp.concatenate([x1 * cos - x2 * sin, x1 * sin + x2 * cos], -1)


def retention(q, k, v, s0, pos0):
    B, T = q.shape[0], q.shape[1]
    C = min(RET_CHUNK, T)
    n = T // C
    pos = pos0 + jnp.arange(T, dtype=jnp.float32)
    q = rotary(q, pos)
    k = rotary(k, pos) * (DK_A ** -0.5)
    lg = jnp.log1p(-jnp.exp2(-5.0 - jnp.arange(H_A, dtype=jnp.float32)))
    idx = jnp.arange(C, dtype=jnp.float32)
    diff = idx[:, None] - idx[None, :]
    dmask = jnp.where(diff >= 0, jnp.exp(lg[:, None, None] * jnp.maximum(diff, 0.0)), 0.0)
    q_decay = jnp.exp(lg[None, :] * (idx[:, None] + 1.0))[None, :, :, None]
    k_decay = jnp.exp(lg[None, :] * (C - 1.0 - idx[:, None]))[None, :, :, None]
    chunk_decay = jnp.exp(lg * C)[None, :, None, None]

    def to_chunks(a):
        return a.reshape(B, n, C, *a.shape[2:]).swapaxes(0, 1)

    def step(s, inp):
        qc, kc, vc = inp
        sc = jnp.einsum('bihd,bjhd->bhij', qc, kc) * dmask
        o = jnp.einsum('bhij,bjhv->bihv', sc, vc) + jnp.einsum('bihd,bhdv->bihv', qc, s) * q_decay
        s = s * chunk_decay + jnp.einsum('bjhd,bjhv->bhdv', kc * k_decay, vc)
        return s, o

    s, o = lax.scan(step, s0, (to_chunks(q), to_chunks(k), to_chunks(v)))
    o = o.swapaxes(0, 1).reshape(B, T, H_A, DV_A)
    return o, s


def multi_scale_pool(u_ext, pos0, T):
    c = jnp.cumsum(u_ext, axis=1)
    c = jnp.concatenate([jnp.zeros_like(c[:, :1]), c], axis=1)
    pos = pos0 + jnp.arange(T)
    end = c[:, POOL_BUF + 1:]
    outs = []
    for g, w in enumerate(POOL_WINDOWS):
        lo, hi = g * GW_B, (g + 1) * GW_B
        start = c[:, POOL_BUF + 1 - w: POOL_BUF + 1 - w + T, lo:hi]
        cnt = jnp.minimum(pos + 1, w).astype(jnp.float32)
        outs.append((end[:, :, lo:hi] - start) / cnt[None, :, None])
    return jnp.concatenate(outs, -1) - u_ext[:, POOL_BUF:]


def t5_bucket(dist):
    max_exact = NUM_BUCKETS // 2
    d = dist.astype(jnp.float32)
    large = max_exact + (jnp.log(jnp.maximum(d, 1.0) / max_exact) / math.log(MAX_DISTANCE / max_exact)
                         * (NUM_BUCKETS - max_exact)).astype(jnp.int32)
    large = jnp.minimum(large, NUM_BUCKETS - 1)
    return jnp.where(dist < max_exact, dist, large)


def window_attention(q, k_ext, v_ext, pos0, sinks, rel_bias):
    B, T = q.shape[0], q.shape[1]
    bq = min(ATT_BLOCK, T)
    nb = T // bq
    span = bq + WINDOW
    kidx = (jnp.arange(nb) * bq)[:, None] + jnp.arange(span)[None, :]
    kb = k_ext[:, kidx]
    vb = v_ext[:, kidx]
    qb = q.reshape(B, nb, bq, KV_C, G_C, HD_C)
    s = jnp.einsum('bnqkgd,bnskd->bnkgqs', qb, kb).astype(jnp.float32) * (HD_C ** -0.5)
    dist = jnp.arange(bq)[:, None] + WINDOW - jnp.arange(span)[None, :]
    bias = rel_bias.astype(jnp.float32)[t5_bucket(jnp.maximum(dist, 0))]
    bias = bias.transpose(2, 0, 1).reshape(KV_C, G_C, bq, span)
    kpos = pos0 - WINDOW + kidx
    valid = ((dist >= 0) & (dist < WINDOW))[None] & (kpos >= 0)[:, None, :]
    s = jnp.where(valid[None, :, None, None], s + bias, jnp.finfo(jnp.float32).min)
    sink = sinks.astype(jnp.float32).reshape(KV_C, G_C)[None, None, :, :, None, None]
    m = jnp.maximum(jnp.max(s, -1, keepdims=True), sink)
    p = jnp.exp(s - m)
    p = p / (jnp.sum(p, -1, keepdims=True) + jnp.exp(sink - m))
    o = jnp.einsum('bnkgqs,bnskd->bnqkgd', p.astype(vb.dtype), vb)
    return o.reshape(B, T, W_C)


def mixer_layer(x, pos0, s_ret, win_k, win_v, pool_buf, w_in, w_ret_o, w_pool_map, pool_scale,
                w_pool_o, sinks, w_att_o, w_out, ln_g, ln_b, rel_bias, alpha):
    B, T, _ = x.shape
    dt = x.dtype
    f32 = jnp.float32
    h = x @ w_in
    offs = [int(o) for o in np.cumsum(SPLITS)[:-1]]
    qa, ka, va, ga, ub, gb, qc, kc, vc, gc, ma, mb, mc = jnp.split(h, offs, axis=-1)
    oa, s_new = retention(qa.reshape(B, T, H_A, DK_A).astype(f32), ka.reshape(B, T, H_A, DK_A).astype(f32),
                          va.reshape(B, T, H_A, DV_A).astype(f32), s_ret.astype(f32), pos0)
    oa = oa * lax.rsqrt(jnp.mean(oa * oa, -1, keepdims=True) + RMS_EPS)
    ya = (oa.reshape(B, T, W_A).astype(dt) * jax.nn.silu(ga)) @ w_ret_o
    u_ext = jnp.concatenate([pool_buf.astype(dt), ub], axis=1)
    p = multi_scale_pool(u_ext.astype(f32), pos0, T)
    p = jnp.einsum('btgc,gcd->btgd', p.reshape(B, T, N_POOL_GROUPS, GW_B), w_pool_map.astype(f32))
    p = p.reshape(B, T, W_B) * pool_scale.astype(f32)
    yb = (p.astype(dt) * jax.nn.silu(gb)) @ w_pool_o
    k_ext = jnp.concatenate([win_k.astype(dt), kc.reshape(B, T, KV_C, HD_C)], axis=1)
    v_ext = jnp.concatenate([win_v.astype(dt), vc.reshape(B, T, KV_C, HD_C)], axis=1)
    oc = window_attention(qc.reshape(B, T, H_C, HD_C), k_ext, v_ext, pos0, sinks, rel_bias)
    yc = (oc.astype(dt) * jax.nn.silu(gc)) @ w_att_o
    merged = jax.nn.sigmoid(ma) * ya + jax.nn.sigmoid(mb) * yb + jax.nn.sigmoid(mc) * yc
    y = merged @ w_out
    x_new = layer_norm(alpha * x + y, ln_g, ln_b)
    return x_new, s_new.astype(dt), k_ext[:, -WINDOW:], v_ext[:, -WINDOW:], u_ext[:, -POOL_BUF:]


def setup_inputs(seed: int = 0) -> dict:
    key = jax.random.key(seed)
    ks = jax.random.split(key, 20)
    beta = (8.0 * DEPTH) ** -0.25
    nrm = lambda k, shape: jax.random.normal(k, shape, dtype=jnp.float32)
    win = min(WINDOW, PAST_LEN)
    return {
        'x_prompt': nrm(ks[0], (BATCH, SEQ, D_MODEL)),
        'x_sample': nrm(ks[1], (DEC_BATCH, DEC_SEQ, D_MODEL)),
        'state_ret': nrm(ks[2], (DEPTH, DEC_BATCH, H_A, DK_A, DV_A)),
        'cache_win_k': nrm(ks[3], (DEPTH, DEC_BATCH, win, KV_C, HD_C)),
        'cache_win_v': nrm(ks[4], (DEPTH, DEC_BATCH, win, KV_C, HD_C)),
        'state_pool': nrm(ks[5], (DEPTH, DEC_BATCH, POOL_BUF, W_B)),
        'w_in': nrm(ks[6], (DEPTH, D_MODEL, N_IN)) * D_MODEL ** -0.5,
        'w_ret_o': nrm(ks[7], (DEPTH, W_A, D_MODEL)) * (W_A ** -0.5 * beta),
        'w_pool_map': nrm(ks[8], (DEPTH, N_POOL_GROUPS, GW_B, GW_B)) * GW_B ** -0.5,
        'pool_scale': 1.0 + 0.1 * nrm(ks[9], (DEPTH, W_B)),
        'w_pool_o': nrm(ks[10], (DEPTH, W_B, D_MODEL)) * (W_B ** -0.5 * beta),
        'attn_sinks': 0.5 * nrm(ks[11], (DEPTH, H_C)),
        'w_att_o': nrm(ks[12], (DEPTH, W_C, D_MODEL)) * (W_C ** -0.5 * beta),
        'w_out': nrm(ks[13], (DEPTH, D_MODEL, D_MODEL)) * (D_MODEL ** -0.5 * beta),
        'ln_g': 1.0 + 0.02 * nrm(ks[14], (DEPTH, D_MODEL)),
        'ln_b': 0.02 * nrm(ks[15], (DEPTH, D_MODEL)),
        'rel_bias': 0.5 * nrm(ks[16], (NUM_BUCKETS, H_C)),
    }


def reference(x_prompt, x_sample, state_ret, cache_win_k, cache_win_v, state_pool, w_in, w_ret_o,
              w_pool_map, pool_scale, w_pool_o, attn_sinks, w_att_o, w_out, ln_g, ln_b, rel_bias):
    alpha = (2.0 * DEPTH) ** 0.25
    dt = x_prompt.dtype
    bp = x_prompt.shape[0]
    zero_ret = jnp.zeros((bp, H_A, DK_A, DV_A), dt)
    zero_win = jnp.zeros((bp, WINDOW, KV_C, HD_C), dt)
    zero_pool = jnp.zeros((bp, POOL_BUF, W_B), dt)
    xp, xs = x_prompt, x_sample
    ret_p, ret_s, kp_l, ks_l, vp_l, vs_l, pp_l, ps_l = [], [], [], [], [], [], [], []
    for l in range(DEPTH):
        xp, r1, k1, v1, p1 = mixer_layer(xp, 0, zero_ret, zero_win, zero_win, zero_pool,
                                         w_in[l], w_ret_o[l], w_pool_map[l], pool_scale[l], w_pool_o[l],
                                         attn_sinks[l], w_att_o[l], w_out[l], ln_g[l], ln_b[l], rel_bias, alpha)
        xs, r2, k2, v2, p2 = mixer_layer(xs, PAST_LEN, state_ret[l], cache_win_k[l], cache_win_v[l], state_pool[l],
                                         w_in[l], w_ret_o[l], w_pool_map[l], pool_scale[l], w_pool_o[l],
                                         attn_sinks[l], w_att_o[l], w_out[l], ln_g[l], ln_b[l], rel_bias, alpha)
        ret_p.append(r1); ret_s.append(r2)
        kp_l.append(k1); ks_l.append(k2)
        vp_l.append(v1); vs_l.append(v2)
        pp_l.append(p1); ps_l.append(p2)
    return (xp, xs, jnp.stack(ret_p), jnp.stack(ret_s), jnp.stack(kp_l), jnp.stack(ks_l),
            jnp.stack(vp_l), jnp.stack(vs_l), jnp.stack(pp_l), jnp.stack(ps_l))
```

```python
import math
from contextlib import ExitStack

import numpy as np
import concourse.bass as bass
import concourse.mybir as mybir
from concourse.bass_utils import run_bass_kernel_spmd

F32 = mybir.dt.float32
BF16 = mybir.dt.bfloat16
AF = mybir.ActivationFunctionType
ALU = mybir.AluOpType
AX = mybir.AxisListType

D = 2048
NIN = 13824
DEPTH = 2
PAST = 8192
ALPHA = (2.0 * DEPTH) ** 0.25
LN_EPS = 1e-5
RMS_EPS = 1e-6
GROUPS = [[0, 1, 2, 3], [4, 5, 6, 7], [8, 9, 10, 11], [12, 13, 14, 15], ["s"]]
ENABLE_SAMPLE = True
STRICT_EXEMPT = ("pe", "sp")
GROUP_SEL = None
STOP_AT = None


class _Stop(Exception):
    pass


class Res:
    __slots__ = ("name", "w", "r", "excl")

    def __init__(self, name, excl=False):
        self.name = name
        self.w = None
        self.r = {}
        self.excl = excl


class Q:
    def __init__(self, name, sem):
        self.name = name
        self.sem = sem
        self.count = 0
        self.seen = {}
        self.ops = []


class Prog:
    def __init__(self, nc, es):
        self.nc = nc
        self.es = es
        self.sems = {}
        self.q = {}
        for name in ("pe", "act", "dve", "pool", "sp"):
            s = es.enter_context(nc.semaphore("q_" + name))
            self.sems["q_" + name] = s
            self.q[name] = Q(name, "q_" + name)
        self.dma_cnt = {}
        self.pending = {}
        self.rrq = {}

    def dma_sem(self, key):
        if key not in self.sems:
            self.sems[key] = self.es.enter_context(self.nc.semaphore(key))
            self.dma_cnt[key] = 0
        return key

    def _need(self, q, tok, waits):
        if tok is None:
            return
        k, v = tok
        if k == q.sem and q.name in STRICT_EXEMPT:
            return
        if q.seen.get(k, 0) >= v:
            return
        q.seen[k] = v
        waits.append((k, v))

    def _deps(self, q, reads, writes):
        waits = []
        for r in reads:
            self._need(q, r.w, waits)
            if r.excl:
                for k, v in r.r.items():
                    if k != q.sem:
                        self._need(q, (k, v), waits)
        for w in writes:
            self._need(q, w.w, waits)
            for k, v in w.r.items():
                self._need(q, (k, v), waits)
        return waits

    def _mark(self, tok, reads, writes):
        k, v = tok
        for r in reads:
            if r.r.get(k, 0) < v:
                r.r[k] = v
        for w in writes:
            w.w = tok
            w.r = {}

    def op(self, qname, fn, reads=(), writes=(), inc=True):
        q = self.q[qname]
        waits = self._deps(q, reads, writes)
        if inc:
            q.count += 1
            tok = (q.sem, q.count)
        else:
            tok = (q.sem, q.count + 1)
        self._mark(tok, reads, writes)
        q.ops.append((waits, fn, (q.sem, 1) if inc else None))
        return tok

    NPOOL = 40

    def dma(self, qname, out, in_, reads=(), writes=(), sem=None, **kw):
        q = self.q[qname]
        if sem is None:
            n = self.NPOOL if qname == "sp" else 12
            i = self.rrq.get(qname, 0)
            self.rrq[qname] = i + 1
            sem = f"d{qname}{i % n}"
        key = self.dma_sem(sem)
        waits = self._deps(q, reads, writes)
        if self.dma_cnt[key] > 0:
            self._need(q, (key, self.dma_cnt[key]), waits)
        self.dma_cnt[key] += 16
        tok = (key, self.dma_cnt[key])
        self._mark(tok, reads, writes)
        q.ops.append((waits, lambda e: e.dma_start(out=out, in_=in_, **kw), (key, 16)))
        self.pending[key] = self.dma_cnt[key]
        return tok

    def wait(self, qname, tok):
        q = self.q[qname]
        waits = []
        self._need(q, tok, waits)
        if waits:
            q.ops.append((waits, None, None))

    def barrier(self, queues=("pe", "act", "dve", "pool", "sp")):
        toks = [(self.q[n].sem, self.q[n].count) for n in queues if self.q[n].count > 0]
        toks += [(k, v) for k, v in self.pending.items() if not (k.startswith("w_") or k.startswith("v_"))]
        for n in queues:
            for t in toks:
                self.wait(n, t)

    def emit(self):
        nc = self.nc
        sems = self.sems
        with nc.Block() as block:
            def runner(q):
                def _(e):
                    for waits, fn, inc in q.ops:
                        for k, v in waits:
                            e.wait_ge(sems[k], v)
                        if fn is not None:
                            ins = fn(e)
                            if inc is not None:
                                ins.then_inc(sems[inc[0]], inc[1])
                return _
            block.tensor(runner(self.q["pe"]))
            block.scalar(runner(self.q["act"]))
            block.vector(runner(self.q["dve"]))
            block.gpsimd(runner(self.q["pool"]))
            block.sync(runner(self.q["sp"]))


def _t5_bucket(dist):
    d = dist.astype(np.float32)
    large = 16 + (np.log(np.maximum(d, np.float32(1.0)) / np.float32(16)) / np.float32(math.log(128 / 16))
                  * np.float32(16)).astype(np.int32)
    large = np.minimum(large, 31)
    return np.where(dist < 16, dist, large)


def make_consts():
    f32 = np.float32
    c = {}
    c["ident"] = np.eye(128, dtype=f32)
    inv = (f32(10000.0) ** (-(np.arange(64, dtype=f32)) / f32(64))).astype(f32)
    cos = np.zeros((128, 17, 64), f32)
    sin = np.zeros((128, 17, 64), f32)
    for t in range(17):
        if t < 16:
            pos = (128 * t + np.arange(128)).astype(f32)
        else:
            pos = (PAST + (np.arange(128) % 8)).astype(f32)
        ang = (pos[:, None] * inv[None, :]).astype(f32)
        cos[:, t] = np.cos(ang.astype(np.float64)).astype(f32)
        sin[:, t] = np.sin(ang.astype(np.float64)).astype(f32)
    c["rot"] = np.concatenate([cos, sin, -sin], axis=2).reshape(128, 17 * 192)
    lg = np.log1p(-np.exp2(-5.0 - np.arange(4, dtype=np.float64)))
    sc = 128.0 ** -0.5
    ret = np.zeros((2, 128, 1028), np.float64)
    idx = np.arange(128)
    for h in range(4):
        diff = idx[None, :] - idx[:, None]
        ret[0, :, h * 128:(h + 1) * 128] = np.where(diff >= 0, np.exp(lg[h] * np.maximum(diff, 0)), 0.0) * sc
        ret[0, :, 512 + h * 128: 512 + (h + 1) * 128] = np.exp(lg[h] * (idx + 1.0))[None, :]
        ret[0, :, 1024 + h] = np.exp(lg[h] * (127.0 - idx)) * sc
        b = idx // 8
        i8 = idx % 8
        same = b[:, None] == b[None, :]
        d8 = i8[None, :] - i8[:, None]
        ret[1, :, h * 128:(h + 1) * 128] = np.where(same & (d8 >= 0), np.exp(lg[h] * np.maximum(d8, 0)), 0.0) * sc
        ret[1, :, 512 + h * 128: 512 + (h + 1) * 128] = np.exp(lg[h] * (i8 + 1.0))[None, :]
        ret[1, :, 1024 + h] = np.exp(lg[h] * (7.0 - i8)) * sc
    c["ret"] = ret.astype(f32)
    c["cd"] = [[float(np.exp(lg[h] * 128.0)) for h in range(4)], [float(np.exp(lg[h] * 8.0)) for h in range(4)]]
    pm = np.zeros((6, 128, 4, 128), np.float64)
    for g, w in enumerate((2, 4, 8, 16)):
        for t in range(128):
            cnt = min(t + 1, w)
            for tp in range(max(0, t - w + 1), t + 1):
                pm[0, tp, g, t] += 1.0 / cnt
            pm[0, t, g, t] -= 1.0
            for tp in range(t - w + 1, t + 1):
                if tp >= 0:
                    pm[1, tp, g, t] += 1.0 / w
                else:
                    pm[2, 128 + tp, g, t] += 1.0 / w
            pm[1, t, g, t] -= 1.0
            b, i = t // 8, t % 8
            for ip in range(max(0, i - w + 1), i + 1):
                pm[3, b * 8 + ip, g, t] += 1.0 / w
            pm[3, t, g, t] -= 1.0
            for r in range(15):
                if r >= 16 + i - w:
                    pm[4 + b // 8, (b % 8) * 15 + r, g, t] += 1.0 / w
    c["pm"] = pm.astype(f32).transpose(1, 0, 2, 3).reshape(128, 6 * 512).copy()
    oh = np.zeros((2, 32, 256, 128), f32)
    valid = np.zeros((2, 128, 256), f32)
    q = np.arange(128)
    for s in range(256):
        dist = q + 128 - s
        v = (dist >= 0) & (dist < 128)
        bk = _t5_bucket(np.maximum(dist, 0).astype(np.int32))
        oh[0, bk[v], s, q[v]] = 1.0
        valid[0, q[v], s] = 1.0
        b, i = q // 8, q % 8
        if s < 128:
            dist = i + 128 - s
            v = dist < 128
        else:
            bp, ip = (s - 128) // 8, (s - 128) % 8
            dist = i - ip
            v = (b == bp) & (ip <= i)
        bk = _t5_bucket(np.maximum(dist, 0).astype(np.int32))
        oh[1, bk[v], s, q[v]] = 1.0
        valid[1, q[v], s] = 1.0
    c["oh"] = oh.reshape(2, 32, 256 * 128)
    c["valid"] = valid
    seq = np.arange(128) // 8
    cm = np.stack([(seq % 4 == j).astype(f32) for j in range(4)], 0)
    c["colmask"] = cm.reshape(1, 512).copy()
    c["rowmask"] = cm.T.copy()
    return c


_CONST_SHAPES = {
    "ident": [128, 128], "rot": [128, 17 * 192], "ret": [2, 128, 1028], "pm": [128, 6 * 512],
    "oh": [2, 32, 256 * 128], "valid": [2, 128, 256], "colmask": [1, 512], "rowmask": [128, 4],
}
_IN_SHAPES = {
    "xp": [2048, D], "xs": [128, D], "st_ret": [2, 16, 4, 128, 256], "ck": [2, 16, 128, 256],
    "cv": [2, 16, 128, 256], "st_pool": [2, 16, 15, 1024],
    "w_in": [2, D, NIN], "w_ret_o": [2, 1024, D], "w_pool_map": [2, 4, 256, 256], "pool_scale": [2, 1024],
    "w_pool_o": [2, 1024, D], "attn_sinks": [2, 16], "w_att_o": [2, 1024, D], "w_out": [2, D, D],
    "ln_g": [2, D], "ln_b": [2, D], "rel_bias": [32, 16],
}
_OUT_SHAPES = {
    "yp": [2048, D], "ys": [128, D], "retp": [2, 4, 128, 256], "rets": [2, 16, 4, 128, 256],
    "wkp": [2, 128, 256], "wks": [2, 16, 128, 256], "wvp": [2, 128, 256], "wvs": [2, 16, 128, 256],
    "plp": [2, 15, 1024], "pls": [2, 16, 15, 1024],
}


def _layer_wspecs(l):
    s = []
    for c0 in (0, 512, 1024, 1536, 2048, 2560):
        s.append(("w_in", l, 0, D, c0, 512))
    for c0 in (3072, 3584, 4096, 4608):
        s.append(("w_in", l, 0, D, c0, 512))
    s.append(("w_pool_map", l, 0, 0, 0, 0))
    for c0 in (5120, 5632, 6144, 6656, 7168):
        s.append(("w_in", l, 0, D, c0, 512))
    for sc4 in range(4):
        for wo, m0 in (("w_ret_o", 7680), ("w_pool_o", 9728), ("w_att_o", 11776)):
            s.append((wo, l, 0, 1024, sc4 * 512, 512))
            s.append(("w_in", l, 0, D, m0 + sc4 * 512, 512))
    for c in range(4):
        s.append(("w_out", l, 0, D, c * 512, 512))
    return s


def build_program():
    nc = bass.Bass("TRN2", target_bir_lowering=False)
    es = ExitStack()
    with es:
        din = {k: nc.dram_tensor(k, v, F32, kind="ExternalInput").ap() for k, v in _IN_SHAPES.items()}
        dc = {k: nc.dram_tensor("c_" + k, v, F32, kind="ExternalInput").ap() for k, v in _CONST_SHAPES.items()}
        dout = {k: nc.dram_tensor(k, v, F32, kind="ExternalOutput").ap() for k, v in _OUT_SHAPES.items()}
        e_dram = nc.dram_tensor("e_scr", [2, 128, 4096], BF16, kind="Internal").ap()
        NCH_L = len(_layer_wspecs(0))
        w_scr = nc.dram_tensor("w_scr", [DEPTH * NCH_L, 128, 8192], BF16, kind="Internal").ap()

        P = Prog(nc, es)
        sb = lambda n, s, d: es.enter_context(nc.sbuf_tensor(n, s, d))
        cds = make_consts()["cd"]

        NTM = 4
        TOKM = NTM * 128
        xres = sb("xres", [128, NTM, D], F32)
        xT = sb("xT", [128, 16, TOKM], BF16)
        zT = sb("zT", [128, 24, TOKM], BF16)
        NWB = 3
        wbuf = [sb(f"wbuf{i}", [128, 16, 512], BF16) for i in range(NWB)]
        Etab = sb("Etab", [128, 16, 256], BF16)
        AR = 21504
        arena = sb("arena", [128, AR], BF16)
        S32 = [sb(f"S32_{l}", [128, 4, 256], F32) for l in range(2)]
        Sb = [sb(f"Sb_{l}", [128, 4, 256], BF16) for l in range(2)]
        u_prev = [sb(f"uprev{l}", [128, 1024], BF16) for l in range(2)]
        kT_prev = [sb(f"kTprev{l}", [128, 4, 128], BF16) for l in range(2)]
        v_prev = [sb(f"vprev{l}", [128, 256], BF16) for l in range(2)]
        identf = sb("identf", [128, 128], F32)
        ident = sb("ident", [128, 128], BF16)
        rot = sb("rot", [128, NTM, 192], F32)
        rett = sb("rett", [128, 1028], F32)
        pmt = sb("pmt", [128, 3, 4, 128], BF16)
        colmask = sb("colmask", [128, 4, 128], BF16)
        rowmask = sb("rowmask", [128, 4], F32)
        sink_bc = sb("sink_bc", [128, 16], F32)
        nsink_bc = sb("nsink_bc", [128, 16], F32)
        psch = sb("psch", [128, 8], F32)
        th = [sb(f"th{i}", [128, 512], F32) for i in range(2)]
        tmpA = sb("tmpA", [128, 512], F32)
        stats = sb("stats", [128, 4, 6], F32)
        mv = sb("mv", [128, 2], F32)
        stats2 = sb("stats2", [128, 4, 6], F32)
        mv2 = sb("mv2", [128, 2], F32)
        sm1 = sb("sm1", [128, 16], F32)
        sm2 = sb("sm2", [128, 16], F32)
        sm3 = sb("sm3", [128, 16], F32)
        sm4 = sb("sm4", [128, 16], F32)
        negm = sb("negm", [128, 16], F32)
        rs = sb("rs", [128, 16], F32)
        ss = sb("ss", [128, 4], F32)
        mhalf = sb("mhalf", [128, 4], F32)

        ps = [es.enter_context(nc.psum_tensor(f"ps{i}", [128, 512], F32)) for i in range(8)]
        psb = [p.bitcast(BF16) for p in ps]
        R_ps = [Res(f"ps{i}", excl=True) for i in range(8)]

        R = Res
        R_xres = [R(f"xres{t}") for t in range(NTM)]
        R_xT = [R(f"xT{t}") for t in range(NTM)]
        R_z = [[R(f"z{b}_{t}") for t in range(NTM)] for b in range(3)]
        R_w = [R(f"w{i}") for i in range(NWB)]
        R_E = R("Etab")
        R_edram = R("edram")
        R_S32 = [R("S32_0"), R("S32_1")]
        R_Sb = [R("Sb0"), R("Sb1")]
        R_uprev = [R("up0"), R("up1")]
        R_kTprev = [R("kp0"), R("kp1")]
        R_vprev = [R("vp0"), R("vp1")]
        R_ident = R("ident")
        R_identf = R("identf")
        R_rot, R_rett, R_pmt, R_masks, R_lp = R("rot"), R("rett"), R("pmt"), R("masks"), R("layerparams")
        R_th = [R("th0"), R("th1")]
        R_tmpA, R_xb, R_stats, R_small = R("tmpA"), R("xb"), R("stats"), R("small")
        R_ar = {}

        def AR_res(name):
            if name not in R_ar:
                R_ar[name] = R("ar_" + name)
            return R_ar[name]

        xb_holder = {}

        def av(off, n, dt=BF16):
            v = arena[:, off:off + n]
            return v.bitcast(F32) if dt == F32 else v

        xb = av(16384, 2048)

        wspecs = []
        wmeta = []
        run_groups = [gi_ for gi_, g in enumerate(GROUPS) if not (g == ["s"] and not ENABLE_SAMPLE) and not (GROUP_SEL is not None and gi_ not in GROUP_SEL)]
        prompt_groups = [gi_ for gi_ in run_groups if GROUPS[gi_] != ["s"]]
        use_scr = len(prompt_groups) >= 1 and any(GROUPS[gi_] == ["s"] for gi_ in run_groups) and run_groups[-1] == len(GROUPS) - 1 and GROUPS[-1] == ["s"]
        for gi_ in run_groups:
            for l in range(DEPTH):
                ls = _layer_wspecs(l)
                wspecs += ls
                wmeta += [(gi_, l, k) for k in range(len(ls))]
        ws = {"cur": 0, "issued": 0}

        def w_issue(i):
            name, l, r0, nr, c0, ncl = wspecs[i]
            slot = i % NWB
            gi_, _, k = wmeta[i]
            flat = wbuf[slot][:].rearrange("p a b -> p (a b)")
            scr_ok = use_scr and not (16 <= k < 40)
            if scr_ok and GROUPS[gi_] == ["s"]:
                P.dma("sp", flat, w_scr[l * NCH_L + k], writes=[R_w[slot]], sem=f"v_{slot}")
                return
            if name == "w_pool_map":
                src = din[name][l].rearrange("g (kc p) d -> p g kc d", p=128)
                dst = wbuf[slot][:, 0:4, :].rearrange("p a (k d) -> p a k d", k=2)
            else:
                src = din[name][l, r0:r0 + nr, c0:c0 + ncl].rearrange("(kc p) n -> p kc n", p=128)
                dst = wbuf[slot][:, 0:nr // 128, 0:ncl]
            P.dma("pool", dst, src, writes=[R_w[slot]], sem=f"w_{slot}")
            if scr_ok and k % len(prompt_groups) == prompt_groups.index(gi_):
                P.dma("sp", w_scr[l * NCH_L + k], flat, reads=[R_w[slot]])

        def w_get(expect, hold=0):
            i = ws["cur"]
            assert wspecs[i][0] == expect[0] and wspecs[i][4] == expect[1], (wspecs[i], expect)
            while ws["issued"] < min(len(wspecs), i + NWB - hold):
                w_issue(ws["issued"])
                ws["issued"] += 1
            ws["cur"] += 1
            slot = i % NWB
            return wbuf[slot], R_w[slot]

        def w_prefetch():
            i = ws["cur"]
            while ws["issued"] < min(len(wspecs), i + NWB):
                w_issue(ws["issued"])
                ws["issued"] += 1

        bank_rr = {}

        def next_bank(lo, hi):
            i = bank_rr.get((lo, hi), 0)
            bank_rr[(lo, hi)] = i + 1
            return lo + i % (hi - lo)

        def mm_group(out_ap, pairs, reads, bank):
            n = len(pairs)
            for i, (lt, rh) in enumerate(pairs):
                P.op("pe", lambda e, o=out_ap, lt=lt, rh=rh, i=i, n=n: e.matmul(o, lt, rh, start=(i == 0), stop=(i == n - 1)),
                     reads=reads, writes=[R_ps[bank]], inc=(i == n - 1))

        def transposes(bank, srcs, reads):
            n = len(srcs)
            for i, s in enumerate(srcs):
                P.op("pe", lambda e, i=i, s=s, bank=bank: e.transpose(psb[bank][:, i * 128:(i + 1) * 128], s, ident[:]),
                     reads=reads + [R_ident], writes=[R_ps[bank]], inc=(i == n - 1))

        def gate_evac(bank, n, out_ap, out_res, thi):
            P.op("act", lambda e, bank=bank, n=n, thi=thi: e.activation(out=th[thi][:, 0:n], in_=ps[bank][:, 0:n], func=AF.Tanh, scale=0.5),
                 reads=[R_ps[bank]], writes=[R_th[thi]])
            P.op("dve", lambda e, bank=bank, n=n, thi=thi, o=out_ap: e.scalar_tensor_tensor(out=o, in0=th[thi][:, 0:n], scalar=1.0, in1=ps[bank][:, 0:n], op0=ALU.add, op1=ALU.mult),
                 reads=[R_th[thi], R_ps[bank]], writes=out_res)

        P.dma("sp", identf[:], dc["ident"], writes=[R_identf])
        P.op("dve", lambda e: e.tensor_copy(out=ident[:], in_=identf[:]), reads=[R_identf], writes=[R_ident])
        P.dma("pool", colmask[:].rearrange("p a b -> p (a b)"), dc["colmask"][0, :].partition_broadcast(128), writes=[R_masks])
        P.dma("sp", rowmask[:], dc["rowmask"], writes=[R_masks])
        P.op("pool", lambda e: e.memset(mhalf[:], -0.5), writes=[R_masks])
        for l in range(2):
            P.op("pool", lambda e, l=l: e.memset(S32[l][:], 0.0), writes=[R_S32[l]])
            P.op("pool", lambda e, l=l: e.memset(Sb[l][:], 0.0), writes=[R_Sb[l]])
            P.op("pool", lambda e, l=l: e.memset(u_prev[l][:], 0.0), writes=[R_uprev[l]])
            P.op("pool", lambda e, l=l: e.memset(kT_prev[l][:], 0.0), writes=[R_kTprev[l]])
            P.op("pool", lambda e, l=l: e.memset(v_prev[l][:], 0.0), writes=[R_vprev[l]])

        ohb = av(0, 4096).rearrange("p (s q) -> p s q", q=128)
        rbt = av(8192, 32, F32)
        rbh = av(8224, 16)
        rbl = av(8240, 16)
        rbh32 = av(8256, 32, F32)
        validt = av(8320, 512, F32)
        etmp = av(8832, 1024, F32)
        R_oh, R_rb, R_valid, R_etmp = AR_res("oh"), AR_res("rb"), AR_res("valid"), AR_res("etmp")
        P.dma("sp", rbt[0:32, :], din["rel_bias"], writes=[R_rb])
        P.op("act", lambda e: e.copy(out=rbh[0:32, :], in_=rbt[0:32, :]), reads=[R_rb], writes=[R_rb])
        P.op("act", lambda e: e.copy(out=rbh32[0:32, :], in_=rbh[0:32, :]), reads=[R_rb], writes=[R_rb])
        P.op("dve", lambda e: e.tensor_tensor(out=rbl[0:32, :], in0=rbt[0:32, :], in1=rbh32[0:32, :], op=ALU.subtract), reads=[R_rb], writes=[R_rb])
        for var in range(2 if ENABLE_SAMPLE else 1):
            P.dma("sp", validt, dc["valid"][var], writes=[R_valid])
            for sc in range(8):
                P.dma("pool", ohb[0:32], dc["oh"][var, :, sc * 4096:(sc + 1) * 4096].rearrange("p (s q) -> p s q", q=128), writes=[R_oh])
                bk = next_bank(0, 8)
                for s in range(32):
                    P.op("pe", lambda e, s=s, bk=bk: e.matmul(ps[bk][:, s * 16:(s + 1) * 16], ohb[0:32, s, :], rbh[0:32, :], start=True, stop=False),
                         reads=[R_oh, R_rb], writes=[R_ps[bk]], inc=False)
                    P.op("pe", lambda e, s=s, bk=bk: e.matmul(ps[bk][:, s * 16:(s + 1) * 16], ohb[0:32, s, :], rbl[0:32, :], start=False, stop=True),
                         reads=[R_oh, R_rb], writes=[R_ps[bk]], inc=(s == 31))
                P.op("act", lambda e, bk=bk: e.activation(out=etmp, in_=ps[bk][:, :], func=AF.Exp), reads=[R_ps[bk]], writes=[R_etmp])
                P.op("dve", lambda e, sc=sc: e.tensor_tensor(out=Etab[:, :, sc * 32:(sc + 1) * 32],
                                                              in0=etmp.rearrange("p (s h) -> p h s", h=16),
                                                              in1=validt[:, sc * 32:(sc + 1) * 32].unsqueeze(1).to_broadcast([128, 16, 32]),
                                                              op=ALU.mult), reads=[R_etmp, R_valid], writes=[R_E])
            P.dma("sp", e_dram[var], Etab[:].rearrange("p h s -> p (h s)"), reads=[R_E], writes=[R_edram])
        P.barrier()

        out_toks = []

        def run_group(gi, tiles):
            is_s = tiles == ["s"]
            NT = len(tiles)
            TOK = NT * 128
            var = 1 if is_s else 0
            first_tile = (not is_s) and tiles[0] == 0
            last_grp = (not is_s) and tiles[-1] == 15
            cd = cds[var]

            P.barrier()
            P.dma("sp", Etab[:].rearrange("p h s -> p (h s)"), e_dram[var], reads=[R_edram], writes=[R_E])
            for t, tl in enumerate(tiles):
                gt = 16 if is_s else tl
                P.dma("sp", rot[:, t, :], dc["rot"][:, gt * 192:(gt + 1) * 192], writes=[R_rot])
                src = din["xs"] if is_s else din["xp"][tl * 128:(tl + 1) * 128, :]
                P.dma("sp", xres[:, t, :], src, writes=[R_xres[t]])
            P.dma("sp", rett[:], dc["ret"][var], writes=[R_rett])
            pmsel = (3, 4, 5) if is_s else ((0, 1, 2))
            for i, pmi in enumerate(pmsel):
                P.dma("pool", pmt[:, i], dc["pm"][:, pmi * 512:(pmi + 1) * 512].rearrange("p (g t) -> p g t", t=128), writes=[R_pmt])

            def make_xT(t):
                P.op("act", lambda e, t=t: e.copy(out=xb[:], in_=xres[:, t, :]), reads=[R_xres[t]], writes=[R_xb])
                for hf in range(2):
                    bk = next_bank(0, 8)
                    transposes(bk, [xb[:, (hf * 8 + i) * 128:(hf * 8 + i + 1) * 128] for i in range(8)], [R_xb])
                    P.op("dve", lambda e, t=t, hf=hf, bk=bk: e.tensor_copy(out=xT[:, hf * 8:(hf + 1) * 8, t * 128:(t + 1) * 128],
                                                                         in_=psb[bk][:, :].rearrange("p (c q) -> p c q", q=128)),
                         reads=[R_ps[bk]], writes=[R_xT[t]])

            for t in range(NT):
                make_xT(t)

            for l in range(DEPTH):
                run_layer(l, tiles, is_s, NT, TOK, var, first_tile, last_grp, cd, make_xT)

        def run_layer(l, tiles, is_s, NT, TOK, var, first_tile, last_grp, cd, make_xT):
            Rx = R_xT[:NT]

            def inproj_tok(wb, Rw, t, ncols, c0=0):
                bk = next_bank(0, 2)
                mm_group(ps[bk][:, 0:ncols], [(xT[:, kc, t * 128:(t + 1) * 128], wb[:, kc, c0:c0 + ncols]) for kc in range(16)],
                         [R_xT[t], Rw], bk)
                return bk

            def inproj_feat(wb, Rw, lhs_fn, lo=0, hi=2):
                bk = next_bank(lo, hi)
                mm_group(ps[bk][:, 0:TOK], [(lhs_fn(kc), xT[:, kc, 0:TOK]) for kc in range(16)], Rx + [Rw], bk)
                return bk

            P.barrier()
            P.dma("sp", sink_bc[:], din["attn_sinks"][l, :].partition_broadcast(128), writes=[R_lp])
            P.dma("sp", psch[:], din["pool_scale"][l, :].rearrange("(c p) -> p c", p=128), writes=[R_lp], allow_slow_non_contiguous=True)
            P.op("dve", lambda e: e.tensor_scalar(out=nsink_bc[:], in0=sink_bc[:], scalar1=-1.0, scalar2=None, op0=ALU.mult), reads=[R_lp], writes=[R_lp])
            P.op("dve", lambda e: e.tensor_scalar(out=psch[:], in0=psch[:], scalar1=0.5, scalar2=None, op0=ALU.mult), reads=[R_lp], writes=[R_lp])

            if STOP_AT == "X":
                raise _Stop()
            q_rot = av(0, NT * 512).rearrange("p (t c) -> p t c", c=512)
            k_rot = av(2048, NT * 512).rearrange("p (t c) -> p t c", c=512)
            v_a = av(4096, NT * 1024).rearrange("p (t c) -> p t c", c=1024)
            sg_a = av(8192, NT * 1024).rearrange("p (t c) -> p t c", c=1024)
            k_dec = av(12288, 512)
            qT_a = av(12800, 512).rearrange("p (h i) -> p h i", i=128)
            kT_a = av(13312, 512).rearrange("p (h i) -> p h i", i=128)
            qdT_a = av(13824, 512).rearrange("p (h i) -> p h i", i=128)
            scT_a = av(14336, 512).rearrange("p (h i) -> p h i", i=128)
            za_tm = av(14848, 1024)
            rotB = av(15872, 1024, F32)
            R_qr = [AR_res(f"qrot{t}") for t in range(NT)]
            R_kr = [AR_res(f"krot{t}") for t in range(NT)]
            R_va = [AR_res(f"va{t}") for t in range(NT)]
            R_sga = [AR_res(f"sga{t}") for t in range(NT)]
            R_kdec, R_qT, R_kT, R_qdT, R_scT, R_zatm, R_rotB = (AR_res(n) for n in ("kdec", "qTa", "kTa", "qdTa", "scTa", "zatm", "rotB"))

            def rot_evac(bk, t, dst, Rdst):
                cosb = rot[:, t, 0:64].unsqueeze(1).to_broadcast([128, 8, 64])
                sinb = rot[:, t, 64:128].unsqueeze(1).to_broadcast([128, 4, 64])
                nsinb = rot[:, t, 128:192].unsqueeze(1).to_broadcast([128, 4, 64])
                x8 = ps[bk][:, :].rearrange("p (g d) -> p g d", d=64)
                x42 = ps[bk][:, :].rearrange("p (h two d) -> p h two d", two=2, d=64)
                B42 = rotB.rearrange("p (h two d) -> p h two d", two=2, d=64)
                P.op("dve", lambda e: e.tensor_tensor(out=tmpA[:, :].rearrange("p (g d) -> p g d", d=64), in0=x8, in1=cosb, op=ALU.mult),
                     reads=[R_ps[bk], R_rot], writes=[R_tmpA])
                P.op("dve", lambda e: e.tensor_tensor(out=B42[:, :, 0, :], in0=x42[:, :, 1, :], in1=nsinb, op=ALU.mult),
                     reads=[R_ps[bk], R_rot], writes=[R_rotB])
                P.op("dve", lambda e: e.tensor_tensor(out=B42[:, :, 1, :], in0=x42[:, :, 0, :], in1=sinb, op=ALU.mult),
                     reads=[R_ps[bk], R_rot], writes=[R_rotB])
                P.op("pool", lambda e: e.tensor_tensor(out=dst[:, t, :], in0=tmpA[:, :], in1=rotB, op=ALU.add),
                     reads=[R_tmpA, R_rotB], writes=[Rdst[t]])

            wb, Rw = w_get(("w_in", 0))
            for t in range(NT):
                bk = inproj_tok(wb, Rw, t, 512)
                rot_evac(bk, t, q_rot, R_qr)
            wb, Rw = w_get(("w_in", 512))
            for t in range(NT):
                bk = inproj_tok(wb, Rw, t, 512)
                rot_evac(bk, t, k_rot, R_kr)
            for c in range(2):
                wb, Rw = w_get(("w_in", 1024 + c * 512))
                for t in range(NT):
                    bk = inproj_tok(wb, Rw, t, 512)
                    P.op("act", lambda e, bk=bk, t=t, c=c: e.copy(out=v_a[:, t, c * 512:(c + 1) * 512], in_=ps[bk][:, :]),
                         reads=[R_ps[bk]], writes=[R_va[t]])
            for c in range(2):
                wb, Rw = w_get(("w_in", 2048 + c * 512))
                for t in range(NT):
                    bk = inproj_tok(wb, Rw, t, 512)
                    gate_evac(bk, 512, sg_a[:, t, c * 512:(c + 1) * 512], [R_sga[t]], t % 2)
            w_prefetch()

            dmT = rett[:, 0:512].rearrange("p (h i) -> p h i", i=128)
            qdec = rett[:, 512:1024].rearrange("p (h i) -> p h i", i=128)
            kdecs = rett[:, 1024:1028]
            for t in range(NT):
                transposes(2, [q_rot[:, t, h * 128:(h + 1) * 128] for h in range(4)] + [k_rot[:, t, h * 128:(h + 1) * 128] for h in range(4)],
                           [R_qr[t], R_kr[t]])
                pT3 = psb[2][:, :].rearrange("p (c q) -> p c q", q=128)
                P.op("act", lambda e, pT3=pT3: e.copy(out=qT_a, in_=pT3[:, 0:4, :]), reads=[R_ps[2]], writes=[R_qT])
                P.op("act", lambda e, pT3=pT3: e.copy(out=kT_a, in_=pT3[:, 4:8, :]), reads=[R_ps[2]], writes=[R_kT])
                P.op("dve", lambda e, pT3=pT3: e.tensor_tensor(out=qdT_a, in0=pT3[:, 0:4, :], in1=qdec, op=ALU.mult),
                     reads=[R_ps[2], R_rett], writes=[R_qdT])
                P.op("pool", lambda e, t=t: e.tensor_tensor(out=k_dec.rearrange("p (h d) -> p h d", d=128),
                                                            in0=k_rot[:, t, :].rearrange("p (h d) -> p h d", d=128),
                                                            in1=kdecs.unsqueeze(2).to_broadcast([128, 4, 128]), op=ALU.mult),
                     reads=[R_kr[t], R_rett], writes=[R_kdec])
                for h in range(4):
                    P.op("pe", lambda e, h=h: e.matmul(ps[3][:, h * 128:(h + 1) * 128], kT_a[:, h, :], qT_a[:, h, :], start=True, stop=True),
                         reads=[R_kT, R_qT], writes=[R_ps[3]], inc=(h == 3))
                P.op("dve", lambda e: e.tensor_tensor(out=scT_a, in0=ps[3][:, :].rearrange("p (h i) -> p h i", i=128), in1=dmT, op=ALU.mult),
                     reads=[R_ps[3], R_rett], writes=[R_scT])
                use_cross = not (first_tile and t == 0) and not is_s
                for h in range(4):
                    if is_s:
                        break
                    ob = 4 + h // 2
                    oap = ps[ob][:, (h % 2) * 256:(h % 2 + 1) * 256]
                    P.op("pe", lambda e, h=h, oap=oap, t=t, uc=use_cross: e.matmul(oap, scT_a[:, h, :], v_a[:, t, h * 256:(h + 1) * 256], start=True, stop=not uc),
                         reads=[R_scT, R_va[t]], writes=[R_ps[ob]], inc=(not use_cross) and (h % 2 == 1))
                    if use_cross:
                        P.op("pe", lambda e, h=h, oap=oap: e.matmul(oap, qdT_a[:, h, :], Sb[l][:, h, :], start=False, stop=True),
                             reads=[R_qdT, R_Sb[l]], writes=[R_ps[ob]], inc=(h % 2 == 1))
                if is_s:
                    sample_ret(l, scT_a, R_scT, v_a, R_va, qdT_a, R_qdT, k_dec, R_kdec, cd)
                else:
                    for h in range(4):
                        sbk = 6 + h // 2
                        P.op("pe", lambda e, h=h, sbk=sbk, t=t: e.matmul(ps[sbk][:, (h % 2) * 256:(h % 2 + 1) * 256], k_dec[:, h * 128:(h + 1) * 128],
                                                                       v_a[:, t, h * 256:(h + 1) * 256], start=True, stop=True),
                             reads=[R_kdec, R_va[t]], writes=[R_ps[sbk]], inc=(h % 2 == 1))
                    for h in range(4):
                        sbk = 6 + h // 2
                        P.op("dve", lambda e, h=h, sbk=sbk: e.scalar_tensor_tensor(out=S32[l][:, h, :], in0=S32[l][:, h, :], scalar=cd[h],
                                                                                   in1=ps[sbk][:, (h % 2) * 256:(h % 2 + 1) * 256], op0=ALU.mult, op1=ALU.add),
                             reads=[R_ps[sbk], R_S32[l]], writes=[R_S32[l]])
                    P.op("act", lambda e: e.copy(out=Sb[l][:], in_=S32[l][:]), reads=[R_S32[l]], writes=[R_Sb[l]])
                    if last_grp and t == NT - 1:
                        out_toks.append(P.dma("sp", dout["retp"][l].rearrange("h d v -> d h v"), S32[l][:], reads=[R_S32[l]]))
                for h in range(4):
                    ob = 4 + h // 2
                    P.op("act", lambda e, h=h, ob=ob: e.activation(out=tmpA[:, 0:256], in_=ps[ob][:, (h % 2) * 256:(h % 2 + 1) * 256], func=AF.Square,
                                                                   accum_out=ss[:, h:h + 1]),
                         reads=[R_ps[ob]], writes=[R_tmpA, R_small])
                P.op("dve", lambda e: e.tensor_scalar(out=sm1[:, 0:4], in0=ss[:, :], scalar1=4.0 / 256, scalar2=4.0 * RMS_EPS, op0=ALU.mult, op1=ALU.add),
                     reads=[R_small], writes=[R_small])
                P.op("pool", lambda e: e.tensor_tensor(out=sm2[:, 0:4], in0=sm1[:, 0:4], in1=mhalf[:, 0:4], op=ALU.pow),
                     reads=[R_small], writes=[R_small])
                for h in range(4):
                    ob = 4 + h // 2
                    P.op("dve", lambda e, h=h, ob=ob, t=t: e.scalar_tensor_tensor(out=za_tm[:, h * 256:(h + 1) * 256], in0=ps[ob][:, (h % 2) * 256:(h % 2 + 1) * 256],
                                                                                scalar=sm2[:, h:h + 1], in1=sg_a[:, t, h * 256:(h + 1) * 256], op0=ALU.mult, op1=ALU.mult),
                         reads=[R_ps[ob], R_small, R_sga[t]], writes=[R_zatm])
                transposes(2, [za_tm[:, c * 128:(c + 1) * 128] for c in range(8)], [R_zatm])
                P.op("act", lambda e, t=t: e.copy(out=zT[:, 0:8, t * 128:(t + 1) * 128], in_=psb[2][:, :].rearrange("p (c q) -> p c q", q=128)),
                     reads=[R_ps[2]], writes=[R_z[0][t]])

            if STOP_AT == "A":
                raise _Stop()
            P.barrier()
            u_all = av(0, (NT + 1) * 1024).rearrange("p (t c) -> p t c", c=1024)
            u32 = av(5120, 2048, F32)
            pT_b = av(7168, 1024).rearrange("p (c t) -> p c t", t=128)
            stb = av(8192, 2048).rearrange("p (a c) -> p a c", c=1024)
            R_u = [AR_res(f"u{t}") for t in range(NT + 1)]
            R_u32, R_pTb, R_stb = AR_res("u32"), AR_res("pTb"), AR_res("stb")
            if is_s:
                for hf in range(2):
                    P.dma("pool", stb[0:120, hf, :], din["st_pool"][l, hf * 8:(hf + 1) * 8].rearrange("b r c -> (b r) c"), writes=[R_stb])
            else:
                P.op("pool", lambda e: e.tensor_copy(out=u_all[:, 0, :], in_=u_prev[l][:]), reads=[R_uprev[l]], writes=[R_u[0]])
            want_u32 = is_s or last_grp
            for c in range(2):
                wb, Rw = w_get(("w_in", 3072 + c * 512))
                for t in range(NT):
                    bk = inproj_tok(wb, Rw, t, 512)
                    P.op("act", lambda e, bk=bk, t=t, c=c: e.copy(out=u_all[:, t + 1, c * 512:(c + 1) * 512], in_=ps[bk][:, :]),
                         reads=[R_ps[bk]], writes=[R_u[t + 1]])
                    if want_u32 and t == NT - 1:
                        lo = 0 if is_s else 64
                        P.op("dve", lambda e, bk=bk, c=c, lo=lo: e.tensor_copy(out=u32[lo:128, c * 512:(c + 1) * 512], in_=ps[bk][lo:128, :]),
                             reads=[R_ps[bk]], writes=[R_u32])
            if last_grp:
                out_toks.append(P.dma("sp", dout["plp"][l], u32[113:128, :], reads=[R_u32]))
            if is_s:
                for b in range(16):
                    out_toks.append(P.dma("sp", dout["pls"][l, b, 7:15, :], u32[b * 8:(b + 1) * 8, :], reads=[R_u32]))
                    out_toks.append(P.dma("sp", dout["pls"][l, b, 0:7, :], din["st_pool"][l, b, 8:15, :]))
            for c in range(2):
                wb, Rw = w_get(("w_in", 4096 + c * 512))
                for j in range(4):
                    bk = inproj_feat(wb, Rw, lambda kc, j=j, wb=wb: wb[:, kc, j * 128:(j + 1) * 128])
                    gate_evac(bk, TOK, zT[:, 8 + c * 4 + j, 0:TOK], R_z[1][:NT], j % 2)
            wb, Rw = w_get(("w_pool_map", 0))
            wmap = wb[:, 0:4, :].rearrange("p a (k d) -> p a k d", k=2)
            for t in range(NT):
                cur = 0 if (first_tile and t == 0) else 1
                if is_s:
                    cur = 0
                has_prev = not (first_tile and t == 0)
                for cc in range(8):
                    g = cc // 2
                    pb = 2 + cc // 4
                    oap = ps[pb][:, (cc % 4) * 128:(cc % 4 + 1) * 128]
                    pairs = [(u_all[:, t + 1, cc * 128:(cc + 1) * 128], pmt[:, cur, g, :])]
                    rds = [R_u[t + 1], R_pmt]
                    if is_s:
                        for hf in range(2):
                            pairs.append((stb[0:120, hf, cc * 128:(cc + 1) * 128], pmt[0:120, 1 + hf, g, :]))
                        rds.append(R_stb)
                    elif has_prev:
                        pairs.append((u_all[:, t, cc * 128:(cc + 1) * 128], pmt[:, 2, g, :]))
                        rds.append(R_u[t])
                    n = len(pairs)
                    for i, (lt, rh) in enumerate(pairs):
                        P.op("pe", lambda e, oap=oap, lt=lt, rh=rh, i=i, n=n: e.matmul(oap, lt, rh, start=(i == 0), stop=(i == n - 1)),
                             reads=rds, writes=[R_ps[pb]], inc=(i == n - 1) and (cc % 4 == 3))
                for hf in range(2):
                    P.op("act", lambda e, hf=hf: e.copy(out=pT_b[:, hf * 4:(hf + 1) * 4, :], in_=ps[2 + hf][:, :].rearrange("p (c t) -> p c t", t=128)),
                         reads=[R_ps[2 + hf]], writes=[R_pTb])
                for idx in range(8):
                    g, dcc = idx // 2, idx % 2
                    mb = 4 + idx // 4
                    oap = ps[mb][:, (idx % 4) * 128:(idx % 4 + 1) * 128]
                    for kc in range(2):
                        P.op("pe", lambda e, oap=oap, g=g, kc=kc, dcc=dcc: e.matmul(oap, wmap[:, g, kc, dcc * 128:(dcc + 1) * 128], pT_b[:, g * 2 + kc, :],
                                                                                   start=(kc == 0), stop=(kc == 1)),
                             reads=[Rw, R_pTb], writes=[R_ps[mb]], inc=(kc == 1) and (idx % 4 == 3))
                for idx in range(8):
                    mb = 4 + idx // 4
                    P.op("dve", lambda e, idx=idx, mb=mb, t=t: e.scalar_tensor_tensor(out=zT[:, 8 + idx, t * 128:(t + 1) * 128],
                                                                                    in0=ps[mb][:, (idx % 4) * 128:(idx % 4 + 1) * 128],
                                                                                    scalar=psch[:, idx:idx + 1], in1=zT[:, 8 + idx, t * 128:(t + 1) * 128],
                                                                                    op0=ALU.mult, op1=ALU.mult),
                         reads=[R_ps[mb], R_lp, R_z[1][t]], writes=[R_z[1][t]])
            if not is_s:
                P.op("pool", lambda e: e.tensor_copy(out=u_prev[l][:], in_=u_all[:, NT, :]), reads=[R_u[NT]], writes=[R_uprev[l]])

            if STOP_AT == "B":
                raise _Stop()
            P.barrier()
            if is_s:
                o_q, o_k, o_v, o_sg, o_e, o_p, o_pT, o_to, o_zc = 0, 1024, 2048, 2560, 3584, 5632, 6656, 7680, 8704
                Vc = av(9728, 4096).rearrange("p (b c) -> p b c", c=256)
                Kraw = av(13824, 2048).rearrange("p (b c) -> p b c", c=256)
                pTm = av(15872, 2048).rearrange("p (j h q) -> p j h q", j=4, h=4)
                qTm = xres[:, 1, :].bitcast(BF16).rearrange("p (j c q) -> p j c q", j=4, c=8)
                KTc = xres[:, 2:4, :].rearrange("p a b -> p (a b)").bitcast(BF16).rearrange("p (k s q) -> p k s q", k=4, s=16)
                R_Vc, R_Kraw, R_pTm, R_KTc = AR_res("Vc"), AR_res("Kraw"), AR_res("pTm"), AR_res("KTc")
            else:
                o_q, o_k, o_v, o_sg, o_e, o_p, o_pT, o_to, o_zc = 0, 4096, 6656, 7936, 12032, 14080, 15104, 16128, 17152
            qT_c = av(o_q, 8 * TOK).rearrange("p (c t) -> p c t", t=TOK)
            kT_c = av(o_k, 4 * (TOK + 128)).rearrange("p (c t) -> p c t", t=TOK + 128)
            v_all = av(o_v, (NT + 1) * 256).rearrange("p (t c) -> p t c", c=256)
            sgc = av(o_sg, NT * 1024).rearrange("p (t c) -> p t c", c=1024)
            e_c = av(o_e, 2048, F32).rearrange("p (h s) -> p h s", s=256)
            p_c = av(o_p, 1024).rearrange("p (h s) -> p h s", s=256)
            o_e2, o_p2 = (17920, 19968) if is_s else (18432, 20480)
            e_cs = [e_c, av(o_e2, 2048, F32).rearrange("p (h s) -> p h s", s=256)]
            p_cs = [p_c, av(o_p2, 1024).rearrange("p (h s) -> p h s", s=256)]
            pT_c = av(o_pT, 1024).rearrange("p (c q) -> p c q", q=128)
            tmpo = av(o_to, 1024, F32)
            zc_tm = av(o_zc, 1024)
            kv32 = av(o_e, 1024, F32)
            R_qTc, R_kTc, R_sgc = AR_res("qTc"), AR_res("kTc"), [AR_res(f"sgc{t}") for t in range(NT)]
            R_vall = [AR_res(f"vall{t}") for t in range(NT + 1)]
            R_ec, R_pc, R_pTc, R_tmpo, R_zctm = (AR_res(n) for n in ("ec", "pc", "pTc", "tmpo", "zctm"))
            R_ecs = [[AR_res(f"ec{i}_{h}") for h in range(4)] for i in range(2)]
            R_pcs = [[AR_res(f"pc{i}_{h}") for h in range(4)] for i in range(2)]
            if not is_s:
                P.op("pool", lambda e: e.tensor_copy(out=kT_c[:, :, 0:128], in_=kT_prev[l][:]), reads=[R_kTprev[l]], writes=[R_kTc])
                P.op("pool", lambda e: e.tensor_copy(out=v_all[:, 0, :], in_=v_prev[l][:]), reads=[R_vprev[l]], writes=[R_vall[0]])
            else:
                P.dma("pool", Vc, din["cv"][l].rearrange("b k c -> k b c"), writes=[R_Vc])
                for hs in range(2):
                    P.dma("pool", Kraw, din["ck"][l, hs * 8:(hs + 1) * 8].rearrange("b k c -> k b c"), writes=[R_Kraw])
                    for kv in range(4):
                        bk = next_bank(2, 8)
                        for b in range(8):
                            for hf in range(2):
                                P.op("pe", lambda e, bk=bk, b=b, hf=hf, kv=kv: e.transpose(psb[bk][hf * 64:(hf + 1) * 64, b * 128:(b + 1) * 128], Kraw[:, b, kv * 64:(kv + 1) * 64],
                                                                                       ident[:], tile_position=(0, hf * 64)),
                                     reads=[R_Kraw, R_ident], writes=[R_ps[bk]], inc=(b == 7 and hf == 1))
                        P.op("act", lambda e, bk=bk, kv=kv, hs=hs: e.copy(out=KTc[:, kv, hs * 8:(hs + 1) * 8, :], in_=psb[bk][:, :].rearrange("p (b q) -> p b q", q=128)),
                             reads=[R_ps[bk]], writes=[R_KTc, R_xres[2], R_xres[3]])
            for c in range(2):
                wb, Rw = w_get(("w_in", 5120 + c * 512))
                for hl in range(4):
                    lhs = lambda kc, hl=hl, wb=wb: wb[:, kc, hl * 128:(hl + 1) * 128]
                    bk = inproj_feat(wb, Rw, lhs)
                    P.op("act", lambda e, bk=bk, c=c, hl=hl: e.activation(out=qT_c[:, c * 4 + hl, :], in_=ps[bk][:, 0:TOK], func=AF.Copy, scale=0.125),
                         reads=[R_ps[bk]], writes=[R_qTc])
            if is_s:
                for j in range(4):
                    P.op("pool", lambda e, j=j: e.tensor_tensor(out=qTm[:, j], in0=qT_c, in1=colmask[:, j, :].unsqueeze(1).to_broadcast([128, 8, 128]), op=ALU.mult),
                         reads=[R_qTc, R_masks], writes=[R_xres[1]])
            wb, Rw = w_get(("w_in", 6144))
            for kv in range(4):
                bk = next_bank(0, 2)
                for hf in range(2):
                    for kc in range(16):
                        P.op("pe", lambda e, bk=bk, hf=hf, kc=kc, kv=kv, wb=wb: e.matmul(ps[bk][hf * 64:(hf + 1) * 64, 0:TOK], wb[:, kc, kv * 64:(kv + 1) * 64], xT[:, kc, 0:TOK],
                                                                                      start=(kc == 0), stop=(kc == 15), tile_position=(0, hf * 64)),
                             reads=Rx + [Rw], writes=[R_ps[bk]], inc=(kc == 15 and hf == 1))
                P.op("act", lambda e, bk=bk, kv=kv: e.copy(out=kT_c[:, kv, 128:128 + TOK], in_=ps[bk][:, 0:TOK]), reads=[R_ps[bk]], writes=[R_kTc])
            for t in range(NT):
                bk = inproj_tok(wb, Rw, t, 256, c0=256)
                P.op("act", lambda e, bk=bk, t=t: e.copy(out=v_all[:, t + 1, :], in_=ps[bk][:, 0:256]), reads=[R_ps[bk]], writes=[R_vall[t + 1]])
                if (last_grp or is_s) and t == NT - 1:
                    P.op("dve", lambda e, bk=bk: e.tensor_copy(out=kv32[:, 0:256], in_=ps[bk][:, 0:256]), reads=[R_ps[bk]], writes=[R_ec])
                    bk2 = inproj_tok(wb, Rw, t, 256, c0=0)
                    P.op("dve", lambda e, bk2=bk2: e.tensor_copy(out=kv32[:, 256:512], in_=ps[bk2][:, 0:256]), reads=[R_ps[bk2]], writes=[R_ec])
                    if last_grp:
                        out_toks.append(P.dma("sp", dout["wvp"][l], kv32[:, 0:256], reads=[R_ec]))
                        out_toks.append(P.dma("sp", dout["wkp"][l], kv32[:, 256:512], reads=[R_ec]))
                    else:
                        sample_kv_out(l, kv32, R_ec)
            for c in range(2):
                wb, Rw = w_get(("w_in", 6656 + c * 512))
                for t in range(NT):
                    bk = inproj_tok(wb, Rw, t, 512)
                    gate_evac(bk, 512, sgc[:, t, c * 512:(c + 1) * 512], [R_sgc[t]], t % 2)
            w_prefetch()
            if not is_s:
                P.op("pool", lambda e: e.tensor_copy(out=kT_prev[l][:], in_=kT_c[:, :, TOK:TOK + 128]), reads=[R_kTc], writes=[R_kTprev[l]])
                P.op("pool", lambda e: e.tensor_copy(out=v_prev[l][:], in_=v_all[:, NT, :]), reads=[R_vall[NT]], writes=[R_vprev[l]])
            for t in range(NT):
                koff = 128 if (first_tile and t == 0) else 0
                nh = 2 - koff // 128
                R_smk = [AR_res(f"smk{k}") for k in range(4)]
                R_rsk = [AR_res(f"rsk{k}") for k in range(4)]

                def st_scores(kvg):
                    sb0 = 2 if kvg % 2 == 0 else 0
                    for hl in range(4):
                        sbk = sb0 + hl % 2
                        oap = ps[sbk][:, (hl // 2) * 256 + koff:(hl // 2 + 1) * 256]
                        hh = kvg * 4 + hl
                        pq = (hh % 2) * 64
                        if is_s:
                            cb = hl // 2
                            P.op("pe", lambda e, sbk=sbk, cb=cb, pq=pq, hh=hh, kvg=kvg: e.matmul(ps[sbk][:, cb * 256 + 128:cb * 256 + 256], qT_c[pq:pq + 64, hh // 2, 0:128],
                                                                                               kT_c[pq:pq + 64, kvg, 128:256], start=True, stop=True),
                                 reads=[R_qTc, R_kTc], writes=[R_ps[sbk]], inc=False)
                            for Q in range(4):
                                for j in range(4):
                                    last = (Q == 3 and j == 3)
                                    P.op("pe", lambda e, sbk=sbk, cb=cb, pq=pq, hh=hh, kvg=kvg, Q=Q, j=j: e.matmul(
                                        ps[sbk][32 * Q:32 * Q + 32, cb * 256:cb * 256 + 128], qTm[pq:pq + 64, j, hh // 2, 32 * Q:32 * Q + 32], KTc[pq:pq + 64, kvg, 4 * Q + j, :],
                                        start=(j == 0), stop=(j == 3), tile_position=(pq, 32 * Q)),
                                        reads=[R_xres[1], R_KTc], writes=[R_ps[sbk]], inc=(last and hl >= 2))
                        else:
                            P.op("pe", lambda e, oap=oap, hh=hh, pq=pq, kvg=kvg, t=t, koff=koff: e.matmul(
                                oap, qT_c[pq:pq + 64, hh // 2, t * 128:(t + 1) * 128], kT_c[pq:pq + 64, kvg, t * 128 + koff:t * 128 + 256],
                                start=True, stop=True), reads=[R_qTc, R_kTc], writes=[R_ps[sbk]], inc=(hl >= 2))

                def st_max(kvg):
                    sb0 = 2 if kvg % 2 == 0 else 0
                    for b2 in range(2):
                        P.op("dve", lambda e, b2=b2, kvg=kvg, sb0=sb0, koff=koff: e.tensor_reduce(out=sm1[:, kvg * 4 + b2:kvg * 4 + 4:2],
                                                                                      in_=ps[sb0 + b2][:, :].rearrange("p (h s) -> p h s", s=256)[:, :, koff:256],
                                                                                      axis=AX.X, op=ALU.max),
                             reads=[R_ps[sb0 + b2]], writes=[R_smk[kvg]])
                    P.op("dve", lambda e, kvg=kvg: e.tensor_scalar(out=sm2[:, kvg * 4:kvg * 4 + 4], in0=sm1[:, kvg * 4:kvg * 4 + 4], scalar1=-1.0, scalar2=None, op0=ALU.mult),
                         reads=[R_smk[kvg]], writes=[R_smk[kvg]])
                    P.op("dve", lambda e, kvg=kvg: e.tensor_tensor(out=negm[:, kvg * 4:kvg * 4 + 4], in0=sm2[:, kvg * 4:kvg * 4 + 4], in1=nsink_bc[:, kvg * 4:kvg * 4 + 4], op=ALU.min),
                         reads=[R_smk[kvg], R_lp], writes=[R_smk[kvg]])

                def st_exp(kvg):
                    sb0 = 2 if kvg % 2 == 0 else 0
                    ec, pc, Rec, Rpc = e_cs[kvg % 2], p_cs[kvg % 2], R_ecs[kvg % 2], R_pcs[kvg % 2]
                    if kvg % 2 == 0:
                        Rec = [Rec[0], Rec[1], Rec[2], Rec[3]]
                    for hl in range(4):
                        h = kvg * 4 + hl
                        sbk = sb0 + hl % 2
                        P.op("act", lambda e, hl=hl, h=h, sbk=sbk, ec=ec, koff=koff: e.activation(out=ec[:, hl, koff:256], in_=ps[sbk][:, (hl // 2) * 256 + koff:(hl // 2 + 1) * 256],
                                                                                     func=AF.Exp, bias=negm[:, h:h + 1], scale=1.0),
                             reads=[R_ps[sbk], R_smk[kvg]], writes=[Rec[hl]] + ([R_ec] if kvg % 2 == 0 and hl < 2 else []))
                        P.op("dve", lambda e, hl=hl, h=h, ec=ec, pc=pc, koff=koff: e.scalar_tensor_tensor(out=pc[:, hl, koff:256], in0=ec[:, hl, koff:256], scalar=1.0,
                                                                                             in1=Etab[:, h, koff:256], op0=ALU.mult, op1=ALU.mult, accum_out=rs[:, h:h + 1]),
                             reads=[Rec[hl], R_E], writes=[Rpc[hl], R_rsk[kvg]])

                def st_tr(kvg):
                    tbk = 4 if kvg % 2 == 0 else 7
                    pc, Rpc = p_cs[kvg % 2], R_pcs[kvg % 2]
                    srcs = []
                    for hl in range(4):
                        for h2 in range(koff // 128, 2):
                            srcs.append(pc[:, hl, h2 * 128:(h2 + 1) * 128])
                    transposes(tbk, srcs, list(Rpc))

                def st_pv(kvg):
                    tbk = 4 if kvg % 2 == 0 else 7
                    nsl = 4 * nh
                    P.op("act", lambda e, nsl=nsl, tbk=tbk: e.copy(out=pT_c[:, 0:nsl, :], in_=psb[tbk][:, 0:nsl * 128].rearrange("p (c q) -> p c q", q=128)),
                         reads=[R_ps[tbk]], writes=[R_pTc])
                    if is_s:
                        for j in range(4):
                            P.op("dve", lambda e, j=j: e.tensor_tensor(out=pTm[:, j], in0=pT_c[:, 0:8:2, :], in1=colmask[:, j, :].unsqueeze(1).to_broadcast([128, 4, 128]), op=ALU.mult),
                                 reads=[R_pTc, R_masks], writes=[R_pTm])
                    for hl in range(4):
                        h = kvg * 4 + hl
                        ob = 5 + h // 8
                        oap = ps[ob][:, (h % 8) * 64:(h % 8 + 1) * 64]
                        if is_s:
                            P.op("pe", lambda e, oap=oap, hl=hl, kvg=kvg: e.matmul(oap, pT_c[:, hl * 2 + 1, :], v_all[:, 1, kvg * 64:(kvg + 1) * 64], start=True, stop=False),
                                 reads=[R_pTc, R_vall[1]], writes=[R_ps[ob]], inc=False)
                            for Q in range(4):
                                for j in range(4):
                                    last = (Q == 3 and j == 3)
                                    P.op("pe", lambda e, ob=ob, h=h, hl=hl, kvg=kvg, Q=Q, j=j, last=last: e.matmul(
                                        ps[ob][32 * Q:32 * Q + 32, (h % 8) * 64:(h % 8 + 1) * 64], pTm[:, j, hl, 32 * Q:32 * Q + 32], Vc[:, 4 * Q + j, kvg * 64:(kvg + 1) * 64],
                                        start=False, stop=(j == 3), tile_position=(0, 32 * Q)),
                                        reads=[R_pTm, R_Vc], writes=[R_ps[ob]], inc=(last and hl == 3))
                        else:
                            for i2, h2 in enumerate(range(koff // 128, 2)):
                                P.op("pe", lambda e, oap=oap, hl=hl, i2=i2, h2=h2, kvg=kvg, t=t, nh=nh: e.matmul(
                                    oap, pT_c[:, hl * nh + i2, :], v_all[:, t + h2, kvg * 64:(kvg + 1) * 64], start=(i2 == 0), stop=(i2 == nh - 1)),
                                    reads=[R_pTc, R_vall[t + h2]], writes=[R_ps[ob]], inc=(i2 == nh - 1) and (hl == 3))

                st_scores(0)
                for kvg in range(4):
                    if kvg + 1 < 4:
                        st_scores(kvg + 1)
                    st_max(kvg)
                    if kvg >= 1:
                        st_pv(kvg - 1)
                    st_exp(kvg)
                    st_tr(kvg)
                st_pv(3)
                P.op("dve", lambda e: e.tensor_tensor(out=sm3[:, :], in0=sink_bc[:, :], in1=negm[:, :], op=ALU.add), reads=[R_small, R_lp] + R_smk, writes=[R_small])
                P.op("act", lambda e: e.activation(out=sm4[:, :], in_=sm3[:, :], func=AF.Exp), reads=[R_small], writes=[R_small])
                P.op("dve", lambda e: e.tensor_tensor(out=sm3[:, :], in0=sm4[:, :], in1=rs[:, :], op=ALU.add), reads=[R_small] + R_smk + R_rsk, writes=[R_small])
                P.op("dve", lambda e: e.reciprocal(out=sm4[:, :], in_=sm3[:, :]), reads=[R_small], writes=[R_small])
                P.op("dve", lambda e: e.tensor_scalar(out=sm3[:, :], in0=sm4[:, :], scalar1=0.5, scalar2=None, op0=ALU.mult), reads=[R_small], writes=[R_small])
                for b2 in range(2):
                    P.op("dve", lambda e, b2=b2: e.tensor_tensor(out=tmpo.rearrange("p (h d) -> p h d", d=64), in0=ps[5 + b2][:, :].rearrange("p (h d) -> p h d", d=64),
                                                                 in1=sm3[:, b2 * 8:(b2 + 1) * 8].unsqueeze(2).to_broadcast([128, 8, 64]), op=ALU.mult),
                         reads=[R_ps[5 + b2], R_small], writes=[R_tmpo])
                    P.op("pool", lambda e, b2=b2, t=t: e.tensor_tensor(out=zc_tm[:, b2 * 512:(b2 + 1) * 512], in0=tmpo, in1=sgc[:, t, b2 * 512:(b2 + 1) * 512], op=ALU.mult),
                         reads=[R_tmpo, R_sgc[t]], writes=[R_zctm])
                transposes(7, [zc_tm[:, c * 128:(c + 1) * 128] for c in range(8)], [R_zctm])
                P.op("act", lambda e, t=t: e.copy(out=zT[:, 16:24, t * 128:(t + 1) * 128], in_=psb[7][:, :].rearrange("p (c q) -> p c q", q=128)),
                     reads=[R_ps[7]], writes=[R_z[2][t]])

            if STOP_AT == "C":
                raise _Stop()
            P.barrier()
            mT = av(0, 16 * TOK).rearrange("p (c t) -> p c t", t=TOK)
            acc = av(8192, 4 * TOK * 2, F32).rearrange("p (c t) -> p c t", t=TOK)
            t2 = av(12288, TOK * 2, F32)
            R_mT, R_acc, R_t2 = [AR_res(f"mT{t}") for t in range(NT)], AR_res("acc"), AR_res("t2")
            for sc4 in range(4):
                for br, (wo, m0) in enumerate((("w_ret_o", 7680), ("w_pool_o", 9728), ("w_att_o", 11776))):
                    wbo, Rwo = w_get((wo, sc4 * 512))
                    for cl in range(4):
                        mm_group(ps[cl][:, 0:TOK], [(wbo[:, kc, cl * 128:(cl + 1) * 128], zT[:, br * 8 + kc, 0:TOK]) for kc in range(8)],
                                 R_z[br][:NT] + [Rwo], cl)
                    wbm, Rwm = w_get(("w_in", m0 + sc4 * 512))
                    for cl in range(4):
                        c = sc4 * 4 + cl
                        yb = cl
                        mb = next_bank(4, 8)
                        mm_group(ps[mb][:, 0:TOK], [(wbm[:, kc, cl * 128:(cl + 1) * 128], xT[:, kc, 0:TOK]) for kc in range(16)], Rx + [Rwm], mb)
                        thi = cl % 2
                        P.op("act", lambda e, mb=mb, thi=thi: e.activation(out=th[thi][:, 0:TOK], in_=ps[mb][:, 0:TOK], func=AF.Tanh, scale=0.5),
                             reads=[R_ps[mb]], writes=[R_th[thi]])
                        if br == 0:
                            P.op("dve", lambda e, yb=yb, thi=thi, cl=cl: e.scalar_tensor_tensor(out=acc[:, cl, :], in0=th[thi][:, 0:TOK], scalar=1.0, in1=ps[yb][:, 0:TOK],
                                                                                              op0=ALU.add, op1=ALU.mult),
                                 reads=[R_th[thi], R_ps[yb]], writes=[R_acc])
                        else:
                            P.op("dve", lambda e, yb=yb, thi=thi: e.scalar_tensor_tensor(out=t2, in0=th[thi][:, 0:TOK], scalar=1.0, in1=ps[yb][:, 0:TOK],
                                                                                       op0=ALU.add, op1=ALU.mult),
                                 reads=[R_th[thi], R_ps[yb]], writes=[R_t2])
                            if br == 1:
                                P.op("pool", lambda e, cl=cl: e.tensor_tensor(out=acc[:, cl, :], in0=acc[:, cl, :], in1=t2, op=ALU.add),
                                     reads=[R_t2, R_acc], writes=[R_acc])
                            else:
                                P.op("pool", lambda e, cl=cl, c=c: e.tensor_tensor(out=mT[:, c, :], in0=acc[:, cl, :], in1=t2, op=ALU.add),
                                     reads=[R_t2, R_acc], writes=R_mT)

            if STOP_AT == "D":
                raise _Stop()
            P.barrier()
            g_bc = av(8192, 4096, F32)
            b_bc = av(12288, 4096, F32)
            R_gb = AR_res("gbc")
            P.dma("sp", g_bc, din["ln_g"][l, :].partition_broadcast(128), writes=[R_gb])
            P.dma("sp", b_bc, din["ln_b"][l, :].partition_broadcast(128), writes=[R_gb])
            for c in range(4):
                wb, Rw = w_get(("w_out", c * 512))
                for t in range(NT):
                    bk = next_bank(0, 8)
                    mm_group(ps[bk][:, :], [(mT[:, kc, t * 128:(t + 1) * 128], wb[:, kc, :]) for kc in range(16)], [R_mT[t], Rw], bk)
                    P.op("dve", lambda e, bk=bk, t=t, c=c: e.scalar_tensor_tensor(out=xres[:, t, c * 512:(c + 1) * 512], in0=xres[:, t, c * 512:(c + 1) * 512],
                                                                                scalar=2.0 * ALPHA, in1=ps[bk][:, :], op0=ALU.mult, op1=ALU.add),
                         reads=[R_ps[bk], R_xres[t]], writes=[R_xres[t]])
            w_prefetch()
            R_lnst = [AR_res("lnst0"), AR_res("lnst1")]
            for t in range(NT):
                st_, mv_, Rst, c0 = ((stats, mv, R_lnst[0], 0) if t % 2 == 0 else (stats2, mv2, R_lnst[1], 4))
                for c in range(4):
                    P.op("dve", lambda e, t=t, c=c, st_=st_: e.bn_stats(out=st_[:, c, :], in_=xres[:, t, c * 512:(c + 1) * 512]), reads=[R_xres[t]], writes=[Rst])
                P.op("dve", lambda e, st_=st_, mv_=mv_: e.bn_aggr(out=mv_[:, :], in_=st_[:, :, :]), reads=[Rst], writes=[Rst])
                P.op("dve", lambda e, mv_=mv_, c0=c0: e.tensor_scalar(out=sm1[:, c0 + 2:c0 + 3], in0=mv_[:, 1:2], scalar1=4.0 * LN_EPS, scalar2=None, op0=ALU.add),
                     reads=[Rst], writes=[Rst])
                P.op("pool", lambda e, c0=c0: e.tensor_tensor(out=sm1[:, c0:c0 + 1], in0=sm1[:, c0 + 2:c0 + 3], in1=mhalf[:, 0:1], op=ALU.pow),
                     reads=[Rst], writes=[Rst])
                P.op("dve", lambda e, mv_=mv_, c0=c0: e.scalar_tensor_tensor(out=sm1[:, c0 + 1:c0 + 2], in0=mv_[:, 0:1], scalar=-1.0, in1=sm1[:, c0:c0 + 1], op0=ALU.mult, op1=ALU.mult),
                     reads=[Rst], writes=[Rst])
                P.op("act", lambda e, t=t, c0=c0: e.activation(out=xres[:, t, :], in_=xres[:, t, :], func=AF.Identity, bias=sm1[:, c0 + 1:c0 + 2], scale=sm1[:, c0:c0 + 1]),
                     reads=[R_xres[t], Rst], writes=[R_xres[t]])
                P.op("dve", lambda e, t=t: e.tensor_tensor(out=xres[:, t, :], in0=xres[:, t, :], in1=g_bc, op=ALU.mult), reads=[R_xres[t], R_gb], writes=[R_xres[t]])
                P.op("pool", lambda e, t=t: e.tensor_tensor(out=xres[:, t, :], in0=xres[:, t, :], in1=b_bc, op=ALU.add), reads=[R_xres[t], R_gb], writes=[R_xres[t]])
                def finish(t):
                    if l == DEPTH - 1:
                        dst = dout["ys"] if is_s else dout["yp"][tiles[t] * 128:(tiles[t] + 1) * 128, :]
                        out_toks.append(P.dma("sp", dst, xres[:, t, :], reads=[R_xres[t]]))
                    else:
                        make_xT(t)
                if t >= 1:
                    finish(t - 1)
                if t == NT - 1:
                    finish(t)

        def sample_ret(l, scT_a, R_scT, v_a, R_va, qdT_a, R_qdT, k_dec, R_kdec, cd):
            qdTm = av(5120, 2048).rearrange("p (j h i) -> p j h i", j=4, h=4)
            kdm = av(9216, 2048).rearrange("p (j c) -> p j c", j=4)
            R_qdTm, R_kdm = AR_res("qdTm"), AR_res("kdm")
            for j in range(4):
                P.op("pool", lambda e, j=j: e.tensor_tensor(out=qdTm[:, j], in0=qdT_a, in1=colmask[:, j, :].unsqueeze(1).to_broadcast([128, 4, 128]), op=ALU.mult),
                     reads=[R_qdT, R_masks], writes=[R_qdTm])
                P.op("pool", lambda e, j=j: e.tensor_scalar(out=kdm[:, j, :], in0=k_dec, scalar1=rowmask[:, j:j + 1], scalar2=None, op0=ALU.mult),
                     reads=[R_kdec, R_masks], writes=[R_kdm])
            S_old = [xres[:, 1 + i, :].rearrange("p (b v) -> p b v", v=256) for i in range(2)]
            S16f = xres[:, 3, :].bitcast(BF16)
            S16 = [S16f[:, i * 2048:(i + 1) * 2048].rearrange("p (b v) -> p b v", v=256) for i in range(2)]
            R_So = [R_xres[1], R_xres[2]]
            R_S16 = [AR_res("S16_0"), AR_res("S16_1")]
            banks = [6, 7, 0, 1]
            it = 0

            def load_state(i):
                hh_, hf_ = i // 2, i % 2
                P.dma("sp", S_old[i % 2], din["st_ret"][l, hf_ * 8:(hf_ + 1) * 8, hh_].rearrange("b d v -> d b v"), writes=[R_So[i % 2]])

            load_state(0)
            for h in range(4):
                ob = 4 + h // 2
                c0 = (h % 2) * 256
                P.op("pe", lambda e, ob=ob, c0=c0, h=h: e.matmul(ps[ob][:, c0:c0 + 256], scT_a[:, h, :], v_a[:, 0, h * 256:(h + 1) * 256], start=True, stop=False),
                     reads=[R_scT, R_va[0]], writes=[R_ps[ob]], inc=False)
                for half in range(2):
                    bi = it % 2
                    it += 1
                    if it < 8:
                        load_state(it)
                    P.op("act", lambda e, bi=bi: e.copy(out=S16[bi], in_=S_old[bi]), reads=[R_So[bi]], writes=[R_S16[bi]])
                    for b in range(8):
                        seq = half * 8 + b
                        Q, j = seq // 4, seq % 4
                        last = (half == 1 and b == 7)
                        P.op("pe", lambda e, ob=ob, c0=c0, h=h, Q=Q, j=j, b=b, bi=bi, last=last: e.matmul(
                            ps[ob][32 * Q:32 * Q + 32, c0:c0 + 256], qdTm[:, j, h, 32 * Q:32 * Q + 32], S16[bi][:, b, :], start=False, stop=(j == 3), tile_position=(0, 32 * Q)),
                            reads=[R_qdTm, R_S16[bi]], writes=[R_ps[ob]], inc=last)
                    for b in range(8):
                        seq = half * 8 + b
                        Q, j = seq // 4, seq % 4
                        bk = banks[b // 2]
                        P.op("pe", lambda e, bk=bk, b=b, Q=Q, j=j, h=h: e.matmul(
                            ps[bk][:, (b % 2) * 256:(b % 2 + 1) * 256], kdm[32 * Q:32 * Q + 32, j, h * 128:(h + 1) * 128], v_a[32 * Q:32 * Q + 32, 0, h * 256:(h + 1) * 256],
                            start=True, stop=True, tile_position=(32 * Q, 0)),
                            reads=[R_kdm, R_va[0]], writes=[R_ps[bk]], inc=(b % 2 == 1))
                    for i in range(4):
                        bk = banks[i]
                        sv = S_old[bi][:, 2 * i:2 * i + 2, :]
                        P.op("dve", lambda e, bk=bk, sv=sv, h=h: e.scalar_tensor_tensor(out=sv, in0=sv, scalar=cd[h], in1=ps[bk][:, :].rearrange("p (b v) -> p b v", v=256),
                                                                                      op0=ALU.mult, op1=ALU.add),
                             reads=[R_ps[bk], R_So[bi]], writes=[R_So[bi]])
                    out_toks.append(P.dma("sp", dout["rets"][l, half * 8:(half + 1) * 8, h].rearrange("b d v -> d b v"), S_old[bi], reads=[R_So[bi]]))

        def sample_kv_out(l, kv32, R_ec):
            out_toks.append(P.dma("sp", dout["wvs"][l, :, 0:120, :], din["cv"][l, :, 8:128, :]))
            out_toks.append(P.dma("sp", dout["wks"][l, :, 0:120, :], din["ck"][l, :, 8:128, :]))
            for b in range(16):
                out_toks.append(P.dma("sp", dout["wvs"][l, b, 120:128, :], kv32[b * 8:(b + 1) * 8, 0:256], reads=[R_ec]))
                out_toks.append(P.dma("sp", dout["wks"][l, b, 120:128, :], kv32[b * 8:(b + 1) * 8, 256:512], reads=[R_ec]))

        try:
            for gi, tiles in enumerate(GROUPS):
                if STOP_AT == "setup":
                    raise _Stop()
                if tiles == ["s"] and not ENABLE_SAMPLE:
                    continue
                if GROUP_SEL is not None and gi not in GROUP_SEL:
                    continue
                run_group(gi, tiles)
                if STOP_AT == "G0":
                    raise _Stop()
        except _Stop:
            pass

        for tk in out_toks:
            if tk is not None:
                P.wait("sp", tk)
        print("sbuf bytes remaining", nc.sbuf_bytes_remaining, flush=True)
        P.emit()
    return nc


_CACHE = {}


def kernel(x_prompt, x_sample, state_ret, cache_win_k, cache_win_v, state_pool, w_in, w_ret_o, w_pool_map,
           pool_scale, w_pool_o, attn_sinks, w_att_o, w_out, ln_g, ln_b, rel_bias):
    f = lambda a: np.ascontiguousarray(np.asarray(a, dtype=np.float32))
    if "nc" not in _CACHE:
        _CACHE["nc"] = build_program()
        _CACHE["consts"] = make_consts()
    nc = _CACHE["nc"]
    consts = _CACHE["consts"]
    shared = {"w_in": f(w_in), "w_ret_o": f(w_ret_o), "w_pool_map": f(w_pool_map), "pool_scale": f(pool_scale),
              "w_pool_o": f(w_pool_o), "attn_sinks": f(attn_sinks), "w_att_o": f(w_att_o), "w_out": f(w_out),
              "ln_g": f(ln_g), "ln_b": f(ln_b), "rel_bias": f(rel_bias)}
    for k in _CONST_SHAPES:
        shared["c_" + k] = f(consts[k]).reshape(_CONST_SHAPES[k])
    xp, xs = f(x_prompt), f(x_sample)
    sr, ck, cv, sp = f(state_ret), f(cache_win_k), f(cache_win_v), f(state_pool)
    in_maps = []
    for c in range(8):
        m = dict(shared)
        b0, b1 = 16 * c, 16 * (c + 1)
        m["xp"] = xp[c]
        m["xs"] = xs[b0:b1].reshape(128, D)
        m["st_ret"] = np.ascontiguousarray(sr[:, b0:b1])
        m["ck"] = np.ascontiguousarray(ck[:, b0:b1].reshape(2, 16, 128, 256))
        m["cv"] = np.ascontiguousarray(cv[:, b0:b1].reshape(2, 16, 128, 256))
        m["st_pool"] = np.ascontiguousarray(sp[:, b0:b1])
        in_maps.append(m)
    res = run_bass_kernel_spmd(nc, in_maps, core_ids=list(range(8)))
    r = res.results
    cat = lambda k, ax: np.concatenate([r[c][k] for c in range(8)], axis=ax)
    y_prompt = np.stack([r[c]["yp"] for c in range(8)], 0)
    y_sample = cat("ys", 0).reshape(128, 8, D)
    ret_p = np.stack([r[c]["retp"] for c in range(8)], 1)
    ret_s = cat("rets", 1)
    wk_p = np.stack([r[c]["wkp"] for c in range(8)], 1).reshape(2, 8, 128, 4, 64)
    wk_s = cat("wks", 1).reshape(2, 128, 128, 4, 64)
    wv_p = np.stack([r[c]["wvp"] for c in range(8)], 1).reshape(2, 8, 128, 4, 64)
    wv_s = cat("wvs", 1).reshape(2, 128, 128, 4, 64)
    pl_p = np.stack([r[c]["plp"] for c in range(8)], 1)
    pl_s = cat("pls", 1)
    return (y_prompt, y_sample, ret_p, ret_s, wk_p, wk_s, wv_p, wv_s, pl_p, pl_s)
```

```python
import math
from contextlib import ExitStack

import numpy as np
import concourse.bass as bass
import concourse.mybir as mybir
from concourse.bass_utils import run_bass_kernel_spmd

F32 = mybir.dt.float32
BF16 = mybir.dt.bfloat16
AF = mybir.ActivationFunctionType
ALU = mybir.AluOpType
AX = mybir.AxisListType

D = 2048
NIN = 13824
DEPTH = 2
PAST = 8192
ALPHA = (2.0 * DEPTH) ** 0.25
LN_EPS = 1e-5
RMS_EPS = 1e-6
GROUPS = [[0, 1, 2, 3], [4, 5, 6, 7], [8, 9, 10, 11], [12, 13, 14, 15], ["s"]]
ENABLE_SAMPLE = True
STRICT_EXEMPT = ("pe", "sp", "act")
GROUP_SEL = None
STOP_AT = None


class _Stop(Exception):
    pass


class Res:
    __slots__ = ("name", "w", "r", "excl")

    def __init__(self, name, excl=False):
        self.name = name
        self.w = None
        self.r = {}
        self.excl = excl


class Q:
    def __init__(self, name, sem):
        self.name = name
        self.sem = sem
        self.count = 0
        self.seen = {}
        self.ops = []


class Prog:
    def __init__(self, nc, es):
        self.nc = nc
        self.es = es
        self.sems = {}
        self.q = {}
        for name in ("pe", "act", "dve", "pool", "sp"):
            s = es.enter_context(nc.semaphore("q_" + name))
            self.sems["q_" + name] = s
            self.q[name] = Q(name, "q_" + name)
        self.dma_cnt = {}
        self.pending = {}
        self.rrq = {}

    def dma_sem(self, key):
        if key not in self.sems:
            self.sems[key] = self.es.enter_context(self.nc.semaphore(key))
            self.dma_cnt[key] = 0
        return key

    def _need(self, q, tok, waits):
        if tok is None:
            return
        k, v = tok
        if k == q.sem and q.name in STRICT_EXEMPT:
            return
        if q.seen.get(k, 0) >= v:
            return
        q.seen[k] = v
        waits.append((k, v))

    def _deps(self, q, reads, writes):
        waits = []
        for r in reads:
            self._need(q, r.w, waits)
            if r.excl:
                for k, v in r.r.items():
                    if k != q.sem:
                        self._need(q, (k, v), waits)
        for w in writes:
            self._need(q, w.w, waits)
            for k, v in w.r.items():
                self._need(q, (k, v), waits)
        return waits

    def _mark(self, tok, reads, writes):
        k, v = tok
        for r in reads:
            if r.r.get(k, 0) < v:
                r.r[k] = v
        for w in writes:
            w.w = tok
            w.r = {}

    def op(self, qname, fn, reads=(), writes=(), inc=True):
        q = self.q[qname]
        waits = self._deps(q, reads, writes)
        if inc:
            q.count += 1
            tok = (q.sem, q.count)
        else:
            tok = (q.sem, q.count + 1)
        self._mark(tok, reads, writes)
        q.ops.append((waits, fn, (q.sem, 1) if inc else None))
        return tok

    NPOOL = 40

    def dma(self, qname, out, in_, reads=(), writes=(), sem=None, **kw):
        q = self.q[qname]
        if sem is None:
            n = self.NPOOL if qname == "sp" else 12
            i = self.rrq.get(qname, 0)
            self.rrq[qname] = i + 1
            sem = f"d{qname}{i % n}"
        key = self.dma_sem(sem)
        waits = self._deps(q, reads, writes)
        if self.dma_cnt[key] > 0:
            self._need(q, (key, self.dma_cnt[key]), waits)
        self.dma_cnt[key] += 16
        tok = (key, self.dma_cnt[key])
        self._mark(tok, reads, writes)
        q.ops.append((waits, lambda e: e.dma_start(out=out, in_=in_, **kw), (key, 16)))
        self.pending[key] = self.dma_cnt[key]
        return tok

    def wait(self, qname, tok):
        q = self.q[qname]
        waits = []
        self._need(q, tok, waits)
        if waits:
            q.ops.append((waits, None, None))

    def barrier(self, queues=("pe", "act", "dve", "pool", "sp")):
        toks = [(self.q[n].sem, self.q[n].count) for n in queues if self.q[n].count > 0]
        toks += [(k, v) for k, v in self.pending.items() if not k.startswith("w_")]
        for n in queues:
            for t in toks:
                self.wait(n, t)

    def emit(self):
        nc = self.nc
        sems = self.sems
        with nc.Block() as block:
            def runner(q):
                def _(e):
                    for waits, fn, inc in q.ops:
                        for k, v in waits:
                            e.wait_ge(sems[k], v)
                        if fn is not None:
                            ins = fn(e)
                            if inc is not None:
                                ins.then_inc(sems[inc[0]], inc[1])
                return _
            block.tensor(runner(self.q["pe"]))
            block.scalar(runner(self.q["act"]))
            block.vector(runner(self.q["dve"]))
            block.gpsimd(runner(self.q["pool"]))
            block.sync(runner(self.q["sp"]))


def _t5_bucket(dist):
    d = dist.astype(np.float32)
    large = 16 + (np.log(np.maximum(d, np.float32(1.0)) / np.float32(16)) / np.float32(math.log(128 / 16))
                  * np.float32(16)).astype(np.int32)
    large = np.minimum(large, 31)
    return np.where(dist < 16, dist, large)


def make_consts():
    f32 = np.float32
    c = {}
    c["ident"] = np.eye(128, dtype=f32)
    inv = (f32(10000.0) ** (-(np.arange(64, dtype=f32)) / f32(64))).astype(f32)
    cos = np.zeros((128, 17, 64), f32)
    sin = np.zeros((128, 17, 64), f32)
    for t in range(17):
        if t < 16:
            pos = (128 * t + np.arange(128)).astype(f32)
        else:
            pos = (PAST + (np.arange(128) % 8)).astype(f32)
        ang = (pos[:, None] * inv[None, :]).astype(f32)
        cos[:, t] = np.cos(ang.astype(np.float64)).astype(f32)
        sin[:, t] = np.sin(ang.astype(np.float64)).astype(f32)
    c["rot"] = np.concatenate([cos, sin, -sin], axis=2).reshape(128, 17 * 192)
    lg = np.log1p(-np.exp2(-5.0 - np.arange(4, dtype=np.float64)))
    sc = 128.0 ** -0.5
    ret = np.zeros((2, 128, 1028), np.float64)
    idx = np.arange(128)
    for h in range(4):
        diff = idx[None, :] - idx[:, None]
        ret[0, :, h * 128:(h + 1) * 128] = np.where(diff >= 0, np.exp(lg[h] * np.maximum(diff, 0)), 0.0) * sc
        ret[0, :, 512 + h * 128: 512 + (h + 1) * 128] = np.exp(lg[h] * (idx + 1.0))[None, :]
        ret[0, :, 1024 + h] = np.exp(lg[h] * (127.0 - idx)) * sc
        b = idx // 8
        i8 = idx % 8
        same = b[:, None] == b[None, :]
        d8 = i8[None, :] - i8[:, None]
        ret[1, :, h * 128:(h + 1) * 128] = np.where(same & (d8 >= 0), np.exp(lg[h] * np.maximum(d8, 0)), 0.0) * sc
        ret[1, :, 512 + h * 128: 512 + (h + 1) * 128] = np.exp(lg[h] * (i8 + 1.0))[None, :]
        ret[1, :, 1024 + h] = np.exp(lg[h] * (7.0 - i8)) * sc
    c["ret"] = ret.astype(f32)
    c["cd"] = [[float(np.exp(lg[h] * 128.0)) for h in range(4)], [float(np.exp(lg[h] * 8.0)) for h in range(4)]]
    pm = np.zeros((6, 128, 4, 128), np.float64)
    for g, w in enumerate((2, 4, 8, 16)):
        for t in range(128):
            cnt = min(t + 1, w)
            for tp in range(max(0, t - w + 1), t + 1):
                pm[0, tp, g, t] += 1.0 / cnt
            pm[0, t, g, t] -= 1.0
            for tp in range(t - w + 1, t + 1):
                if tp >= 0:
                    pm[1, tp, g, t] += 1.0 / w
                else:
                    pm[2, 128 + tp, g, t] += 1.0 / w
            pm[1, t, g, t] -= 1.0
            b, i = t // 8, t % 8
            for ip in range(max(0, i - w + 1), i + 1):
                pm[3, b * 8 + ip, g, t] += 1.0 / w
            pm[3, t, g, t] -= 1.0
            for r in range(15):
                if r >= 16 + i - w:
                    pm[4 + b // 8, (b % 8) * 15 + r, g, t] += 1.0 / w
    c["pm"] = pm.astype(f32).transpose(1, 0, 2, 3).reshape(128, 6 * 512).copy()
    oh = np.zeros((2, 32, 256, 128), f32)
    valid = np.zeros((2, 128, 256), f32)
    q = np.arange(128)
    for s in range(256):
        dist = q + 128 - s
        v = (dist >= 0) & (dist < 128)
        bk = _t5_bucket(np.maximum(dist, 0).astype(np.int32))
        oh[0, bk[v], s, q[v]] = 1.0
        valid[0, q[v], s] = 1.0
        b, i = q // 8, q % 8
        if s < 128:
            dist = i + 128 - s
            v = dist < 128
        else:
            bp, ip = (s - 128) // 8, (s - 128) % 8
            dist = i - ip
            v = (b == bp) & (ip <= i)
        bk = _t5_bucket(np.maximum(dist, 0).astype(np.int32))
        oh[1, bk[v], s, q[v]] = 1.0
        valid[1, q[v], s] = 1.0
    c["oh"] = oh.reshape(2, 32, 256 * 128)
    c["valid"] = valid
    seq = np.arange(128) // 8
    cm = np.stack([(seq % 4 == j).astype(f32) for j in range(4)], 0)
    c["colmask"] = cm.reshape(1, 512).copy()
    c["rowmask"] = cm.T.copy()
    return c


_CONST_SHAPES = {
    "ident": [128, 128], "rot": [128, 17 * 192], "ret": [2, 128, 1028], "pm": [128, 6 * 512],
    "oh": [2, 32, 256 * 128], "valid": [2, 128, 256], "colmask": [1, 512], "rowmask": [128, 4],
}
_IN_SHAPES = {
    "xp": [2048, D], "xs": [128, D], "st_ret": [2, 16, 4, 128, 256], "ck": [2, 16, 128, 256],
    "cv": [2, 16, 128, 256], "st_pool": [2, 16, 15, 1024],
    "w_in": [2, D, NIN], "w_ret_o": [2, 1024, D], "w_pool_map": [2, 4, 256, 256], "pool_scale": [2, 1024],
    "w_pool_o": [2, 1024, D], "attn_sinks": [2, 16], "w_att_o": [2, 1024, D], "w_out": [2, D, D],
    "ln_g": [2, D], "ln_b": [2, D], "rel_bias": [32, 16],
}
_OUT_SHAPES = {
    "yp": [2048, D], "ys": [128, D], "retp": [2, 4, 128, 256], "rets": [2, 16, 4, 128, 256],
    "wkp": [2, 128, 256], "wks": [2, 16, 128, 256], "wvp": [2, 128, 256], "wvs": [2, 16, 128, 256],
    "plp": [2, 15, 1024], "pls": [2, 16, 15, 1024],
}


def _layer_wspecs(l):
    s = []
    for c0 in (0, 512, 1024, 1536, 2048, 2560):
        s.append(("w_in", l, 0, D, c0, 512))
    for c0 in (3072, 3584, 4096, 4608):
        s.append(("w_in", l, 0, D, c0, 512))
    s.append(("w_pool_map", l, 0, 0, 0, 0))
    for c0 in (5120, 5632, 6144, 6656, 7168):
        s.append(("w_in", l, 0, D, c0, 512))
    for sc4 in range(4):
        for wo, m0 in (("w_ret_o", 7680), ("w_pool_o", 9728), ("w_att_o", 11776)):
            s.append((wo, l, 0, 1024, sc4 * 512, 512))
            s.append(("w_in", l, 0, D, m0 + sc4 * 512, 512))
    for c in range(4):
        s.append(("w_out", l, 0, D, c * 512, 512))
    return s


def build_program():
    nc = bass.Bass("TRN2", target_bir_lowering=False)
    es = ExitStack()
    with es:
        din = {k: nc.dram_tensor(k, v, F32, kind="ExternalInput").ap() for k, v in _IN_SHAPES.items()}
        dc = {k: nc.dram_tensor("c_" + k, v, F32, kind="ExternalInput").ap() for k, v in _CONST_SHAPES.items()}
        dout = {k: nc.dram_tensor(k, v, F32, kind="ExternalOutput").ap() for k, v in _OUT_SHAPES.items()}
        e_dram = nc.dram_tensor("e_scr", [2, 128, 4096], BF16, kind="Internal").ap()

        P = Prog(nc, es)
        sb = lambda n, s, d: es.enter_context(nc.sbuf_tensor(n, s, d))
        cds = make_consts()["cd"]

        NTM = 4
        TOKM = NTM * 128
        xres = sb("xres", [128, NTM, D], F32)
        xT = sb("xT", [128, 16, TOKM], BF16)
        zT = sb("zT", [128, 24, TOKM], BF16)
        NWB = 3
        wbuf = [sb(f"wbuf{i}", [128, 16, 512], BF16) for i in range(NWB)]
        Etab = sb("Etab", [128, 16, 256], BF16)
        AR = 21504
        arena = sb("arena", [128, AR], BF16)
        S32 = [sb(f"S32_{l}", [128, 4, 256], F32) for l in range(2)]
        Sb = [sb(f"Sb_{l}", [128, 4, 256], BF16) for l in range(2)]
        u_prev = [sb(f"uprev{l}", [128, 1024], BF16) for l in range(2)]
        kT_prev = [sb(f"kTprev{l}", [128, 4, 128], BF16) for l in range(2)]
        v_prev = [sb(f"vprev{l}", [128, 256], BF16) for l in range(2)]
        identf = sb("identf", [128, 128], F32)
        ident = sb("ident", [128, 128], BF16)
        rot = sb("rot", [128, NTM, 192], F32)
        rett = sb("rett", [128, 1028], F32)
        pmt = sb("pmt", [128, 3, 4, 128], BF16)
        colmask = sb("colmask", [128, 4, 128], BF16)
        rowmask = sb("rowmask", [128, 4], F32)
        sink_bc = sb("sink_bc", [128, 16], F32)
        nsink_bc = sb("nsink_bc", [128, 16], F32)
        psch = sb("psch", [128, 8], F32)
        th = [sb(f"th{i}", [128, 512], F32) for i in range(2)]
        tmpA = sb("tmpA", [128, 512], F32)
        stats = sb("stats", [128, 4, 6], F32)
        mv = sb("mv", [128, 2], F32)
        stats2 = sb("stats2", [128, 4, 6], F32)
        mv2 = sb("mv2", [128, 2], F32)
        sm1 = sb("sm1", [128, 16], F32)
        sm2 = sb("sm2", [128, 16], F32)
        sm3 = sb("sm3", [128, 16], F32)
        sm4 = sb("sm4", [128, 16], F32)
        negm = sb("negm", [128, 16], F32)
        rs = sb("rs", [128, 16], F32)
        ss = sb("ss", [128, 4], F32)
        mhalf = sb("mhalf", [128, 4], F32)

        ps = [es.enter_context(nc.psum_tensor(f"ps{i}", [128, 512], F32)) for i in range(8)]
        psb = [p.bitcast(BF16) for p in ps]
        R_ps = [Res(f"ps{i}", excl=True) for i in range(8)]

        R = Res
        R_xres = [R(f"xres{t}") for t in range(NTM)]
        R_xT = [R(f"xT{t}") for t in range(NTM)]
        R_z = [[R(f"z{b}_{t}") for t in range(NTM)] for b in range(3)]
        R_w = [R(f"w{i}") for i in range(NWB)]
        R_E = R("Etab")
        R_edram = R("edram")
        R_S32 = [R("S32_0"), R("S32_1")]
        R_Sb = [R("Sb0"), R("Sb1")]
        R_uprev = [R("up0"), R("up1")]
        R_kTprev = [R("kp0"), R("kp1")]
        R_vprev = [R("vp0"), R("vp1")]
        R_ident = R("ident")
        R_identf = R("identf")
        R_rot, R_rett, R_pmt, R_masks, R_lp = R("rot"), R("rett"), R("pmt"), R("masks"), R("layerparams")
        R_th = [R("th0"), R("th1")]
        R_tmpA, R_xb, R_stats, R_small = R("tmpA"), R("xb"), R("stats"), R("small")
        R_ar = {}

        def AR_res(name):
            if name not in R_ar:
                R_ar[name] = R("ar_" + name)
            return R_ar[name]

        xb_holder = {}

        def av(off, n, dt=BF16):
            v = arena[:, off:off + n]
            return v.bitcast(F32) if dt == F32 else v

        xb = av(16384, 2048)

        wspecs = []
        for gi_, g in enumerate(GROUPS):
            if g == ["s"] and not ENABLE_SAMPLE:
                continue
            if GROUP_SEL is not None and gi_ not in GROUP_SEL:
                continue
            for l in range(DEPTH):
                wspecs += _layer_wspecs(l)
        ws = {"cur": 0, "issued": 0}

        def w_issue(i):
            name, l, r0, nr, c0, ncl = wspecs[i]
            slot = i % NWB
            if name == "w_pool_map":
                src = din[name][l].rearrange("g (kc p) d -> p g kc d", p=128)
                dst = wbuf[slot][:, 0:4, :].rearrange("p a (k d) -> p a k d", k=2)
            else:
                src = din[name][l, r0:r0 + nr, c0:c0 + ncl].rearrange("(kc p) n -> p kc n", p=128)
                dst = wbuf[slot][:, 0:nr // 128, 0:ncl]
            P.dma("pool", dst, src, writes=[R_w[slot]], sem=f"w_{slot}")

        def w_get(expect, hold=0):
            i = ws["cur"]
            assert wspecs[i][0] == expect[0] and wspecs[i][4] == expect[1], (wspecs[i], expect)
            while ws["issued"] < min(len(wspecs), i + NWB - hold):
                w_issue(ws["issued"])
                ws["issued"] += 1
            ws["cur"] += 1
            slot = i % NWB
            return wbuf[slot], R_w[slot]

        def w_prefetch():
            i = ws["cur"]
            while ws["issued"] < min(len(wspecs), i + NWB):
                w_issue(ws["issued"])
                ws["issued"] += 1

        bank_rr = {}

        def next_bank(lo, hi):
            i = bank_rr.get((lo, hi), 0)
            bank_rr[(lo, hi)] = i + 1
            return lo + i % (hi - lo)

        def mm_group(out_ap, pairs, reads, bank):
            n = len(pairs)
            for i, (lt, rh) in enumerate(pairs):
                P.op("pe", lambda e, o=out_ap, lt=lt, rh=rh, i=i, n=n: e.matmul(o, lt, rh, start=(i == 0), stop=(i == n - 1)),
                     reads=reads, writes=[R_ps[bank]], inc=(i == n - 1))

        def transposes(bank, srcs, reads):
            n = len(srcs)
            for i, s in enumerate(srcs):
                P.op("pe", lambda e, i=i, s=s, bank=bank: e.transpose(psb[bank][:, i * 128:(i + 1) * 128], s, ident[:]),
                     reads=reads + [R_ident], writes=[R_ps[bank]], inc=(i == n - 1))

        def gate_evac(bank, n, out_ap, out_res, thi):
            P.op("act", lambda e, bank=bank, n=n, thi=thi: e.activation(out=th[thi][:, 0:n], in_=ps[bank][:, 0:n], func=AF.Tanh, scale=0.5),
                 reads=[R_ps[bank]], writes=[R_th[thi]])
            P.op("dve", lambda e, bank=bank, n=n, thi=thi, o=out_ap: e.scalar_tensor_tensor(out=o, in0=th[thi][:, 0:n], scalar=1.0, in1=ps[bank][:, 0:n], op0=ALU.add, op1=ALU.mult),
                 reads=[R_th[thi], R_ps[bank]], writes=out_res)

        P.dma("sp", identf[:], dc["ident"], writes=[R_identf])
        P.op("dve", lambda e: e.tensor_copy(out=ident[:], in_=identf[:]), reads=[R_identf], writes=[R_ident])
        P.dma("pool", colmask[:].rearrange("p a b -> p (a b)"), dc["colmask"][0, :].partition_broadcast(128), writes=[R_masks])
        P.dma("sp", rowmask[:], dc["rowmask"], writes=[R_masks])
        P.op("pool", lambda e: e.memset(mhalf[:], -0.5), writes=[R_masks])
        for l in range(2):
            P.op("pool", lambda e, l=l: e.memset(S32[l][:], 0.0), writes=[R_S32[l]])
            P.op("pool", lambda e, l=l: e.memset(Sb[l][:], 0.0), writes=[R_Sb[l]])
            P.op("pool", lambda e, l=l: e.memset(u_prev[l][:], 0.0), writes=[R_uprev[l]])
            P.op("pool", lambda e, l=l: e.memset(kT_prev[l][:], 0.0), writes=[R_kTprev[l]])
            P.op("pool", lambda e, l=l: e.memset(v_prev[l][:], 0.0), writes=[R_vprev[l]])

        ohb = av(0, 4096).rearrange("p (s q) -> p s q", q=128)
        rbt = av(8192, 32, F32)
        rbh = av(8224, 16)
        rbl = av(8240, 16)
        rbh32 = av(8256, 32, F32)
        validt = av(8320, 512, F32)
        etmp = av(8832, 1024, F32)
        R_oh, R_rb, R_valid, R_etmp = AR_res("oh"), AR_res("rb"), AR_res("valid"), AR_res("etmp")
        P.dma("sp", rbt[0:32, :], din["rel_bias"], writes=[R_rb])
        P.op("act", lambda e: e.copy(out=rbh[0:32, :], in_=rbt[0:32, :]), reads=[R_rb], writes=[R_rb])
        P.op("act", lambda e: e.copy(out=rbh32[0:32, :], in_=rbh[0:32, :]), reads=[R_rb], writes=[R_rb])
        P.op("dve", lambda e: e.tensor_tensor(out=rbl[0:32, :], in0=rbt[0:32, :], in1=rbh32[0:32, :], op=ALU.subtract), reads=[R_rb], writes=[R_rb])
        for var in range(2 if ENABLE_SAMPLE else 1):
            P.dma("sp", validt, dc["valid"][var], writes=[R_valid])
            for sc in range(8):
                P.dma("pool", ohb[0:32], dc["oh"][var, :, sc * 4096:(sc + 1) * 4096].rearrange("p (s q) -> p s q", q=128), writes=[R_oh])
                bk = next_bank(0, 8)
                for s in range(32):
                    P.op("pe", lambda e, s=s, bk=bk: e.matmul(ps[bk][:, s * 16:(s + 1) * 16], ohb[0:32, s, :], rbh[0:32, :], start=True, stop=False),
                         reads=[R_oh, R_rb], writes=[R_ps[bk]], inc=False)
                    P.op("pe", lambda e, s=s, bk=bk: e.matmul(ps[bk][:, s * 16:(s + 1) * 16], ohb[0:32, s, :], rbl[0:32, :], start=False, stop=True),
                         reads=[R_oh, R_rb], writes=[R_ps[bk]], inc=(s == 31))
                P.op("act", lambda e, bk=bk: e.activation(out=etmp, in_=ps[bk][:, :], func=AF.Exp), reads=[R_ps[bk]], writes=[R_etmp])
                P.op("dve", lambda e, sc=sc: e.tensor_tensor(out=Etab[:, :, sc * 32:(sc + 1) * 32],
                                                              in0=etmp.rearrange("p (s h) -> p h s", h=16),
                                                              in1=validt[:, sc * 32:(sc + 1) * 32].unsqueeze(1).to_broadcast([128, 16, 32]),
                                                              op=ALU.mult), reads=[R_etmp, R_valid], writes=[R_E])
            P.dma("sp", e_dram[var], Etab[:].rearrange("p h s -> p (h s)"), reads=[R_E], writes=[R_edram])
        P.barrier()

        out_toks = []

        def run_group(gi, tiles):
            is_s = tiles == ["s"]
            NT = len(tiles)
            TOK = NT * 128
            var = 1 if is_s else 0
            first_tile = (not is_s) and tiles[0] == 0
            last_grp = (not is_s) and tiles[-1] == 15
            cd = cds[var]

            P.barrier()
            P.dma("sp", Etab[:].rearrange("p h s -> p (h s)"), e_dram[var], reads=[R_edram], writes=[R_E])
            for t, tl in enumerate(tiles):
                gt = 16 if is_s else tl
                P.dma("sp", rot[:, t, :], dc["rot"][:, gt * 192:(gt + 1) * 192], writes=[R_rot])
                src = din["xs"] if is_s else din["xp"][tl * 128:(tl + 1) * 128, :]
                P.dma("sp", xres[:, t, :], src, writes=[R_xres[t]])
            P.dma("sp", rett[:], dc["ret"][var], writes=[R_rett])
            pmsel = (3, 4, 5) if is_s else ((0, 1, 2))
            for i, pmi in enumerate(pmsel):
                P.dma("pool", pmt[:, i], dc["pm"][:, pmi * 512:(pmi + 1) * 512].rearrange("p (g t) -> p g t", t=128), writes=[R_pmt])

            def make_xT(t):
                P.op("act", lambda e, t=t: e.copy(out=xb[:], in_=xres[:, t, :]), reads=[R_xres[t]], writes=[R_xb])
                for hf in range(2):
                    bk = next_bank(0, 8)
                    transposes(bk, [xb[:, (hf * 8 + i) * 128:(hf * 8 + i + 1) * 128] for i in range(8)], [R_xb])
                    P.op("dve", lambda e, t=t, hf=hf, bk=bk: e.tensor_copy(out=xT[:, hf * 8:(hf + 1) * 8, t * 128:(t + 1) * 128],
                                                                         in_=psb[bk][:, :].rearrange("p (c q) -> p c q", q=128)),
                         reads=[R_ps[bk]], writes=[R_xT[t]])

            for t in range(NT):
                make_xT(t)

            for l in range(DEPTH):
                run_layer(l, tiles, is_s, NT, TOK, var, first_tile, last_grp, cd, make_xT)

        def run_layer(l, tiles, is_s, NT, TOK, var, first_tile, last_grp, cd, make_xT):
            Rx = R_xT[:NT]

            def inproj_tok(wb, Rw, t, ncols, c0=0):
                bk = next_bank(0, 2)
                mm_group(ps[bk][:, 0:ncols], [(xT[:, kc, t * 128:(t + 1) * 128], wb[:, kc, c0:c0 + ncols]) for kc in range(16)],
                         [R_xT[t], Rw], bk)
                return bk

            def inproj_feat(wb, Rw, lhs_fn, lo=0, hi=2):
                bk = next_bank(lo, hi)
                mm_group(ps[bk][:, 0:TOK], [(lhs_fn(kc), xT[:, kc, 0:TOK]) for kc in range(16)], Rx + [Rw], bk)
                return bk

            P.barrier()
            P.dma("sp", sink_bc[:], din["attn_sinks"][l, :].partition_broadcast(128), writes=[R_lp])
            P.dma("sp", psch[:], din["pool_scale"][l, :].rearrange("(c p) -> p c", p=128), writes=[R_lp], allow_slow_non_contiguous=True)
            P.op("dve", lambda e: e.tensor_scalar(out=nsink_bc[:], in0=sink_bc[:], scalar1=-1.0, scalar2=None, op0=ALU.mult), reads=[R_lp], writes=[R_lp])
            P.op("dve", lambda e: e.tensor_scalar(out=psch[:], in0=psch[:], scalar1=0.5, scalar2=None, op0=ALU.mult), reads=[R_lp], writes=[R_lp])

            if STOP_AT == "X":
                raise _Stop()
            q_rot = av(0, NT * 512).rearrange("p (t c) -> p t c", c=512)
            k_rot = av(2048, NT * 512).rearrange("p (t c) -> p t c", c=512)
            v_a = av(4096, NT * 1024).rearrange("p (t c) -> p t c", c=1024)
            sg_a = av(8192, NT * 1024).rearrange("p (t c) -> p t c", c=1024)
            k_dec = av(12288, 512)
            qT_a = av(12800, 512).rearrange("p (h i) -> p h i", i=128)
            kT_a = av(13312, 512).rearrange("p (h i) -> p h i", i=128)
            qdT_a = av(13824, 512).rearrange("p (h i) -> p h i", i=128)
            scT_a = av(14336, 512).rearrange("p (h i) -> p h i", i=128)
            za_tm = av(14848, 1024)
            rotB = av(15872, 1024, F32)
            R_qr = [AR_res(f"qrot{t}") for t in range(NT)]
            R_kr = [AR_res(f"krot{t}") for t in range(NT)]
            R_va = [AR_res(f"va{t}") for t in range(NT)]
            R_sga = [AR_res(f"sga{t}") for t in range(NT)]
            R_kdec, R_qT, R_kT, R_qdT, R_scT, R_zatm, R_rotB = (AR_res(n) for n in ("kdec", "qTa", "kTa", "qdTa", "scTa", "zatm", "rotB"))

            def rot_evac(bk, t, dst, Rdst):
                cosb = rot[:, t, 0:64].unsqueeze(1).to_broadcast([128, 8, 64])
                sinb = rot[:, t, 64:128].unsqueeze(1).to_broadcast([128, 4, 64])
                nsinb = rot[:, t, 128:192].unsqueeze(1).to_broadcast([128, 4, 64])
                x8 = ps[bk][:, :].rearrange("p (g d) -> p g d", d=64)
                x42 = ps[bk][:, :].rearrange("p (h two d) -> p h two d", two=2, d=64)
                B42 = rotB.rearrange("p (h two d) -> p h two d", two=2, d=64)
                P.op("dve", lambda e: e.tensor_tensor(out=tmpA[:, :].rearrange("p (g d) -> p g d", d=64), in0=x8, in1=cosb, op=ALU.mult),
                     reads=[R_ps[bk], R_rot], writes=[R_tmpA])
                P.op("dve", lambda e: e.tensor_tensor(out=B42[:, :, 0, :], in0=x42[:, :, 1, :], in1=nsinb, op=ALU.mult),
                     reads=[R_ps[bk], R_rot], writes=[R_rotB])
                P.op("dve", lambda e: e.tensor_tensor(out=B42[:, :, 1, :], in0=x42[:, :, 0, :], in1=sinb, op=ALU.mult),
                     reads=[R_ps[bk], R_rot], writes=[R_rotB])
                P.op("pool", lambda e: e.tensor_tensor(out=dst[:, t, :], in0=tmpA[:, :], in1=rotB, op=ALU.add),
                     reads=[R_tmpA, R_rotB], writes=[Rdst[t]])

            wb, Rw = w_get(("w_in", 0))
            for t in range(NT):
                bk = inproj_tok(wb, Rw, t, 512)
                rot_evac(bk, t, q_rot, R_qr)
            wb, Rw = w_get(("w_in", 512))
            for t in range(NT):
                bk = inproj_tok(wb, Rw, t, 512)
                rot_evac(bk, t, k_rot, R_kr)
            for c in range(2):
                wb, Rw = w_get(("w_in", 1024 + c * 512))
                for t in range(NT):
                    bk = inproj_tok(wb, Rw, t, 512)
                    P.op("act", lambda e, bk=bk, t=t, c=c: e.copy(out=v_a[:, t, c * 512:(c + 1) * 512], in_=ps[bk][:, :]),
                         reads=[R_ps[bk]], writes=[R_va[t]])
            for c in range(2):
                wb, Rw = w_get(("w_in", 2048 + c * 512))
                for t in range(NT):
                    bk = inproj_tok(wb, Rw, t, 512)
                    gate_evac(bk, 512, sg_a[:, t, c * 512:(c + 1) * 512], [R_sga[t]], t % 2)
            w_prefetch()

            dmT = rett[:, 0:512].rearrange("p (h i) -> p h i", i=128)
            qdec = rett[:, 512:1024].rearrange("p (h i) -> p h i", i=128)
            kdecs = rett[:, 1024:1028]
            for t in range(NT):
                transposes(2, [q_rot[:, t, h * 128:(h + 1) * 128] for h in range(4)] + [k_rot[:, t, h * 128:(h + 1) * 128] for h in range(4)],
                           [R_qr[t], R_kr[t]])
                pT3 = psb[2][:, :].rearrange("p (c q) -> p c q", q=128)
                P.op("act", lambda e, pT3=pT3: e.copy(out=qT_a, in_=pT3[:, 0:4, :]), reads=[R_ps[2]], writes=[R_qT])
                P.op("act", lambda e, pT3=pT3: e.copy(out=kT_a, in_=pT3[:, 4:8, :]), reads=[R_ps[2]], writes=[R_kT])
                P.op("dve", lambda e, pT3=pT3: e.tensor_tensor(out=qdT_a, in0=pT3[:, 0:4, :], in1=qdec, op=ALU.mult),
                     reads=[R_ps[2], R_rett], writes=[R_qdT])
                P.op("pool", lambda e, t=t: e.tensor_tensor(out=k_dec.rearrange("p (h d) -> p h d", d=128),
                                                            in0=k_rot[:, t, :].rearrange("p (h d) -> p h d", d=128),
                                                            in1=kdecs.unsqueeze(2).to_broadcast([128, 4, 128]), op=ALU.mult),
                     reads=[R_kr[t], R_rett], writes=[R_kdec])
                for h in range(4):
                    P.op("pe", lambda e, h=h: e.matmul(ps[3][:, h * 128:(h + 1) * 128], kT_a[:, h, :], qT_a[:, h, :], start=True, stop=True),
                         reads=[R_kT, R_qT], writes=[R_ps[3]], inc=(h == 3))
                P.op("dve", lambda e: e.tensor_tensor(out=scT_a, in0=ps[3][:, :].rearrange("p (h i) -> p h i", i=128), in1=dmT, op=ALU.mult),
                     reads=[R_ps[3], R_rett], writes=[R_scT])
                use_cross = not (first_tile and t == 0) and not is_s
                for h in range(4):
                    if is_s:
                        break
                    ob = 4 + h // 2
                    oap = ps[ob][:, (h % 2) * 256:(h % 2 + 1) * 256]
                    P.op("pe", lambda e, h=h, oap=oap, t=t, uc=use_cross: e.matmul(oap, scT_a[:, h, :], v_a[:, t, h * 256:(h + 1) * 256], start=True, stop=not uc),
                         reads=[R_scT, R_va[t]], writes=[R_ps[ob]], inc=(not use_cross) and (h % 2 == 1))
                    if use_cross:
                        P.op("pe", lambda e, h=h, oap=oap: e.matmul(oap, qdT_a[:, h, :], Sb[l][:, h, :], start=False, stop=True),
                             reads=[R_qdT, R_Sb[l]], writes=[R_ps[ob]], inc=(h % 2 == 1))
                if is_s:
                    sample_ret(l, scT_a, R_scT, v_a, R_va, qdT_a, R_qdT, k_dec, R_kdec, cd)
                else:
                    for h in range(4):
                        sbk = 6 + h // 2
                        P.op("pe", lambda e, h=h, sbk=sbk, t=t: e.matmul(ps[sbk][:, (h % 2) * 256:(h % 2 + 1) * 256], k_dec[:, h * 128:(h + 1) * 128],
                                                                       v_a[:, t, h * 256:(h + 1) * 256], start=True, stop=True),
                             reads=[R_kdec, R_va[t]], writes=[R_ps[sbk]], inc=(h % 2 == 1))
                    for h in range(4):
                        sbk = 6 + h // 2
                        P.op("dve", lambda e, h=h, sbk=sbk: e.scalar_tensor_tensor(out=S32[l][:, h, :], in0=S32[l][:, h, :], scalar=cd[h],
                                                                                   in1=ps[sbk][:, (h % 2) * 256:(h % 2 + 1) * 256], op0=ALU.mult, op1=ALU.add),
                             reads=[R_ps[sbk], R_S32[l]], writes=[R_S32[l]])
                    P.op("act", lambda e: e.copy(out=Sb[l][:], in_=S32[l][:]), reads=[R_S32[l]], writes=[R_Sb[l]])
                    if last_grp and t == NT - 1:
                        out_toks.append(P.dma("sp", dout["retp"][l].rearrange("h d v -> d h v"), S32[l][:], reads=[R_S32[l]]))
                for h in range(4):
                    ob = 4 + h // 2
                    P.op("act", lambda e, h=h, ob=ob: e.activation(out=tmpA[:, 0:256], in_=ps[ob][:, (h % 2) * 256:(h % 2 + 1) * 256], func=AF.Square,
                                                                   accum_out=ss[:, h:h + 1]),
                         reads=[R_ps[ob]], writes=[R_tmpA, R_small])
                P.op("dve", lambda e: e.tensor_scalar(out=sm1[:, 0:4], in0=ss[:, :], scalar1=4.0 / 256, scalar2=4.0 * RMS_EPS, op0=ALU.mult, op1=ALU.add),
                     reads=[R_small], writes=[R_small])
                P.op("pool", lambda e: e.tensor_tensor(out=sm2[:, 0:4], in0=sm1[:, 0:4], in1=mhalf[:, 0:4], op=ALU.pow),
                     reads=[R_small], writes=[R_small])
                for h in range(4):
                    ob = 4 + h // 2
                    P.op("dve", lambda e, h=h, ob=ob, t=t: e.scalar_tensor_tensor(out=za_tm[:, h * 256:(h + 1) * 256], in0=ps[ob][:, (h % 2) * 256:(h % 2 + 1) * 256],
                                                                                scalar=sm2[:, h:h + 1], in1=sg_a[:, t, h * 256:(h + 1) * 256], op0=ALU.mult, op1=ALU.mult),
                         reads=[R_ps[ob], R_small, R_sga[t]], writes=[R_zatm])
                transposes(2, [za_tm[:, c * 128:(c + 1) * 128] for c in range(8)], [R_zatm])
                P.op("act", lambda e, t=t: e.copy(out=zT[:, 0:8, t * 128:(t + 1) * 128], in_=psb[2][:, :].rearrange("p (c q) -> p c q", q=128)),
                     reads=[R_ps[2]], writes=[R_z[0][t]])

            if STOP_AT == "A":
                raise _Stop()
            P.barrier()
            u_all = av(0, (NT + 1) * 1024).rearrange("p (t c) -> p t c", c=1024)
            u32 = av(5120, 2048, F32)
            pT_b = av(7168, 1024).rearrange("p (c t) -> p c t", t=128)
            stb = av(8192, 2048).rearrange("p (a c) -> p a c", c=1024)
            R_u = [AR_res(f"u{t}") for t in range(NT + 1)]
            R_u32, R_pTb, R_stb = AR_res("u32"), AR_res("pTb"), AR_res("stb")
            if is_s:
                for hf in range(2):
                    P.dma("pool", stb[0:120, hf, :], din["st_pool"][l, hf * 8:(hf + 1) * 8].rearrange("b r c -> (b r) c"), writes=[R_stb])
            else:
                P.op("pool", lambda e: e.tensor_copy(out=u_all[:, 0, :], in_=u_prev[l][:]), reads=[R_uprev[l]], writes=[R_u[0]])
            want_u32 = is_s or last_grp
            for c in range(2):
                wb, Rw = w_get(("w_in", 3072 + c * 512))
                for t in range(NT):
                    bk = inproj_tok(wb, Rw, t, 512)
                    P.op("act", lambda e, bk=bk, t=t, c=c: e.copy(out=u_all[:, t + 1, c * 512:(c + 1) * 512], in_=ps[bk][:, :]),
                         reads=[R_ps[bk]], writes=[R_u[t + 1]])
                    if want_u32 and t == NT - 1:
                        lo = 0 if is_s else 64
                        P.op("dve", lambda e, bk=bk, c=c, lo=lo: e.tensor_copy(out=u32[lo:128, c * 512:(c + 1) * 512], in_=ps[bk][lo:128, :]),
                             reads=[R_ps[bk]], writes=[R_u32])
            if last_grp:
                out_toks.append(P.dma("sp", dout["plp"][l], u32[113:128, :], reads=[R_u32]))
            if is_s:
                for b in range(16):
                    out_toks.append(P.dma("sp", dout["pls"][l, b, 7:15, :], u32[b * 8:(b + 1) * 8, :], reads=[R_u32]))
                    out_toks.append(P.dma("sp", dout["pls"][l, b, 0:7, :], din["st_pool"][l, b, 8:15, :]))
            for c in range(2):
                wb, Rw = w_get(("w_in", 4096 + c * 512))
                for j in range(4):
                    bk = inproj_feat(wb, Rw, lambda kc, j=j, wb=wb: wb[:, kc, j * 128:(j + 1) * 128])
                    gate_evac(bk, TOK, zT[:, 8 + c * 4 + j, 0:TOK], R_z[1][:NT], j % 2)
            wb, Rw = w_get(("w_pool_map", 0))
            wmap = wb[:, 0:4, :].rearrange("p a (k d) -> p a k d", k=2)
            for t in range(NT):
                cur = 0 if (first_tile and t == 0) else 1
                if is_s:
                    cur = 0
                has_prev = not (first_tile and t == 0)
                for cc in range(8):
                    g = cc // 2
                    pb = 2 + cc // 4
                    oap = ps[pb][:, (cc % 4) * 128:(cc % 4 + 1) * 128]
                    pairs = [(u_all[:, t + 1, cc * 128:(cc + 1) * 128], pmt[:, cur, g, :])]
                    rds = [R_u[t + 1], R_pmt]
                    if is_s:
                        for hf in range(2):
                            pairs.append((stb[0:120, hf, cc * 128:(cc + 1) * 128], pmt[0:120, 1 + hf, g, :]))
                        rds.append(R_stb)
                    elif has_prev:
                        pairs.append((u_all[:, t, cc * 128:(cc + 1) * 128], pmt[:, 2, g, :]))
                        rds.append(R_u[t])
                    n = len(pairs)
                    for i, (lt, rh) in enumerate(pairs):
                        P.op("pe", lambda e, oap=oap, lt=lt, rh=rh, i=i, n=n: e.matmul(oap, lt, rh, start=(i == 0), stop=(i == n - 1)),
                             reads=rds, writes=[R_ps[pb]], inc=(i == n - 1) and (cc % 4 == 3))
                for hf in range(2):
                    P.op("act", lambda e, hf=hf: e.copy(out=pT_b[:, hf * 4:(hf + 1) * 4, :], in_=ps[2 + hf][:, :].rearrange("p (c t) -> p c t", t=128)),
                         reads=[R_ps[2 + hf]], writes=[R_pTb])
                for idx in range(8):
                    g, dcc = idx // 2, idx % 2
                    mb = 4 + idx // 4
                    oap = ps[mb][:, (idx % 4) * 128:(idx % 4 + 1) * 128]
                    for kc in range(2):
                        P.op("pe", lambda e, oap=oap, g=g, kc=kc, dcc=dcc: e.matmul(oap, wmap[:, g, kc, dcc * 128:(dcc + 1) * 128], pT_b[:, g * 2 + kc, :],
                                                                                   start=(kc == 0), stop=(kc == 1)),
                             reads=[Rw, R_pTb], writes=[R_ps[mb]], inc=(kc == 1) and (idx % 4 == 3))
                for idx in range(8):
                    mb = 4 + idx // 4
                    P.op("dve", lambda e, idx=idx, mb=mb, t=t: e.scalar_tensor_tensor(out=zT[:, 8 + idx, t * 128:(t + 1) * 128],
                                                                                    in0=ps[mb][:, (idx % 4) * 128:(idx % 4 + 1) * 128],
                                                                                    scalar=psch[:, idx:idx + 1], in1=zT[:, 8 + idx, t * 128:(t + 1) * 128],
                                                                                    op0=ALU.mult, op1=ALU.mult),
                         reads=[R_ps[mb], R_lp, R_z[1][t]], writes=[R_z[1][t]])
            if not is_s:
                P.op("pool", lambda e: e.tensor_copy(out=u_prev[l][:], in_=u_all[:, NT, :]), reads=[R_u[NT]], writes=[R_uprev[l]])

            if STOP_AT == "B":
                raise _Stop()
            P.barrier()
            if is_s:
                o_q, o_k, o_v, o_sg, o_e, o_p, o_pT, o_to, o_zc = 0, 1024, 2048, 2560, 3584, 5632, 6656, 7680, 8704
                Vc = av(9728, 4096).rearrange("p (b c) -> p b c", c=256)
                Kraw = av(13824, 2048).rearrange("p (b c) -> p b c", c=256)
                pTm = av(15872, 2048).rearrange("p (j h q) -> p j h q", j=4, h=4)
                qTm = xres[:, 1, :].bitcast(BF16).rearrange("p (j c q) -> p j c q", j=4, c=8)
                KTc = xres[:, 2:4, :].rearrange("p a b -> p (a b)").bitcast(BF16).rearrange("p (k s q) -> p k s q", k=4, s=16)
                R_Vc, R_Kraw, R_pTm, R_KTc = AR_res("Vc"), AR_res("Kraw"), AR_res("pTm"), AR_res("KTc")
            else:
                o_q, o_k, o_v, o_sg, o_e, o_p, o_pT, o_to, o_zc = 0, 4096, 6656, 7936, 12032, 14080, 15104, 16128, 17152
            qT_c = av(o_q, 8 * TOK).rearrange("p (c t) -> p c t", t=TOK)
            kT_c = av(o_k, 4 * (TOK + 128)).rearrange("p (c t) -> p c t", t=TOK + 128)
            v_all = av(o_v, (NT + 1) * 256).rearrange("p (t c) -> p t c", c=256)
            sgc = av(o_sg, NT * 1024).rearrange("p (t c) -> p t c", c=1024)
            e_c = av(o_e, 2048, F32).rearrange("p (h s) -> p h s", s=256)
            p_c = av(o_p, 1024).rearrange("p (h s) -> p h s", s=256)
            o_e2, o_p2 = (17920, 19968) if is_s else (18432, 20480)
            e_cs = [e_c, av(o_e2, 2048, F32).rearrange("p (h s) -> p h s", s=256)]
            p_cs = [p_c, av(o_p2, 1024).rearrange("p (h s) -> p h s", s=256)]
            pT_c = av(o_pT, 1024).rearrange("p (c q) -> p c q", q=128)
            tmpo = av(o_to, 1024, F32)
            zc_tm = av(o_zc, 1024)
            kv32 = av(o_e, 1024, F32)
            R_qTc, R_kTc, R_sgc = AR_res("qTc"), AR_res("kTc"), [AR_res(f"sgc{t}") for t in range(NT)]
            R_vall = [AR_res(f"vall{t}") for t in range(NT + 1)]
            R_ec, R_pc, R_pTc, R_tmpo, R_zctm = (AR_res(n) for n in ("ec", "pc", "pTc", "tmpo", "zctm"))
            R_ecs = [[AR_res(f"ec{i}_{h}") for h in range(4)] for i in range(2)]
            R_pcs = [[AR_res(f"pc{i}_{h}") for h in range(4)] for i in range(2)]
            if not is_s:
                P.op("pool", lambda e: e.tensor_copy(out=kT_c[:, :, 0:128], in_=kT_prev[l][:]), reads=[R_kTprev[l]], writes=[R_kTc])
                P.op("pool", lambda e: e.tensor_copy(out=v_all[:, 0, :], in_=v_prev[l][:]), reads=[R_vprev[l]], writes=[R_vall[0]])
            else:
                P.dma("pool", Vc, din["cv"][l].rearrange("b k c -> k b c"), writes=[R_Vc])
                for hs in range(2):
                    P.dma("pool", Kraw, din["ck"][l, hs * 8:(hs + 1) * 8].rearrange("b k c -> k b c"), writes=[R_Kraw])
                    for kv in range(4):
                        bk = next_bank(2, 8)
                        for b in range(8):
                            for hf in range(2):
                                P.op("pe", lambda e, bk=bk, b=b, hf=hf, kv=kv: e.transpose(psb[bk][hf * 64:(hf + 1) * 64, b * 128:(b + 1) * 128], Kraw[:, b, kv * 64:(kv + 1) * 64],
                                                                                       ident[:], tile_position=(0, hf * 64)),
                                     reads=[R_Kraw, R_ident], writes=[R_ps[bk]], inc=(b == 7 and hf == 1))
                        P.op("act", lambda e, bk=bk, kv=kv, hs=hs: e.copy(out=KTc[:, kv, hs * 8:(hs + 1) * 8, :], in_=psb[bk][:, :].rearrange("p (b q) -> p b q", q=128)),
                             reads=[R_ps[bk]], writes=[R_KTc, R_xres[2], R_xres[3]])
            for c in range(2):
                wb, Rw = w_get(("w_in", 5120 + c * 512))
                for hl in range(4):
                    lhs = lambda kc, hl=hl, wb=wb: wb[:, kc, hl * 128:(hl + 1) * 128]
                    bk = inproj_feat(wb, Rw, lhs)
                    P.op("act", lambda e, bk=bk, c=c, hl=hl: e.activation(out=qT_c[:, c * 4 + hl, :], in_=ps[bk][:, 0:TOK], func=AF.Copy, scale=0.125),
                         reads=[R_ps[bk]], writes=[R_qTc])
            if is_s:
                for j in range(4):
                    P.op("pool", lambda e, j=j: e.tensor_tensor(out=qTm[:, j], in0=qT_c, in1=colmask[:, j, :].unsqueeze(1).to_broadcast([128, 8, 128]), op=ALU.mult),
                         reads=[R_qTc, R_masks], writes=[R_xres[1]])
            wb, Rw = w_get(("w_in", 6144))
            for kv in range(4):
                bk = next_bank(0, 2)
                for hf in range(2):
                    for kc in range(16):
                        P.op("pe", lambda e, bk=bk, hf=hf, kc=kc, kv=kv, wb=wb: e.matmul(ps[bk][hf * 64:(hf + 1) * 64, 0:TOK], wb[:, kc, kv * 64:(kv + 1) * 64], xT[:, kc, 0:TOK],
                                                                                      start=(kc == 0), stop=(kc == 15), tile_position=(0, hf * 64)),
                             reads=Rx + [Rw], writes=[R_ps[bk]], inc=(kc == 15 and hf == 1))
                P.op("act", lambda e, bk=bk, kv=kv: e.copy(out=kT_c[:, kv, 128:128 + TOK], in_=ps[bk][:, 0:TOK]), reads=[R_ps[bk]], writes=[R_kTc])
            for t in range(NT):
                bk = inproj_tok(wb, Rw, t, 256, c0=256)
                P.op("act", lambda e, bk=bk, t=t: e.copy(out=v_all[:, t + 1, :], in_=ps[bk][:, 0:256]), reads=[R_ps[bk]], writes=[R_vall[t + 1]])
                if (last_grp or is_s) and t == NT - 1:
                    P.op("dve", lambda e, bk=bk: e.tensor_copy(out=kv32[:, 0:256], in_=ps[bk][:, 0:256]), reads=[R_ps[bk]], writes=[R_ec])
                    bk2 = inproj_tok(wb, Rw, t, 256, c0=0)
                    P.op("dve", lambda e, bk2=bk2: e.tensor_copy(out=kv32[:, 256:512], in_=ps[bk2][:, 0:256]), reads=[R_ps[bk2]], writes=[R_ec])
                    if last_grp:
                        out_toks.append(P.dma("sp", dout["wvp"][l], kv32[:, 0:256], reads=[R_ec]))
                        out_toks.append(P.dma("sp", dout["wkp"][l], kv32[:, 256:512], reads=[R_ec]))
                    else:
                        sample_kv_out(l, kv32, R_ec)
            for c in range(2):
                wb, Rw = w_get(("w_in", 6656 + c * 512))
                for t in range(NT):
                    bk = inproj_tok(wb, Rw, t, 512)
                    gate_evac(bk, 512, sgc[:, t, c * 512:(c + 1) * 512], [R_sgc[t]], t % 2)
            w_prefetch()
            if not is_s:
                P.op("pool", lambda e: e.tensor_copy(out=kT_prev[l][:], in_=kT_c[:, :, TOK:TOK + 128]), reads=[R_kTc], writes=[R_kTprev[l]])
                P.op("pool", lambda e: e.tensor_copy(out=v_prev[l][:], in_=v_all[:, NT, :]), reads=[R_vall[NT]], writes=[R_vprev[l]])
            for t in range(NT):
                koff = 128 if (first_tile and t == 0) else 0
                nh = 2 - koff // 128
                R_smk = [AR_res(f"smk{k}") for k in range(4)]
                R_rsk = [AR_res(f"rsk{k}") for k in range(4)]

                def st_scores(kvg):
                    sb0 = 2 if kvg % 2 == 0 else 0
                    for hl in range(4):
                        sbk = sb0 + hl % 2
                        oap = ps[sbk][:, (hl // 2) * 256 + koff:(hl // 2 + 1) * 256]
                        hh = kvg * 4 + hl
                        pq = (hh % 2) * 64
                        if is_s:
                            cb = hl // 2
                            P.op("pe", lambda e, sbk=sbk, cb=cb, pq=pq, hh=hh, kvg=kvg: e.matmul(ps[sbk][:, cb * 256 + 128:cb * 256 + 256], qT_c[pq:pq + 64, hh // 2, 0:128],
                                                                                               kT_c[pq:pq + 64, kvg, 128:256], start=True, stop=True),
                                 reads=[R_qTc, R_kTc], writes=[R_ps[sbk]], inc=False)
                            for Q in range(4):
                                for j in range(4):
                                    last = (Q == 3 and j == 3)
                                    P.op("pe", lambda e, sbk=sbk, cb=cb, pq=pq, hh=hh, kvg=kvg, Q=Q, j=j: e.matmul(
                                        ps[sbk][32 * Q:32 * Q + 32, cb * 256:cb * 256 + 128], qTm[pq:pq + 64, j, hh // 2, 32 * Q:32 * Q + 32], KTc[pq:pq + 64, kvg, 4 * Q + j, :],
                                        start=(j == 0), stop=(j == 3), tile_position=(pq, 32 * Q)),
                                        reads=[R_xres[1], R_KTc], writes=[R_ps[sbk]], inc=(last and hl >= 2))
                        else:
                            P.op("pe", lambda e, oap=oap, hh=hh, pq=pq, kvg=kvg, t=t, koff=koff: e.matmul(
                                oap, qT_c[pq:pq + 64, hh // 2, t * 128:(t + 1) * 128], kT_c[pq:pq + 64, kvg, t * 128 + koff:t * 128 + 256],
                                start=True, stop=True), reads=[R_qTc, R_kTc], writes=[R_ps[sbk]], inc=(hl >= 2))

                def st_max(kvg):
                    sb0 = 2 if kvg % 2 == 0 else 0
                    for b2 in range(2):
                        P.op("dve", lambda e, b2=b2, kvg=kvg, sb0=sb0, koff=koff: e.tensor_reduce(out=sm1[:, kvg * 4 + b2:kvg * 4 + 4:2],
                                                                                      in_=ps[sb0 + b2][:, :].rearrange("p (h s) -> p h s", s=256)[:, :, koff:256],
                                                                                      axis=AX.X, op=ALU.max),
                             reads=[R_ps[sb0 + b2]], writes=[R_smk[kvg]])
                    P.op("dve", lambda e, kvg=kvg: e.tensor_scalar(out=sm2[:, kvg * 4:kvg * 4 + 4], in0=sm1[:, kvg * 4:kvg * 4 + 4], scalar1=-1.0, scalar2=None, op0=ALU.mult),
                         reads=[R_smk[kvg]], writes=[R_smk[kvg]])
                    P.op("dve", lambda e, kvg=kvg: e.tensor_tensor(out=negm[:, kvg * 4:kvg * 4 + 4], in0=sm2[:, kvg * 4:kvg * 4 + 4], in1=nsink_bc[:, kvg * 4:kvg * 4 + 4], op=ALU.min),
                         reads=[R_smk[kvg], R_lp], writes=[R_smk[kvg]])

                def st_exp(kvg):
                    sb0 = 2 if kvg % 2 == 0 else 0
                    ec, pc, Rec, Rpc = e_cs[kvg % 2], p_cs[kvg % 2], R_ecs[kvg % 2], R_pcs[kvg % 2]
                    if kvg % 2 == 0:
                        Rec = [Rec[0], Rec[1], Rec[2], Rec[3]]
                    for hl in range(4):
                        h = kvg * 4 + hl
                        sbk = sb0 + hl % 2
                        P.op("act", lambda e, hl=hl, h=h, sbk=sbk, ec=ec, koff=koff: e.activation(out=ec[:, hl, koff:256], in_=ps[sbk][:, (hl // 2) * 256 + koff:(hl // 2 + 1) * 256],
                                                                                     func=AF.Exp, bias=negm[:, h:h + 1], scale=1.0),
                             reads=[R_ps[sbk], R_smk[kvg]], writes=[Rec[hl]] + ([R_ec] if kvg % 2 == 0 and hl < 2 else []))
                        P.op("dve", lambda e, hl=hl, h=h, ec=ec, pc=pc, koff=koff: e.scalar_tensor_tensor(out=pc[:, hl, koff:256], in0=ec[:, hl, koff:256], scalar=1.0,
                                                                                             in1=Etab[:, h, koff:256], op0=ALU.mult, op1=ALU.mult, accum_out=rs[:, h:h + 1]),
                             reads=[Rec[hl], R_E], writes=[Rpc[hl], R_rsk[kvg]])

                def st_tr(kvg):
                    tbk = 4 if kvg % 2 == 0 else 7
                    pc, Rpc = p_cs[kvg % 2], R_pcs[kvg % 2]
                    srcs = []
                    for hl in range(4):
                        for h2 in range(koff // 128, 2):
                            srcs.append(pc[:, hl, h2 * 128:(h2 + 1) * 128])
                    transposes(tbk, srcs, list(Rpc))

                def st_pv(kvg):
                    tbk = 4 if kvg % 2 == 0 else 7
                    nsl = 4 * nh
                    P.op("act", lambda e, nsl=nsl, tbk=tbk: e.copy(out=pT_c[:, 0:nsl, :], in_=psb[tbk][:, 0:nsl * 128].rearrange("p (c q) -> p c q", q=128)),
                         reads=[R_ps[tbk]], writes=[R_pTc])
                    if is_s:
                        for j in range(4):
                            P.op("dve", lambda e, j=j: e.tensor_tensor(out=pTm[:, j], in0=pT_c[:, 0:8:2, :], in1=colmask[:, j, :].unsqueeze(1).to_broadcast([128, 4, 128]), op=ALU.mult),
                                 reads=[R_pTc, R_masks], writes=[R_pTm])
                    for hl in range(4):
                        h = kvg * 4 + hl
                        ob = 5 + h // 8
                        oap = ps[ob][:, (h % 8) * 64:(h % 8 + 1) * 64]
                        if is_s:
                            P.op("pe", lambda e, oap=oap, hl=hl, kvg=kvg: e.matmul(oap, pT_c[:, hl * 2 + 1, :], v_all[:, 1, kvg * 64:(kvg + 1) * 64], start=True, stop=False),
                                 reads=[R_pTc, R_vall[1]], writes=[R_ps[ob]], inc=False)
                            for Q in range(4):
                                for j in range(4):
                                    last = (Q == 3 and j == 3)
                                    P.op("pe", lambda e, ob=ob, h=h, hl=hl, kvg=kvg, Q=Q, j=j, last=last: e.matmul(
                                        ps[ob][32 * Q:32 * Q + 32, (h % 8) * 64:(h % 8 + 1) * 64], pTm[:, j, hl, 32 * Q:32 * Q + 32], Vc[:, 4 * Q + j, kvg * 64:(kvg + 1) * 64],
                                        start=False, stop=(j == 3), tile_position=(0, 32 * Q)),
                                        reads=[R_pTm, R_Vc], writes=[R_ps[ob]], inc=(last and hl == 3))
                        else:
                            for i2, h2 in enumerate(range(koff // 128, 2)):
                                P.op("pe", lambda e, oap=oap, hl=hl, i2=i2, h2=h2, kvg=kvg, t=t, nh=nh: e.matmul(
                                    oap, pT_c[:, hl * nh + i2, :], v_all[:, t + h2, kvg * 64:(kvg + 1) * 64], start=(i2 == 0), stop=(i2 == nh - 1)),
                                    reads=[R_pTc, R_vall[t + h2]], writes=[R_ps[ob]], inc=(i2 == nh - 1) and (hl == 3))

                st_scores(0)
                for kvg in range(4):
                    if kvg + 1 < 4:
                        st_scores(kvg + 1)
                    st_max(kvg)
                    if kvg >= 1:
                        st_pv(kvg - 1)
                    st_exp(kvg)
                    st_tr(kvg)
                st_pv(3)
                P.op("dve", lambda e: e.tensor_tensor(out=sm3[:, :], in0=sink_bc[:, :], in1=negm[:, :], op=ALU.add), reads=[R_small, R_lp] + R_smk, writes=[R_small])
                P.op("act", lambda e: e.activation(out=sm4[:, :], in_=sm3[:, :], func=AF.Exp), reads=[R_small], writes=[R_small])
                P.op("dve", lambda e: e.tensor_tensor(out=sm3[:, :], in0=sm4[:, :], in1=rs[:, :], op=ALU.add), reads=[R_small] + R_smk + R_rsk, writes=[R_small])
                P.op("dve", lambda e: e.reciprocal(out=sm4[:, :], in_=sm3[:, :]), reads=[R_small], writes=[R_small])
                P.op("dve", lambda e: e.tensor_scalar(out=sm3[:, :], in0=sm4[:, :], scalar1=0.5, scalar2=None, op0=ALU.mult), reads=[R_small], writes=[R_small])
                for b2 in range(2):
                    P.op("dve", lambda e, b2=b2: e.tensor_tensor(out=tmpo.rearrange("p (h d) -> p h d", d=64), in0=ps[5 + b2][:, :].rearrange("p (h d) -> p h d", d=64),
                                                                 in1=sm3[:, b2 * 8:(b2 + 1) * 8].unsqueeze(2).to_broadcast([128, 8, 64]), op=ALU.mult),
                         reads=[R_ps[5 + b2], R_small], writes=[R_tmpo])
                    P.op("pool", lambda e, b2=b2, t=t: e.tensor_tensor(out=zc_tm[:, b2 * 512:(b2 + 1) * 512], in0=tmpo, in1=sgc[:, t, b2 * 512:(b2 + 1) * 512], op=ALU.mult),
                         reads=[R_tmpo, R_sgc[t]], writes=[R_zctm])
                transposes(7, [zc_tm[:, c * 128:(c + 1) * 128] for c in range(8)], [R_zctm])
                P.op("act", lambda e, t=t: e.copy(out=zT[:, 16:24, t * 128:(t + 1) * 128], in_=psb[7][:, :].rearrange("p (c q) -> p c q", q=128)),
                     reads=[R_ps[7]], writes=[R_z[2][t]])

            if STOP_AT == "C":
                raise _Stop()
            P.barrier()
            mT = av(0, 16 * TOK).rearrange("p (c t) -> p c t", t=TOK)
            acc = av(8192, 4 * TOK * 2, F32).rearrange("p (c t) -> p c t", t=TOK)
            t2 = av(12288, TOK * 2, F32)
            R_mT, R_acc, R_t2 = [AR_res(f"mT{t}") for t in range(NT)], AR_res("acc"), AR_res("t2")
            for sc4 in range(4):
                for br, (wo, m0) in enumerate((("w_ret_o", 7680), ("w_pool_o", 9728), ("w_att_o", 11776))):
                    wbo, Rwo = w_get((wo, sc4 * 512))
                    for cl in range(4):
                        mm_group(ps[cl][:, 0:TOK], [(wbo[:, kc, cl * 128:(cl + 1) * 128], zT[:, br * 8 + kc, 0:TOK]) for kc in range(8)],
                                 R_z[br][:NT] + [Rwo], cl)
                    wbm, Rwm = w_get(("w_in", m0 + sc4 * 512))
                    for cl in range(4):
                        c = sc4 * 4 + cl
                        yb = cl
                        mb = next_bank(4, 8)
                        mm_group(ps[mb][:, 0:TOK], [(wbm[:, kc, cl * 128:(cl + 1) * 128], xT[:, kc, 0:TOK]) for kc in range(16)], Rx + [Rwm], mb)
                        thi = cl % 2
                        P.op("act", lambda e, mb=mb, thi=thi: e.activation(out=th[thi][:, 0:TOK], in_=ps[mb][:, 0:TOK], func=AF.Tanh, scale=0.5),
                             reads=[R_ps[mb]], writes=[R_th[thi]])
                        if br == 0:
                            P.op("dve", lambda e, yb=yb, thi=thi, cl=cl: e.scalar_tensor_tensor(out=acc[:, cl, :], in0=th[thi][:, 0:TOK], scalar=1.0, in1=ps[yb][:, 0:TOK],
                                                                                              op0=ALU.add, op1=ALU.mult),
                                 reads=[R_th[thi], R_ps[yb]], writes=[R_acc])
                        else:
                            P.op("dve", lambda e, yb=yb, thi=thi: e.scalar_tensor_tensor(out=t2, in0=th[thi][:, 0:TOK], scalar=1.0, in1=ps[yb][:, 0:TOK],
                                                                                       op0=ALU.add, op1=ALU.mult),
                                 reads=[R_th[thi], R_ps[yb]], writes=[R_t2])
                            if br == 1:
                                P.op("pool", lambda e, cl=cl: e.tensor_tensor(out=acc[:, cl, :], in0=acc[:, cl, :], in1=t2, op=ALU.add),
                                     reads=[R_t2, R_acc], writes=[R_acc])
                            else:
                                P.op("pool", lambda e, cl=cl, c=c: e.tensor_tensor(out=mT[:, c, :], in0=acc[:, cl, :], in1=t2, op=ALU.add),
                                     reads=[R_t2, R_acc], writes=R_mT)

            if STOP_AT == "D":
                raise _Stop()
            P.barrier()
            g_bc = av(8192, 4096, F32)
            b_bc = av(12288, 4096, F32)
            R_gb = AR_res("gbc")
            P.dma("sp", g_bc, din["ln_g"][l, :].partition_broadcast(128), writes=[R_gb])
            P.dma("sp", b_bc, din["ln_b"][l, :].partition_broadcast(128), writes=[R_gb])
            for c in range(4):
                wb, Rw = w_get(("w_out", c * 512))
                for t in range(NT):
                    bk = next_bank(0, 8)
                    mm_group(ps[bk][:, :], [(mT[:, kc, t * 128:(t + 1) * 128], wb[:, kc, :]) for kc in range(16)], [R_mT[t], Rw], bk)
                    P.op("dve", lambda e, bk=bk, t=t, c=c: e.scalar_tensor_tensor(out=xres[:, t, c * 512:(c + 1) * 512], in0=xres[:, t, c * 512:(c + 1) * 512],
                                                                                scalar=2.0 * ALPHA, in1=ps[bk][:, :], op0=ALU.mult, op1=ALU.add),
                         reads=[R_ps[bk], R_xres[t]], writes=[R_xres[t]])
            w_prefetch()
            R_lnst = [AR_res("lnst0"), AR_res("lnst1")]
            for t in range(NT):
                st_, mv_, Rst, c0 = ((stats, mv, R_lnst[0], 0) if t % 2 == 0 else (stats2, mv2, R_lnst[1], 4))
                for c in range(4):
                    P.op("dve", lambda e, t=t, c=c, st_=st_: e.bn_stats(out=st_[:, c, :], in_=xres[:, t, c * 512:(c + 1) * 512]), reads=[R_xres[t]], writes=[Rst])
                P.op("dve", lambda e, st_=st_, mv_=mv_: e.bn_aggr(out=mv_[:, :], in_=st_[:, :, :]), reads=[Rst], writes=[Rst])
                P.op("dve", lambda e, mv_=mv_, c0=c0: e.tensor_scalar(out=sm1[:, c0 + 2:c0 + 3], in0=mv_[:, 1:2], scalar1=4.0 * LN_EPS, scalar2=None, op0=ALU.add),
                     reads=[Rst], writes=[Rst])
                P.op("pool", lambda e, c0=c0: e.tensor_tensor(out=sm1[:, c0:c0 + 1], in0=sm1[:, c0 + 2:c0 + 3], in1=mhalf[:, 0:1], op=ALU.pow),
                     reads=[Rst], writes=[Rst])
                P.op("dve", lambda e, mv_=mv_, c0=c0: e.scalar_tensor_tensor(out=sm1[:, c0 + 1:c0 + 2], in0=mv_[:, 0:1], scalar=-1.0, in1=sm1[:, c0:c0 + 1], op0=ALU.mult, op1=ALU.mult),
                     reads=[Rst], writes=[Rst])
                P.op("act", lambda e, t=t, c0=c0: e.activation(out=xres[:, t, :], in_=xres[:, t, :], func=AF.Identity, bias=sm1[:, c0 + 1:c0 + 2], scale=sm1[:, c0:c0 + 1]),
                     reads=[R_xres[t], Rst], writes=[R_xres[t]])
                P.op("dve", lambda e, t=t: e.tensor_tensor(out=xres[:, t, :], in0=xres[:, t, :], in1=g_bc, op=ALU.mult), reads=[R_xres[t], R_gb], writes=[R_xres[t]])
                P.op("pool", lambda e, t=t: e.tensor_tensor(out=xres[:, t, :], in0=xres[:, t, :], in1=b_bc, op=ALU.add), reads=[R_xres[t], R_gb], writes=[R_xres[t]])
                def finish(t):
                    if l == DEPTH - 1:
                        dst = dout["ys"] if is_s else dout["yp"][tiles[t] * 128:(tiles[t] + 1) * 128, :]
                        out_toks.append(P.dma("sp", dst, xres[:, t, :], reads=[R_xres[t]]))
                    else:
                        make_xT(t)
                if t >= 1:
                    finish(t - 1)
                if t == NT - 1:
                    finish(t)

        def sample_ret(l, scT_a, R_scT, v_a, R_va, qdT_a, R_qdT, k_dec, R_kdec, cd):
            qdTm = av(5120, 2048).rearrange("p (j h i) -> p j h i", j=4, h=4)
            kdm = av(9216, 2048).rearrange("p (j c) -> p j c", j=4)
            R_qdTm, R_kdm = AR_res("qdTm"), AR_res("kdm")
            for j in range(4):
                P.op("pool", lambda e, j=j: e.tensor_tensor(out=qdTm[:, j], in0=qdT_a, in1=colmask[:, j, :].unsqueeze(1).to_broadcast([128, 4, 128]), op=ALU.mult),
                     reads=[R_qdT, R_masks], writes=[R_qdTm])
                P.op("pool", lambda e, j=j: e.tensor_scalar(out=kdm[:, j, :], in0=k_dec, scalar1=rowmask[:, j:j + 1], scalar2=None, op0=ALU.mult),
                     reads=[R_kdec, R_masks], writes=[R_kdm])
            S_old = [xres[:, 1 + i, :].rearrange("p (b v) -> p b v", v=256) for i in range(2)]
            S16f = xres[:, 3, :].bitcast(BF16)
            S16 = [S16f[:, i * 2048:(i + 1) * 2048].rearrange("p (b v) -> p b v", v=256) for i in range(2)]
            R_So = [R_xres[1], R_xres[2]]
            R_S16 = [AR_res("S16_0"), AR_res("S16_1")]
            banks = [6, 7, 0, 1]
            it = 0

            def load_state(i):
                hh_, hf_ = i // 2, i % 2
                P.dma("sp", S_old[i % 2], din["st_ret"][l, hf_ * 8:(hf_ + 1) * 8, hh_].rearrange("b d v -> d b v"), writes=[R_So[i % 2]])

            load_state(0)
            for h in range(4):
                ob = 4 + h // 2
                c0 = (h % 2) * 256
                P.op("pe", lambda e, ob=ob, c0=c0, h=h: e.matmul(ps[ob][:, c0:c0 + 256], scT_a[:, h, :], v_a[:, 0, h * 256:(h + 1) * 256], start=True, stop=False),
                     reads=[R_scT, R_va[0]], writes=[R_ps[ob]], inc=False)
                for half in range(2):
                    bi = it % 2
                    it += 1
                    if it < 8:
                        load_state(it)
                    P.op("act", lambda e, bi=bi: e.copy(out=S16[bi], in_=S_old[bi]), reads=[R_So[bi]], writes=[R_S16[bi]])
                    for b in range(8):
                        seq = half * 8 + b
                        Q, j = seq // 4, seq % 4
                        last = (half == 1 and b == 7)
                        P.op("pe", lambda e, ob=ob, c0=c0, h=h, Q=Q, j=j, b=b, bi=bi, last=last: e.matmul(
                            ps[ob][32 * Q:32 * Q + 32, c0:c0 + 256], qdTm[:, j, h, 32 * Q:32 * Q + 32], S16[bi][:, b, :], start=False, stop=(j == 3), tile_position=(0, 32 * Q)),
                            reads=[R_qdTm, R_S16[bi]], writes=[R_ps[ob]], inc=last)
                    for b in range(8):
                        seq = half * 8 + b
                        Q, j = seq // 4, seq % 4
                        bk = banks[b // 2]
                        P.op("pe", lambda e, bk=bk, b=b, Q=Q, j=j, h=h: e.matmul(
                            ps[bk][:, (b % 2) * 256:(b % 2 + 1) * 256], kdm[32 * Q:32 * Q + 32, j, h * 128:(h + 1) * 128], v_a[32 * Q:32 * Q + 32, 0, h * 256:(h + 1) * 256],
                            start=True, stop=True, tile_position=(32 * Q, 0)),
                            reads=[R_kdm, R_va[0]], writes=[R_ps[bk]], inc=(b % 2 == 1))
                    for i in range(4):
                        bk = banks[i]
                        sv = S_old[bi][:, 2 * i:2 * i + 2, :]
                        P.op("dve", lambda e, bk=bk, sv=sv, h=h: e.scalar_tensor_tensor(out=sv, in0=sv, scalar=cd[h], in1=ps[bk][:, :].rearrange("p (b v) -> p b v", v=256),
                                                                                      op0=ALU.mult, op1=ALU.add),
                             reads=[R_ps[bk], R_So[bi]], writes=[R_So[bi]])
                    out_toks.append(P.dma("sp", dout["rets"][l, half * 8:(half + 1) * 8, h].rearrange("b d v -> d b v"), S_old[bi], reads=[R_So[bi]]))

        def sample_kv_out(l, kv32, R_ec):
            out_toks.append(P.dma("sp", dout["wvs"][l, :, 0:120, :], din["cv"][l, :, 8:128, :]))
            out_toks.append(P.dma("sp", dout["wks"][l, :, 0:120, :], din["ck"][l, :, 8:128, :]))
            for b in range(16):
                out_toks.append(P.dma("sp", dout["wvs"][l, b, 120:128, :], kv32[b * 8:(b + 1) * 8, 0:256], reads=[R_ec]))
                out_toks.append(P.dma("sp", dout["wks"][l, b, 120:128, :], kv32[b * 8:(b + 1) * 8, 256:512], reads=[R_ec]))

        try:
            for gi, tiles in enumerate(GROUPS):
                if STOP_AT == "setup":
                    raise _Stop()
                if tiles == ["s"] and not ENABLE_SAMPLE:
                    continue
                if GROUP_SEL is not None and gi not in GROUP_SEL:
                    continue
                run_group(gi, tiles)
                if STOP_AT == "G0":
                    raise _Stop()
        except _Stop:
            pass

        for tk in out_toks:
            if tk is not None:
                P.wait("sp", tk)
        print("sbuf bytes remaining", nc.sbuf_bytes_remaining, flush=True)
        P.emit()
    return nc


_CACHE = {}


def kernel(x_prompt, x_sample, state_ret, cache_win_k, cache_win_v, state_pool, w_in, w_ret_o, w_pool_map,
           pool_scale, w_pool_o, attn_sinks, w_att_o, w_out, ln_g, ln_b, rel_bias):
    f = lambda a: np.ascontiguousarray(np.asarray(a, dtype=np.float32))
    if "nc" not in _CACHE:
        _CACHE["nc"] = build_program()
        _CACHE["consts"] = make_consts()
    nc = _CACHE["nc"]
    consts = _CACHE["consts"]
    shared = {"w_in": f(w_in), "w_ret_o": f(w_ret_o), "w_pool_map": f(w_pool_map), "pool_scale": f(pool_scale),
              "w_pool_o": f(w_pool_o), "attn_sinks": f(attn_sinks), "w_att_o": f(w_att_o), "w_out": f(w_out),
              "ln_g": f(ln_g), "ln_b": f(ln_b), "rel_bias": f(rel_bias)}
    for k in _CONST_SHAPES:
        shared["c_" + k] = f(consts[k]).reshape(_CONST_SHAPES[k])
    xp, xs = f(x_prompt), f(x_sample)
    sr, ck, cv, sp = f(state_ret), f(cache_win_k), f(cache_win_v), f(state_pool)
    in_maps = []
    for c in range(8):
        m = dict(shared)
        b0, b1 = 16 * c, 16 * (c + 1)
        m["xp"] = xp[c]
        m["xs"] = xs[b0:b1].reshape(128, D)
        m["st_ret"] = np.ascontiguousarray(sr[:, b0:b1])
        m["ck"] = np.ascontiguousarray(ck[:, b0:b1].reshape(2, 16, 128, 256))
        m["cv"] = np.ascontiguousarray(cv[:, b0:b1].reshape(2, 16, 128, 256))
        m["st_pool"] = np.ascontiguousarray(sp[:, b0:b1])
        in_maps.append(m)
    res = run_bass_kernel_spmd(nc, in_maps, core_ids=list(range(8)))
    r = res.results
    cat = lambda k, ax: np.concatenate([r[c][k] for c in range(8)], axis=ax)
    y_prompt = np.stack([r[c]["yp"] for c in range(8)], 0)
    y_sample = cat("ys", 0).reshape(128, 8, D)
    ret_p = np.stack([r[c]["retp"] for c in range(8)], 1)
    ret_s = cat("rets", 1)
    wk_p = np.stack([r[c]["wkp"] for c in range(8)], 1).reshape(2, 8, 128, 4, 64)
    wk_s = cat("wks", 1).reshape(2, 128, 128, 4, 64)
    wv_p = np.stack([r[c]["wvp"] for c in range(8)], 1).reshape(2, 8, 128, 4, 64)
    wv_s = cat("wvs", 1).reshape(2, 128, 128, 4, 64)
    pl_p = np.stack([r[c]["plp"] for c in range(8)], 1)
    pl_s = cat("pls", 1)
    return (y_prompt, y_sample, ret_p, ret_s, wk_p, wk_s, wv_p, wv_s, pl_p, pl_s)
```

```python
import math
from contextlib import ExitStack

import numpy as np
import concourse.bass as bass
import concourse.mybir as mybir
from concourse.bass_utils import run_bass_kernel_spmd

F32 = mybir.dt.float32
BF16 = mybir.dt.bfloat16
AF = mybir.ActivationFunctionType
ALU = mybir.AluOpType
AX = mybir.AxisListType

D = 2048
NIN = 13824
DEPTH = 2
PAST = 8192
ALPHA = (2.0 * DEPTH) ** 0.25
LN_EPS = 1e-5
RMS_EPS = 1e-6
GROUPS = [[0, 1, 2, 3], [4, 5, 6, 7], [8, 9, 10, 11], [12, 13, 14, 15], ["s"]]
ENABLE_SAMPLE = True
STRICT_EXEMPT = ("pe", "sp", "act")
GROUP_SEL = None
STOP_AT = None


class _Stop(Exception):
    pass


class Res:
    __slots__ = ("name", "w", "r", "excl")

    def __init__(self, name, excl=False):
        self.name = name
        self.w = None
        self.r = {}
        self.excl = excl


class Q:
    def __init__(self, name, sem):
        self.name = name
        self.sem = sem
        self.count = 0
        self.seen = {}
        self.ops = []


class Prog:
    def __init__(self, nc, es):
        self.nc = nc
        self.es = es
        self.sems = {}
        self.q = {}
        for name in ("pe", "act", "dve", "pool", "sp"):
            s = es.enter_context(nc.semaphore("q_" + name))
            self.sems["q_" + name] = s
            self.q[name] = Q(name, "q_" + name)
        self.dma_cnt = {}
        self.pending = {}
        self.rrq = {}

    def dma_sem(self, key):
        if key not in self.sems:
            self.sems[key] = self.es.enter_context(self.nc.semaphore(key))
            self.dma_cnt[key] = 0
        return key

    def _need(self, q, tok, waits):
        if tok is None:
            return
        k, v = tok
        if k == q.sem and q.name in STRICT_EXEMPT:
            return
        if q.seen.get(k, 0) >= v:
            return
        q.seen[k] = v
        waits.append((k, v))

    def _deps(self, q, reads, writes):
        waits = []
        for r in reads:
            self._need(q, r.w, waits)
            if r.excl:
                for k, v in r.r.items():
                    if k != q.sem:
                        self._need(q, (k, v), waits)
        for w in writes:
            self._need(q, w.w, waits)
            for k, v in w.r.items():
                self._need(q, (k, v), waits)
        return waits

    def _mark(self, tok, reads, writes):
        k, v = tok
        for r in reads:
            if r.r.get(k, 0) < v:
                r.r[k] = v
        for w in writes:
            w.w = tok
            w.r = {}

    def op(self, qname, fn, reads=(), writes=(), inc=True):
        q = self.q[qname]
        waits = self._deps(q, reads, writes)
        if inc:
            q.count += 1
            tok = (q.sem, q.count)
        else:
            tok = (q.sem, q.count + 1)
        self._mark(tok, reads, writes)
        q.ops.append((waits, fn, (q.sem, 1) if inc else None))
        return tok

    NPOOL = 40

    def dma(self, qname, out, in_, reads=(), writes=(), sem=None, **kw):
        q = self.q[qname]
        if sem is None:
            n = self.NPOOL if qname == "sp" else 12
            i = self.rrq.get(qname, 0)
            self.rrq[qname] = i + 1
            sem = f"d{qname}{i % n}"
        key = self.dma_sem(sem)
        waits = self._deps(q, reads, writes)
        if self.dma_cnt[key] > 0:
            self._need(q, (key, self.dma_cnt[key]), waits)
        self.dma_cnt[key] += 16
        tok = (key, self.dma_cnt[key])
        self._mark(tok, reads, writes)
        q.ops.append((waits, lambda e: e.dma_start(out=out, in_=in_, **kw), (key, 16)))
        self.pending[key] = self.dma_cnt[key]
        return tok

    def wait(self, qname, tok):
        q = self.q[qname]
        waits = []
        self._need(q, tok, waits)
        if waits:
            q.ops.append((waits, None, None))

    def barrier(self, queues=("pe", "act", "dve", "pool", "sp")):
        toks = [(self.q[n].sem, self.q[n].count) for n in queues if self.q[n].count > 0]
        toks += [(k, v) for k, v in self.pending.items() if not k.startswith("w_")]
        for n in queues:
            for t in toks:
                self.wait(n, t)

    def emit(self):
        nc = self.nc
        sems = self.sems
        with nc.Block() as block:
            def runner(q):
                def _(e):
                    for waits, fn, inc in q.ops:
                        for k, v in waits:
                            e.wait_ge(sems[k], v)
                        if fn is not None:
                            ins = fn(e)
                            if inc is not None:
                                ins.then_inc(sems[inc[0]], inc[1])
                return _
            block.tensor(runner(self.q["pe"]))
            block.scalar(runner(self.q["act"]))
            block.vector(runner(self.q["dve"]))
            block.gpsimd(runner(self.q["pool"]))
            block.sync(runner(self.q["sp"]))


def _t5_bucket(dist):
    d = dist.astype(np.float32)
    large = 16 + (np.log(np.maximum(d, np.float32(1.0)) / np.float32(16)) / np.float32(math.log(128 / 16))
                  * np.float32(16)).astype(np.int32)
    large = np.minimum(large, 31)
    return np.where(dist < 16, dist, large)


def make_consts():
    f32 = np.float32
    c = {}
    c["ident"] = np.eye(128, dtype=f32)
    inv = (f32(10000.0) ** (-(np.arange(64, dtype=f32)) / f32(64))).astype(f32)
    cos = np.zeros((128, 17, 64), f32)
    sin = np.zeros((128, 17, 64), f32)
    for t in range(17):
        if t < 16:
            pos = (128 * t + np.arange(128)).astype(f32)
        else:
            pos = (PAST + (np.arange(128) % 8)).astype(f32)
        ang = (pos[:, None] * inv[None, :]).astype(f32)
        cos[:, t] = np.cos(ang.astype(np.float64)).astype(f32)
        sin[:, t] = np.sin(ang.astype(np.float64)).astype(f32)
    c["rot"] = np.concatenate([cos, sin, -sin], axis=2).reshape(128, 17 * 192)
    lg = np.log1p(-np.exp2(-5.0 - np.arange(4, dtype=np.float64)))
    sc = 128.0 ** -0.5
    ret = np.zeros((2, 128, 1028), np.float64)
    idx = np.arange(128)
    for h in range(4):
        diff = idx[None, :] - idx[:, None]
        ret[0, :, h * 128:(h + 1) * 128] = np.where(diff >= 0, np.exp(lg[h] * np.maximum(diff, 0)), 0.0) * sc
        ret[0, :, 512 + h * 128: 512 + (h + 1) * 128] = np.exp(lg[h] * (idx + 1.0))[None, :]
        ret[0, :, 1024 + h] = np.exp(lg[h] * (127.0 - idx)) * sc
        b = idx // 8
        i8 = idx % 8
        same = b[:, None] == b[None, :]
        d8 = i8[None, :] - i8[:, None]
        ret[1, :, h * 128:(h + 1) * 128] = np.where(same & (d8 >= 0), np.exp(lg[h] * np.maximum(d8, 0)), 0.0) * sc
        ret[1, :, 512 + h * 128: 512 + (h + 1) * 128] = np.exp(lg[h] * (i8 + 1.0))[None, :]
        ret[1, :, 1024 + h] = np.exp(lg[h] * (7.0 - i8)) * sc
    c["ret"] = ret.astype(f32)
    c["cd"] = [[float(np.exp(lg[h] * 128.0)) for h in range(4)], [float(np.exp(lg[h] * 8.0)) for h in range(4)]]
    pm = np.zeros((6, 128, 4, 128), np.float64)
    for g, w in enumerate((2, 4, 8, 16)):
        for t in range(128):
            cnt = min(t + 1, w)
            for tp in range(max(0, t - w + 1), t + 1):
                pm[0, tp, g, t] += 1.0 / cnt
            pm[0, t, g, t] -= 1.0
            for tp in range(t - w + 1, t + 1):
                if tp >= 0:
                    pm[1, tp, g, t] += 1.0 / w
                else:
                    pm[2, 128 + tp, g, t] += 1.0 / w
            pm[1, t, g, t] -= 1.0
            b, i = t // 8, t % 8
            for ip in range(max(0, i - w + 1), i + 1):
                pm[3, b * 8 + ip, g, t] += 1.0 / w
            pm[3, t, g, t] -= 1.0
            for r in range(15):
                if r >= 16 + i - w:
                    pm[4 + b // 8, (b % 8) * 15 + r, g, t] += 1.0 / w
    c["pm"] = pm.astype(f32).transpose(1, 0, 2, 3).reshape(128, 6 * 512).copy()
    oh = np.zeros((2, 32, 256, 128), f32)
    valid = np.zeros((2, 128, 256), f32)
    q = np.arange(128)
    for s in range(256):
        dist = q + 128 - s
        v = (dist >= 0) & (dist < 128)
        bk = _t5_bucket(np.maximum(dist, 0).astype(np.int32))
        oh[0, bk[v], s, q[v]] = 1.0
        valid[0, q[v], s] = 1.0
        b, i = q // 8, q % 8
        if s < 128:
            dist = i + 128 - s
            v = dist < 128
        else:
            bp, ip = (s - 128) // 8, (s - 128) % 8
            dist = i - ip
            v = (b == bp) & (ip <= i)
        bk = _t5_bucket(np.maximum(dist, 0).astype(np.int32))
        oh[1, bk[v], s, q[v]] = 1.0
        valid[1, q[v], s] = 1.0
    c["oh"] = oh.reshape(2, 32, 256 * 128)
    c["valid"] = valid
    seq = np.arange(128) // 8
    cm = np.stack([(seq % 4 == j).astype(f32) for j in range(4)], 0)
    c["colmask"] = cm.reshape(1, 512).copy()
    c["rowmask"] = cm.T.copy()
    return c


_CONST_SHAPES = {
    "ident": [128, 128], "rot": [128, 17 * 192], "ret": [2, 128, 1028], "pm": [128, 6 * 512],
    "oh": [2, 32, 256 * 128], "valid": [2, 128, 256], "colmask": [1, 512], "rowmask": [128, 4],
}
_IN_SHAPES = {
    "xp": [2048, D], "xs": [128, D], "st_ret": [2, 16, 4, 128, 256], "ck": [2, 16, 128, 256],
    "cv": [2, 16, 128, 256], "st_pool": [2, 16, 15, 1024],
    "w_in": [2, D, NIN], "w_ret_o": [2, 1024, D], "w_pool_map": [2, 4, 256, 256], "pool_scale": [2, 1024],
    "w_pool_o": [2, 1024, D], "attn_sinks": [2, 16], "w_att_o": [2, 1024, D], "w_out": [2, D, D],
    "ln_g": [2, D], "ln_b": [2, D], "rel_bias": [32, 16],
}
_OUT_SHAPES = {
    "yp": [2048, D], "ys": [128, D], "retp": [2, 4, 128, 256], "rets": [2, 16, 4, 128, 256],
    "wkp": [2, 128, 256], "wks": [2, 16, 128, 256], "wvp": [2, 128, 256], "wvs": [2, 16, 128, 256],
    "plp": [2, 15, 1024], "pls": [2, 16, 15, 1024],
}


def _layer_wspecs(l):
    s = []
    for c0 in (0, 512, 1024, 1536, 2048, 2560):
        s.append(("w_in", l, 0, D, c0, 512))
    for c0 in (3072, 3584, 4096, 4608):
        s.append(("w_in", l, 0, D, c0, 512))
    s.append(("w_pool_map", l, 0, 0, 0, 0))
    for c0 in (5120, 5632, 6144, 6656, 7168):
        s.append(("w_in", l, 0, D, c0, 512))
    for sc4 in range(4):
        for wo, m0 in (("w_ret_o", 7680), ("w_pool_o", 9728), ("w_att_o", 11776)):
            s.append((wo, l, 0, 1024, sc4 * 512, 512))
            s.append(("w_in", l, 0, D, m0 + sc4 * 512, 512))
    for c in range(4):
        s.append(("w_out", l, 0, D, c * 512, 512))
    return s


def build_program():
    nc = bass.Bass("TRN2", target_bir_lowering=False)
    es = ExitStack()
    with es:
        din = {k: nc.dram_tensor(k, v, F32, kind="ExternalInput").ap() for k, v in _IN_SHAPES.items()}
        dc = {k: nc.dram_tensor("c_" + k, v, F32, kind="ExternalInput").ap() for k, v in _CONST_SHAPES.items()}
        dout = {k: nc.dram_tensor(k, v, F32, kind="ExternalOutput").ap() for k, v in _OUT_SHAPES.items()}
        e_dram = nc.dram_tensor("e_scr", [2, 128, 4096], BF16, kind="Internal").ap()

        P = Prog(nc, es)
        sb = lambda n, s, d: es.enter_context(nc.sbuf_tensor(n, s, d))
        cds = make_consts()["cd"]

        NTM = 4
        TOKM = NTM * 128
        xres = sb("xres", [128, NTM, D], F32)
        xT = sb("xT", [128, 16, TOKM], BF16)
        zT = sb("zT", [128, 24, TOKM], BF16)
        NWB = 3
        wbuf = [sb(f"wbuf{i}", [128, 16, 512], BF16) for i in range(NWB)]
        Etab = sb("Etab", [128, 16, 256], BF16)
        AR = 21504
        arena = sb("arena", [128, AR], BF16)
        S32 = [sb(f"S32_{l}", [128, 4, 256], F32) for l in range(2)]
        Sb = [sb(f"Sb_{l}", [128, 4, 256], BF16) for l in range(2)]
        u_prev = [sb(f"uprev{l}", [128, 1024], BF16) for l in range(2)]
        kT_prev = [sb(f"kTprev{l}", [128, 4, 128], BF16) for l in range(2)]
        v_prev = [sb(f"vprev{l}", [128, 256], BF16) for l in range(2)]
        identf = sb("identf", [128, 128], F32)
        ident = sb("ident", [128, 128], BF16)
        rot = sb("rot", [128, NTM, 192], F32)
        rett = sb("rett", [128, 1028], F32)
        pmt = sb("pmt", [128, 3, 4, 128], BF16)
        colmask = sb("colmask", [128, 4, 128], BF16)
        rowmask = sb("rowmask", [128, 4], F32)
        sink_bc = sb("sink_bc", [128, 16], F32)
        nsink_bc = sb("nsink_bc", [128, 16], F32)
        psch = sb("psch", [128, 8], F32)
        th = [sb(f"th{i}", [128, 512], F32) for i in range(2)]
        tmpA = sb("tmpA", [128, 512], F32)
        stats = sb("stats", [128, 4, 6], F32)
        mv = sb("mv", [128, 2], F32)
        stats2 = sb("stats2", [128, 4, 6], F32)
        mv2 = sb("mv2", [128, 2], F32)
        sm1 = sb("sm1", [128, 16], F32)
        sm2 = sb("sm2", [128, 16], F32)
        sm3 = sb("sm3", [128, 16], F32)
        sm4 = sb("sm4", [128, 16], F32)
        negm = sb("negm", [128, 16], F32)
        rs = sb("rs", [128, 16], F32)
        ss = sb("ss", [128, 4], F32)
        mhalf = sb("mhalf", [128, 4], F32)

        ps = [es.enter_context(nc.psum_tensor(f"ps{i}", [128, 512], F32)) for i in range(8)]
        psb = [p.bitcast(BF16) for p in ps]
        R_ps = [Res(f"ps{i}", excl=True) for i in range(8)]

        R = Res
        R_xres = [R(f"xres{t}") for t in range(NTM)]
        R_xT = [R(f"xT{t}") for t in range(NTM)]
        R_z = [[R(f"z{b}_{t}") for t in range(NTM)] for b in range(3)]
        R_w = [R(f"w{i}") for i in range(NWB)]
        R_E = R("Etab")
        R_edram = R("edram")
        R_S32 = [R("S32_0"), R("S32_1")]
        R_Sb = [R("Sb0"), R("Sb1")]
        R_uprev = [R("up0"), R("up1")]
        R_kTprev = [R("kp0"), R("kp1")]
        R_vprev = [R("vp0"), R("vp1")]
        R_ident = R("ident")
        R_identf = R("identf")
        R_rot, R_rett, R_pmt, R_masks, R_lp = R("rot"), R("rett"), R("pmt"), R("masks"), R("layerparams")
        R_th = [R("th0"), R("th1")]
        R_tmpA, R_xb, R_stats, R_small = R("tmpA"), R("xb"), R("stats"), R("small")
        R_ar = {}

        def AR_res(name):
            if name not in R_ar:
                R_ar[name] = R("ar_" + name)
            return R_ar[name]

        xb_holder = {}

        def av(off, n, dt=BF16):
            v = arena[:, off:off + n]
            return v.bitcast(F32) if dt == F32 else v

        xb = av(16384, 2048)

        wspecs = []
        for gi_, g in enumerate(GROUPS):
            if g == ["s"] and not ENABLE_SAMPLE:
                continue
            if GROUP_SEL is not None and gi_ not in GROUP_SEL:
                continue
            for l in range(DEPTH):
                wspecs += _layer_wspecs(l)
        ws = {"cur": 0, "issued": 0}

        def w_issue(i):
            name, l, r0, nr, c0, ncl = wspecs[i]
            slot = i % NWB
            if name == "w_pool_map":
                src = din[name][l].rearrange("g (kc p) d -> p g kc d", p=128)
                dst = wbuf[slot][:, 0:4, :].rearrange("p a (k d) -> p a k d", k=2)
            else:
                src = din[name][l, r0:r0 + nr, c0:c0 + ncl].rearrange("(kc p) n -> p kc n", p=128)
                dst = wbuf[slot][:, 0:nr // 128, 0:ncl]
            P.dma("pool", dst, src, writes=[R_w[slot]], sem=f"w_{slot}")

        def w_get(expect, hold=0):
            i = ws["cur"]
            assert wspecs[i][0] == expect[0] and wspecs[i][4] == expect[1], (wspecs[i], expect)
            while ws["issued"] < min(len(wspecs), i + NWB - hold):
                w_issue(ws["issued"])
                ws["issued"] += 1
            ws["cur"] += 1
            slot = i % NWB
            return wbuf[slot], R_w[slot]

        def w_prefetch():
            i = ws["cur"]
            while ws["issued"] < min(len(wspecs), i + NWB):
                w_issue(ws["issued"])
                ws["issued"] += 1

        bank_rr = {}

        def next_bank(lo, hi):
            i = bank_rr.get((lo, hi), 0)
            bank_rr[(lo, hi)] = i + 1
            return lo + i % (hi - lo)

        def mm_group(out_ap, pairs, reads, bank):
            n = len(pairs)
            for i, (lt, rh) in enumerate(pairs):
                P.op("pe", lambda e, o=out_ap, lt=lt, rh=rh, i=i, n=n: e.matmul(o, lt, rh, start=(i == 0), stop=(i == n - 1)),
                     reads=reads, writes=[R_ps[bank]], inc=(i == n - 1))

        def transposes(bank, srcs, reads):
            n = len(srcs)
            for i, s in enumerate(srcs):
                P.op("pe", lambda e, i=i, s=s, bank=bank: e.transpose(psb[bank][:, i * 128:(i + 1) * 128], s, ident[:]),
                     reads=reads + [R_ident], writes=[R_ps[bank]], inc=(i == n - 1))

        def gate_evac(bank, n, out_ap, out_res, thi):
            P.op("act", lambda e, bank=bank, n=n, thi=thi: e.activation(out=th[thi][:, 0:n], in_=ps[bank][:, 0:n], func=AF.Tanh, scale=0.5),
                 reads=[R_ps[bank]], writes=[R_th[thi]])
            P.op("dve", lambda e, bank=bank, n=n, thi=thi, o=out_ap: e.scalar_tensor_tensor(out=o, in0=th[thi][:, 0:n], scalar=1.0, in1=ps[bank][:, 0:n], op0=ALU.add, op1=ALU.mult),
                 reads=[R_th[thi], R_ps[bank]], writes=out_res)

        P.dma("sp", identf[:], dc["ident"], writes=[R_identf])
        P.op("dve", lambda e: e.tensor_copy(out=ident[:], in_=identf[:]), reads=[R_identf], writes=[R_ident])
        P.dma("pool", colmask[:].rearrange("p a b -> p (a b)"), dc["colmask"][0, :].partition_broadcast(128), writes=[R_masks])
        P.dma("sp", rowmask[:], dc["rowmask"], writes=[R_masks])
        P.op("pool", lambda e: e.memset(mhalf[:], -0.5), writes=[R_masks])
        for l in range(2):
            P.op("pool", lambda e, l=l: e.memset(S32[l][:], 0.0), writes=[R_S32[l]])
            P.op("pool", lambda e, l=l: e.memset(Sb[l][:], 0.0), writes=[R_Sb[l]])
            P.op("pool", lambda e, l=l: e.memset(u_prev[l][:], 0.0), writes=[R_uprev[l]])
            P.op("pool", lambda e, l=l: e.memset(kT_prev[l][:], 0.0), writes=[R_kTprev[l]])
            P.op("pool", lambda e, l=l: e.memset(v_prev[l][:], 0.0), writes=[R_vprev[l]])

        ohb = av(0, 4096).rearrange("p (s q) -> p s q", q=128)
        rbt = av(8192, 32, F32)
        rbh = av(8224, 16)
        rbl = av(8240, 16)
        rbh32 = av(8256, 32, F32)
        validt = av(8320, 512, F32)
        etmp = av(8832, 1024, F32)
        R_oh, R_rb, R_valid, R_etmp = AR_res("oh"), AR_res("rb"), AR_res("valid"), AR_res("etmp")
        P.dma("sp", rbt[0:32, :], din["rel_bias"], writes=[R_rb])
        P.op("act", lambda e: e.copy(out=rbh[0:32, :], in_=rbt[0:32, :]), reads=[R_rb], writes=[R_rb])
        P.op("dve", lambda e: e.tensor_copy(out=rbh32[0:32, :], in_=rbh[0:32, :]), reads=[R_rb], writes=[R_rb])
        P.op("dve", lambda e: e.tensor_tensor(out=rbl[0:32, :], in0=rbt[0:32, :], in1=rbh32[0:32, :], op=ALU.subtract), reads=[R_rb], writes=[R_rb])
        for var in range(2 if ENABLE_SAMPLE else 1):
            P.dma("sp", validt, dc["valid"][var], writes=[R_valid])
            for sc in range(8):
                P.dma("pool", ohb[0:32], dc["oh"][var, :, sc * 4096:(sc + 1) * 4096].rearrange("p (s q) -> p s q", q=128), writes=[R_oh])
                bk = next_bank(0, 8)
                for s in range(32):
                    P.op("pe", lambda e, s=s, bk=bk: e.matmul(ps[bk][:, s * 16:(s + 1) * 16], ohb[0:32, s, :], rbh[0:32, :], start=True, stop=False),
                         reads=[R_oh, R_rb], writes=[R_ps[bk]], inc=False)
                    P.op("pe", lambda e, s=s, bk=bk: e.matmul(ps[bk][:, s * 16:(s + 1) * 16], ohb[0:32, s, :], rbl[0:32, :], start=False, stop=True),
                         reads=[R_oh, R_rb], writes=[R_ps[bk]], inc=(s == 31))
                P.op("act", lambda e, bk=bk: e.activation(out=etmp, in_=ps[bk][:, :], func=AF.Exp), reads=[R_ps[bk]], writes=[R_etmp])
                P.op("dve", lambda e, sc=sc: e.tensor_tensor(out=Etab[:, :, sc * 32:(sc + 1) * 32],
                                                              in0=etmp.rearrange("p (s h) -> p h s", h=16),
                                                              in1=validt[:, sc * 32:(sc + 1) * 32].unsqueeze(1).to_broadcast([128, 16, 32]),
                                                              op=ALU.mult), reads=[R_etmp, R_valid], writes=[R_E])
            P.dma("sp", e_dram[var], Etab[:].rearrange("p h s -> p (h s)"), reads=[R_E], writes=[R_edram])
        P.barrier()

        out_toks = []

        def run_group(gi, tiles):
            is_s = tiles == ["s"]
            NT = len(tiles)
            TOK = NT * 128
            var = 1 if is_s else 0
            first_tile = (not is_s) and tiles[0] == 0
            last_grp = (not is_s) and tiles[-1] == 15
            cd = cds[var]

            P.barrier()
            P.dma("sp", Etab[:].rearrange("p h s -> p (h s)"), e_dram[var], reads=[R_edram], writes=[R_E])
            for t, tl in enumerate(tiles):
                gt = 16 if is_s else tl
                P.dma("sp", rot[:, t, :], dc["rot"][:, gt * 192:(gt + 1) * 192], writes=[R_rot])
                src = din["xs"] if is_s else din["xp"][tl * 128:(tl + 1) * 128, :]
                P.dma("sp", xres[:, t, :], src, writes=[R_xres[t]])
            P.dma("sp", rett[:], dc["ret"][var], writes=[R_rett])
            pmsel = (3, 4, 5) if is_s else ((0, 1, 2))
            for i, pmi in enumerate(pmsel):
                P.dma("pool", pmt[:, i], dc["pm"][:, pmi * 512:(pmi + 1) * 512].rearrange("p (g t) -> p g t", t=128), writes=[R_pmt])

            def make_xT(t):
                P.op("act", lambda e, t=t: e.copy(out=xb[:], in_=xres[:, t, :]), reads=[R_xres[t]], writes=[R_xb])
                for hf in range(2):
                    bk = next_bank(0, 8)
                    transposes(bk, [xb[:, (hf * 8 + i) * 128:(hf * 8 + i + 1) * 128] for i in range(8)], [R_xb])
                    P.op("dve", lambda e, t=t, hf=hf, bk=bk: e.tensor_copy(out=xT[:, hf * 8:(hf + 1) * 8, t * 128:(t + 1) * 128],
                                                                         in_=psb[bk][:, :].rearrange("p (c q) -> p c q", q=128)),
                         reads=[R_ps[bk]], writes=[R_xT[t]])

            for t in range(NT):
                make_xT(t)

            for l in range(DEPTH):
                run_layer(l, tiles, is_s, NT, TOK, var, first_tile, last_grp, cd, make_xT)

        def run_layer(l, tiles, is_s, NT, TOK, var, first_tile, last_grp, cd, make_xT):
            Rx = R_xT[:NT]

            def inproj_tok(wb, Rw, t, ncols, c0=0):
                bk = next_bank(0, 2)
                mm_group(ps[bk][:, 0:ncols], [(xT[:, kc, t * 128:(t + 1) * 128], wb[:, kc, c0:c0 + ncols]) for kc in range(16)],
                         [R_xT[t], Rw], bk)
                return bk

            def inproj_feat(wb, Rw, lhs_fn, lo=0, hi=2):
                bk = next_bank(lo, hi)
                mm_group(ps[bk][:, 0:TOK], [(lhs_fn(kc), xT[:, kc, 0:TOK]) for kc in range(16)], Rx + [Rw], bk)
                return bk

            P.barrier()
            P.dma("sp", sink_bc[:], din["attn_sinks"][l, :].partition_broadcast(128), writes=[R_lp])
            P.dma("sp", psch[:], din["pool_scale"][l, :].rearrange("(c p) -> p c", p=128), writes=[R_lp], allow_slow_non_contiguous=True)
            P.op("dve", lambda e: e.tensor_scalar(out=nsink_bc[:], in0=sink_bc[:], scalar1=-1.0, scalar2=None, op0=ALU.mult), reads=[R_lp], writes=[R_lp])
            P.op("dve", lambda e: e.tensor_scalar(out=psch[:], in0=psch[:], scalar1=0.5, scalar2=None, op0=ALU.mult), reads=[R_lp], writes=[R_lp])

            if STOP_AT == "X":
                raise _Stop()
            q_rot = av(0, NT * 512).rearrange("p (t c) -> p t c", c=512)
            k_rot = av(2048, NT * 512).rearrange("p (t c) -> p t c", c=512)
            v_a = av(4096, NT * 1024).rearrange("p (t c) -> p t c", c=1024)
            sg_a = av(8192, NT * 1024).rearrange("p (t c) -> p t c", c=1024)
            k_dec = av(12288, 512)
            qT_a = av(12800, 512).rearrange("p (h i) -> p h i", i=128)
            kT_a = av(13312, 512).rearrange("p (h i) -> p h i", i=128)
            qdT_a = av(13824, 512).rearrange("p (h i) -> p h i", i=128)
            scT_a = av(14336, 512).rearrange("p (h i) -> p h i", i=128)
            za_tm = av(14848, 1024)
            rotB = av(15872, 1024, F32)
            R_qr = [AR_res(f"qrot{t}") for t in range(NT)]
            R_kr = [AR_res(f"krot{t}") for t in range(NT)]
            R_va = [AR_res(f"va{t}") for t in range(NT)]
            R_sga = [AR_res(f"sga{t}") for t in range(NT)]
            R_kdec, R_qT, R_kT, R_qdT, R_scT, R_zatm, R_rotB = (AR_res(n) for n in ("kdec", "qTa", "kTa", "qdTa", "scTa", "zatm", "rotB"))

            def rot_evac(bk, t, dst, Rdst):
                cosb = rot[:, t, 0:64].unsqueeze(1).to_broadcast([128, 8, 64])
                sinb = rot[:, t, 64:128].unsqueeze(1).to_broadcast([128, 4, 64])
                nsinb = rot[:, t, 128:192].unsqueeze(1).to_broadcast([128, 4, 64])
                x8 = ps[bk][:, :].rearrange("p (g d) -> p g d", d=64)
                x42 = ps[bk][:, :].rearrange("p (h two d) -> p h two d", two=2, d=64)
                B42 = rotB.rearrange("p (h two d) -> p h two d", two=2, d=64)
                P.op("dve", lambda e: e.tensor_tensor(out=tmpA[:, :].rearrange("p (g d) -> p g d", d=64), in0=x8, in1=cosb, op=ALU.mult),
                     reads=[R_ps[bk], R_rot], writes=[R_tmpA])
                P.op("dve", lambda e: e.tensor_tensor(out=B42[:, :, 0, :], in0=x42[:, :, 1, :], in1=nsinb, op=ALU.mult),
                     reads=[R_ps[bk], R_rot], writes=[R_rotB])
                P.op("dve", lambda e: e.tensor_tensor(out=B42[:, :, 1, :], in0=x42[:, :, 0, :], in1=sinb, op=ALU.mult),
                     reads=[R_ps[bk], R_rot], writes=[R_rotB])
                P.op("pool", lambda e: e.tensor_tensor(out=dst[:, t, :], in0=tmpA[:, :], in1=rotB, op=ALU.add),
                     reads=[R_tmpA, R_rotB], writes=[Rdst[t]])

            wb, Rw = w_get(("w_in", 0))
            for t in range(NT):
                bk = inproj_tok(wb, Rw, t, 512)
                rot_evac(bk, t, q_rot, R_qr)
            wb, Rw = w_get(("w_in", 512))
            for t in range(NT):
                bk = inproj_tok(wb, Rw, t, 512)
                rot_evac(bk, t, k_rot, R_kr)
            for c in range(2):
                wb, Rw = w_get(("w_in", 1024 + c * 512))
                for t in range(NT):
                    bk = inproj_tok(wb, Rw, t, 512)
                    P.op("act", lambda e, bk=bk, t=t, c=c: e.copy(out=v_a[:, t, c * 512:(c + 1) * 512], in_=ps[bk][:, :]),
                         reads=[R_ps[bk]], writes=[R_va[t]])
            for c in range(2):
                wb, Rw = w_get(("w_in", 2048 + c * 512))
                for t in range(NT):
                    bk = inproj_tok(wb, Rw, t, 512)
                    gate_evac(bk, 512, sg_a[:, t, c * 512:(c + 1) * 512], [R_sga[t]], t % 2)
            w_prefetch()

            dmT = rett[:, 0:512].rearrange("p (h i) -> p h i", i=128)
            qdec = rett[:, 512:1024].rearrange("p (h i) -> p h i", i=128)
            kdecs = rett[:, 1024:1028]
            for t in range(NT):
                transposes(2, [q_rot[:, t, h * 128:(h + 1) * 128] for h in range(4)] + [k_rot[:, t, h * 128:(h + 1) * 128] for h in range(4)],
                           [R_qr[t], R_kr[t]])
                pT3 = psb[2][:, :].rearrange("p (c q) -> p c q", q=128)
                P.op("act", lambda e, pT3=pT3: e.copy(out=qT_a, in_=pT3[:, 0:4, :]), reads=[R_ps[2]], writes=[R_qT])
                P.op("act", lambda e, pT3=pT3: e.copy(out=kT_a, in_=pT3[:, 4:8, :]), reads=[R_ps[2]], writes=[R_kT])
                P.op("dve", lambda e, pT3=pT3: e.tensor_tensor(out=qdT_a, in0=pT3[:, 0:4, :], in1=qdec, op=ALU.mult),
                     reads=[R_ps[2], R_rett], writes=[R_qdT])
                P.op("pool", lambda e, t=t: e.tensor_tensor(out=k_dec.rearrange("p (h d) -> p h d", d=128),
                                                            in0=k_rot[:, t, :].rearrange("p (h d) -> p h d", d=128),
                                                            in1=kdecs.unsqueeze(2).to_broadcast([128, 4, 128]), op=ALU.mult),
                     reads=[R_kr[t], R_rett], writes=[R_kdec])
                for h in range(4):
                    P.op("pe", lambda e, h=h: e.matmul(ps[3][:, h * 128:(h + 1) * 128], kT_a[:, h, :], qT_a[:, h, :], start=True, stop=True),
                         reads=[R_kT, R_qT], writes=[R_ps[3]], inc=(h == 3))
                P.op("dve", lambda e: e.tensor_tensor(out=scT_a, in0=ps[3][:, :].rearrange("p (h i) -> p h i", i=128), in1=dmT, op=ALU.mult),
                     reads=[R_ps[3], R_rett], writes=[R_scT])
                use_cross = not (first_tile and t == 0) and not is_s
                for h in range(4):
                    if is_s:
                        break
                    ob = 4 + h // 2
                    oap = ps[ob][:, (h % 2) * 256:(h % 2 + 1) * 256]
                    P.op("pe", lambda e, h=h, oap=oap, t=t, uc=use_cross: e.matmul(oap, scT_a[:, h, :], v_a[:, t, h * 256:(h + 1) * 256], start=True, stop=not uc),
                         reads=[R_scT, R_va[t]], writes=[R_ps[ob]], inc=(not use_cross) and (h % 2 == 1))
                    if use_cross:
                        P.op("pe", lambda e, h=h, oap=oap: e.matmul(oap, qdT_a[:, h, :], Sb[l][:, h, :], start=False, stop=True),
                             reads=[R_qdT, R_Sb[l]], writes=[R_ps[ob]], inc=(h % 2 == 1))
                if is_s:
                    sample_ret(l, scT_a, R_scT, v_a, R_va, qdT_a, R_qdT, k_dec, R_kdec, cd)
                else:
                    for h in range(4):
                        sbk = 6 + h // 2
                        P.op("pe", lambda e, h=h, sbk=sbk, t=t: e.matmul(ps[sbk][:, (h % 2) * 256:(h % 2 + 1) * 256], k_dec[:, h * 128:(h + 1) * 128],
                                                                       v_a[:, t, h * 256:(h + 1) * 256], start=True, stop=True),
                             reads=[R_kdec, R_va[t]], writes=[R_ps[sbk]], inc=(h % 2 == 1))
                    for h in range(4):
                        sbk = 6 + h // 2
                        P.op("dve", lambda e, h=h, sbk=sbk: e.scalar_tensor_tensor(out=S32[l][:, h, :], in0=S32[l][:, h, :], scalar=cd[h],
                                                                                   in1=ps[sbk][:, (h % 2) * 256:(h % 2 + 1) * 256], op0=ALU.mult, op1=ALU.add),
                             reads=[R_ps[sbk], R_S32[l]], writes=[R_S32[l]])
                    P.op("act", lambda e: e.copy(out=Sb[l][:], in_=S32[l][:]), reads=[R_S32[l]], writes=[R_Sb[l]])
                    if last_grp and t == NT - 1:
                        out_toks.append(P.dma("sp", dout["retp"][l].rearrange("h d v -> d h v"), S32[l][:], reads=[R_S32[l]]))
                for h in range(4):
                    ob = 4 + h // 2
                    P.op("act", lambda e, h=h, ob=ob: e.activation(out=tmpA[:, 0:256], in_=ps[ob][:, (h % 2) * 256:(h % 2 + 1) * 256], func=AF.Square,
                                                                   accum_out=ss[:, h:h + 1]),
                         reads=[R_ps[ob]], writes=[R_tmpA, R_small])
                P.op("dve", lambda e: e.tensor_scalar(out=sm1[:, 0:4], in0=ss[:, :], scalar1=4.0 / 256, scalar2=4.0 * RMS_EPS, op0=ALU.mult, op1=ALU.add),
                     reads=[R_small], writes=[R_small])
                P.op("pool", lambda e: e.tensor_tensor(out=sm2[:, 0:4], in0=sm1[:, 0:4], in1=mhalf[:, 0:4], op=ALU.pow),
                     reads=[R_small], writes=[R_small])
                for h in range(4):
                    ob = 4 + h // 2
                    P.op("dve", lambda e, h=h, ob=ob, t=t: e.scalar_tensor_tensor(out=za_tm[:, h * 256:(h + 1) * 256], in0=ps[ob][:, (h % 2) * 256:(h % 2 + 1) * 256],
                                                                                scalar=sm2[:, h:h + 1], in1=sg_a[:, t, h * 256:(h + 1) * 256], op0=ALU.mult, op1=ALU.mult),
                         reads=[R_ps[ob], R_small, R_sga[t]], writes=[R_zatm])
                transposes(2, [za_tm[:, c * 128:(c + 1) * 128] for c in range(8)], [R_zatm])
                P.op("act", lambda e, t=t: e.copy(out=zT[:, 0:8, t * 128:(t + 1) * 128], in_=psb[2][:, :].rearrange("p (c q) -> p c q", q=128)),
                     reads=[R_ps[2]], writes=[R_z[0][t]])

            if STOP_AT == "A":
                raise _Stop()
            P.barrier()
            u_all = av(0, (NT + 1) * 1024).rearrange("p (t c) -> p t c", c=1024)
            u32 = av(5120, 2048, F32)
            pT_b = av(7168, 1024).rearrange("p (c t) -> p c t", t=128)
            stb = av(8192, 2048).rearrange("p (a c) -> p a c", c=1024)
            R_u = [AR_res(f"u{t}") for t in range(NT + 1)]
            R_u32, R_pTb, R_stb = AR_res("u32"), AR_res("pTb"), AR_res("stb")
            if is_s:
                for hf in range(2):
                    P.dma("pool", stb[0:120, hf, :], din["st_pool"][l, hf * 8:(hf + 1) * 8].rearrange("b r c -> (b r) c"), writes=[R_stb])
            else:
                P.op("pool", lambda e: e.tensor_copy(out=u_all[:, 0, :], in_=u_prev[l][:]), reads=[R_uprev[l]], writes=[R_u[0]])
            want_u32 = is_s or last_grp
            for c in range(2):
                wb, Rw = w_get(("w_in", 3072 + c * 512))
                for t in range(NT):
                    bk = inproj_tok(wb, Rw, t, 512)
                    P.op("act", lambda e, bk=bk, t=t, c=c: e.copy(out=u_all[:, t + 1, c * 512:(c + 1) * 512], in_=ps[bk][:, :]),
                         reads=[R_ps[bk]], writes=[R_u[t + 1]])
                    if want_u32 and t == NT - 1:
                        lo = 0 if is_s else 64
                        P.op("dve", lambda e, bk=bk, c=c, lo=lo: e.tensor_copy(out=u32[lo:128, c * 512:(c + 1) * 512], in_=ps[bk][lo:128, :]),
                             reads=[R_ps[bk]], writes=[R_u32])
            if last_grp:
                out_toks.append(P.dma("sp", dout["plp"][l], u32[113:128, :], reads=[R_u32]))
            if is_s:
                for b in range(16):
                    out_toks.append(P.dma("sp", dout["pls"][l, b, 7:15, :], u32[b * 8:(b + 1) * 8, :], reads=[R_u32]))
                    out_toks.append(P.dma("sp", dout["pls"][l, b, 0:7, :], din["st_pool"][l, b, 8:15, :]))
            for c in range(2):
                wb, Rw = w_get(("w_in", 4096 + c * 512))
                for j in range(4):
                    bk = inproj_feat(wb, Rw, lambda kc, j=j, wb=wb: wb[:, kc, j * 128:(j + 1) * 128])
                    gate_evac(bk, TOK, zT[:, 8 + c * 4 + j, 0:TOK], R_z[1][:NT], j % 2)
            wb, Rw = w_get(("w_pool_map", 0))
            wmap = wb[:, 0:4, :].rearrange("p a (k d) -> p a k d", k=2)
            for t in range(NT):
                cur = 0 if (first_tile and t == 0) else 1
                if is_s:
                    cur = 0
                has_prev = not (first_tile and t == 0)
                for cc in range(8):
                    g = cc // 2
                    pb = 2 + cc // 4
                    oap = ps[pb][:, (cc % 4) * 128:(cc % 4 + 1) * 128]
                    pairs = [(u_all[:, t + 1, cc * 128:(cc + 1) * 128], pmt[:, cur, g, :])]
                    rds = [R_u[t + 1], R_pmt]
                    if is_s:
                        for hf in range(2):
                            pairs.append((stb[0:120, hf, cc * 128:(cc + 1) * 128], pmt[0:120, 1 + hf, g, :]))
                        rds.append(R_stb)
                    elif has_prev:
                        pairs.append((u_all[:, t, cc * 128:(cc + 1) * 128], pmt[:, 2, g, :]))
                        rds.append(R_u[t])
                    n = len(pairs)
                    for i, (lt, rh) in enumerate(pairs):
                        P.op("pe", lambda e, oap=oap, lt=lt, rh=rh, i=i, n=n: e.matmul(oap, lt, rh, start=(i == 0), stop=(i == n - 1)),
                             reads=rds, writes=[R_ps[pb]], inc=(i == n - 1) and (cc % 4 == 3))
                for hf in range(2):
                    P.op("act", lambda e, hf=hf: e.copy(out=pT_b[:, hf * 4:(hf + 1) * 4, :], in_=ps[2 + hf][:, :].rearrange("p (c t) -> p c t", t=128)),
                         reads=[R_ps[2 + hf]], writes=[R_pTb])
                for idx in range(8):
                    g, dcc = idx // 2, idx % 2
                    mb = 4 + idx // 4
                    oap = ps[mb][:, (idx % 4) * 128:(idx % 4 + 1) * 128]
                    for kc in range(2):
                        P.op("pe", lambda e, oap=oap, g=g, kc=kc, dcc=dcc: e.matmul(oap, wmap[:, g, kc, dcc * 128:(dcc + 1) * 128], pT_b[:, g * 2 + kc, :],
                                                                                   start=(kc == 0), stop=(kc == 1)),
                             reads=[Rw, R_pTb], writes=[R_ps[mb]], inc=(kc == 1) and (idx % 4 == 3))
                for idx in range(8):
                    mb = 4 + idx // 4
                    P.op("dve", lambda e, idx=idx, mb=mb, t=t: e.scalar_tensor_tensor(out=zT[:, 8 + idx, t * 128:(t + 1) * 128],
                                                                                    in0=ps[mb][:, (idx % 4) * 128:(idx % 4 + 1) * 128],
                                                                                    scalar=psch[:, idx:idx + 1], in1=zT[:, 8 + idx, t * 128:(t + 1) * 128],
                                                                                    op0=ALU.mult, op1=ALU.mult),
                         reads=[R_ps[mb], R_lp, R_z[1][t]], writes=[R_z[1][t]])
            if not is_s:
                P.op("pool", lambda e: e.tensor_copy(out=u_prev[l][:], in_=u_all[:, NT, :]), reads=[R_u[NT]], writes=[R_uprev[l]])

            if STOP_AT == "B":
                raise _Stop()
            P.barrier()
            if is_s:
                o_q, o_k, o_v, o_sg, o_e, o_p, o_pT, o_to, o_zc = 0, 1024, 2048, 2560, 3584, 5632, 6656, 7680, 8704
                Vc = av(9728, 4096).rearrange("p (b c) -> p b c", c=256)
                Kraw = av(13824, 2048).rearrange("p (b c) -> p b c", c=256)
                pTm = av(15872, 2048).rearrange("p (j h q) -> p j h q", j=4, h=4)
                qTm = xres[:, 1, :].bitcast(BF16).rearrange("p (j c q) -> p j c q", j=4, c=8)
                KTc = xres[:, 2:4, :].rearrange("p a b -> p (a b)").bitcast(BF16).rearrange("p (k s q) -> p k s q", k=4, s=16)
                R_Vc, R_Kraw, R_pTm, R_KTc = AR_res("Vc"), AR_res("Kraw"), AR_res("pTm"), AR_res("KTc")
            else:
                o_q, o_k, o_v, o_sg, o_e, o_p, o_pT, o_to, o_zc = 0, 4096, 6656, 7936, 12032, 14080, 15104, 16128, 17152
            qT_c = av(o_q, 8 * TOK).rearrange("p (c t) -> p c t", t=TOK)
            kT_c = av(o_k, 4 * (TOK + 128)).rearrange("p (c t) -> p c t", t=TOK + 128)
            v_all = av(o_v, (NT + 1) * 256).rearrange("p (t c) -> p t c", c=256)
            sgc = av(o_sg, NT * 1024).rearrange("p (t c) -> p t c", c=1024)
            e_c = av(o_e, 2048, F32).rearrange("p (h s) -> p h s", s=256)
            p_c = av(o_p, 1024).rearrange("p (h s) -> p h s", s=256)
            o_e2, o_p2 = (17920, 19968) if is_s else (18432, 20480)
            e_cs = [e_c, av(o_e2, 2048, F32).rearrange("p (h s) -> p h s", s=256)]
            p_cs = [p_c, av(o_p2, 1024).rearrange("p (h s) -> p h s", s=256)]
            pT_c = av(o_pT, 1024).rearrange("p (c q) -> p c q", q=128)
            tmpo = av(o_to, 1024, F32)
            zc_tm = av(o_zc, 1024)
            kv32 = av(o_e, 1024, F32)
            R_qTc, R_kTc, R_sgc = AR_res("qTc"), AR_res("kTc"), [AR_res(f"sgc{t}") for t in range(NT)]
            R_vall = [AR_res(f"vall{t}") for t in range(NT + 1)]
            R_ec, R_pc, R_pTc, R_tmpo, R_zctm = (AR_res(n) for n in ("ec", "pc", "pTc", "tmpo", "zctm"))
            R_ecs = [[AR_res(f"ec{i}_{h}") for h in range(4)] for i in range(2)]
            R_pcs = [[AR_res(f"pc{i}_{h}") for h in range(4)] for i in range(2)]
            if not is_s:
                P.op("pool", lambda e: e.tensor_copy(out=kT_c[:, :, 0:128], in_=kT_prev[l][:]), reads=[R_kTprev[l]], writes=[R_kTc])
                P.op("pool", lambda e: e.tensor_copy(out=v_all[:, 0, :], in_=v_prev[l][:]), reads=[R_vprev[l]], writes=[R_vall[0]])
            else:
                P.dma("pool", Vc, din["cv"][l].rearrange("b k c -> k b c"), writes=[R_Vc])
                for hs in range(2):
                    P.dma("pool", Kraw, din["ck"][l, hs * 8:(hs + 1) * 8].rearrange("b k c -> k b c"), writes=[R_Kraw])
                    for kv in range(4):
                        bk = next_bank(2, 8)
                        for b in range(8):
                            for hf in range(2):
                                P.op("pe", lambda e, bk=bk, b=b, hf=hf, kv=kv: e.transpose(psb[bk][hf * 64:(hf + 1) * 64, b * 128:(b + 1) * 128], Kraw[:, b, kv * 64:(kv + 1) * 64],
                                                                                       ident[:], tile_position=(0, hf * 64)),
                                     reads=[R_Kraw, R_ident], writes=[R_ps[bk]], inc=(b == 7 and hf == 1))
                        P.op("act", lambda e, bk=bk, kv=kv, hs=hs: e.copy(out=KTc[:, kv, hs * 8:(hs + 1) * 8, :], in_=psb[bk][:, :].rearrange("p (b q) -> p b q", q=128)),
                             reads=[R_ps[bk]], writes=[R_KTc, R_xres[2], R_xres[3]])
            for c in range(2):
                wb, Rw = w_get(("w_in", 5120 + c * 512))
                for hl in range(4):
                    lhs = lambda kc, hl=hl, wb=wb: wb[:, kc, hl * 128:(hl + 1) * 128]
                    bk = inproj_feat(wb, Rw, lhs)
                    P.op("act", lambda e, bk=bk, c=c, hl=hl: e.activation(out=qT_c[:, c * 4 + hl, :], in_=ps[bk][:, 0:TOK], func=AF.Copy, scale=0.125),
                         reads=[R_ps[bk]], writes=[R_qTc])
            if is_s:
                for j in range(4):
                    P.op("pool", lambda e, j=j: e.tensor_tensor(out=qTm[:, j], in0=qT_c, in1=colmask[:, j, :].unsqueeze(1).to_broadcast([128, 8, 128]), op=ALU.mult),
                         reads=[R_qTc, R_masks], writes=[R_xres[1]])
            wb, Rw = w_get(("w_in", 6144))
            for kv in range(4):
                bk = next_bank(0, 2)
                for hf in range(2):
                    for kc in range(16):
                        P.op("pe", lambda e, bk=bk, hf=hf, kc=kc, kv=kv, wb=wb: e.matmul(ps[bk][hf * 64:(hf + 1) * 64, 0:TOK], wb[:, kc, kv * 64:(kv + 1) * 64], xT[:, kc, 0:TOK],
                                                                                      start=(kc == 0), stop=(kc == 15), tile_position=(0, hf * 64)),
                             reads=Rx + [Rw], writes=[R_ps[bk]], inc=(kc == 15 and hf == 1))
                P.op("act", lambda e, bk=bk, kv=kv: e.copy(out=kT_c[:, kv, 128:128 + TOK], in_=ps[bk][:, 0:TOK]), reads=[R_ps[bk]], writes=[R_kTc])
            for t in range(NT):
                bk = inproj_tok(wb, Rw, t, 256, c0=256)
                P.op("act", lambda e, bk=bk, t=t: e.copy(out=v_all[:, t + 1, :], in_=ps[bk][:, 0:256]), reads=[R_ps[bk]], writes=[R_vall[t + 1]])
                if (last_grp or is_s) and t == NT - 1:
                    P.op("dve", lambda e, bk=bk: e.tensor_copy(out=kv32[:, 0:256], in_=ps[bk][:, 0:256]), reads=[R_ps[bk]], writes=[R_ec])
                    bk2 = inproj_tok(wb, Rw, t, 256, c0=0)
                    P.op("dve", lambda e, bk2=bk2: e.tensor_copy(out=kv32[:, 256:512], in_=ps[bk2][:, 0:256]), reads=[R_ps[bk2]], writes=[R_ec])
                    if last_grp:
                        out_toks.append(P.dma("sp", dout["wvp"][l], kv32[:, 0:256], reads=[R_ec]))
                        out_toks.append(P.dma("sp", dout["wkp"][l], kv32[:, 256:512], reads=[R_ec]))
                    else:
                        sample_kv_out(l, kv32, R_ec)
            for c in range(2):
                wb, Rw = w_get(("w_in", 6656 + c * 512))
                for t in range(NT):
                    bk = inproj_tok(wb, Rw, t, 512)
                    gate_evac(bk, 512, sgc[:, t, c * 512:(c + 1) * 512], [R_sgc[t]], t % 2)
            w_prefetch()
            if not is_s:
                P.op("pool", lambda e: e.tensor_copy(out=kT_prev[l][:], in_=kT_c[:, :, TOK:TOK + 128]), reads=[R_kTc], writes=[R_kTprev[l]])
                P.op("pool", lambda e: e.tensor_copy(out=v_prev[l][:], in_=v_all[:, NT, :]), reads=[R_vall[NT]], writes=[R_vprev[l]])
            for t in range(NT):
                koff = 128 if (first_tile and t == 0) else 0
                nh = 2 - koff // 128
                R_smk = [AR_res(f"smk{k}") for k in range(4)]
                R_rsk = [AR_res(f"rsk{k}") for k in range(4)]

                def st_scores(kvg):
                    sb0 = 2 if kvg % 2 == 0 else 0
                    for hl in range(4):
                        sbk = sb0 + hl % 2
                        oap = ps[sbk][:, (hl // 2) * 256 + koff:(hl // 2 + 1) * 256]
                        hh = kvg * 4 + hl
                        pq = (hh % 2) * 64
                        if is_s:
                            cb = hl // 2
                            P.op("pe", lambda e, sbk=sbk, cb=cb, pq=pq, hh=hh, kvg=kvg: e.matmul(ps[sbk][:, cb * 256 + 128:cb * 256 + 256], qT_c[pq:pq + 64, hh // 2, 0:128],
                                                                                               kT_c[pq:pq + 64, kvg, 128:256], start=True, stop=True),
                                 reads=[R_qTc, R_kTc], writes=[R_ps[sbk]], inc=False)
                            for Q in range(4):
                                for j in range(4):
                                    last = (Q == 3 and j == 3)
                                    P.op("pe", lambda e, sbk=sbk, cb=cb, pq=pq, hh=hh, kvg=kvg, Q=Q, j=j: e.matmul(
                                        ps[sbk][32 * Q:32 * Q + 32, cb * 256:cb * 256 + 128], qTm[pq:pq + 64, j, hh // 2, 32 * Q:32 * Q + 32], KTc[pq:pq + 64, kvg, 4 * Q + j, :],
                                        start=(j == 0), stop=(j == 3), tile_position=(pq, 32 * Q)),
                                        reads=[R_xres[1], R_KTc], writes=[R_ps[sbk]], inc=(last and hl >= 2))
                        else:
                            P.op("pe", lambda e, oap=oap, hh=hh, pq=pq, kvg=kvg, t=t, koff=koff: e.matmul(
                                oap, qT_c[pq:pq + 64, hh // 2, t * 128:(t + 1) * 128], kT_c[pq:pq + 64, kvg, t * 128 + koff:t * 128 + 256],
                                start=True, stop=True), reads=[R_qTc, R_kTc], writes=[R_ps[sbk]], inc=(hl >= 2))

                def st_max(kvg):
                    sb0 = 2 if kvg % 2 == 0 else 0
                    for b2 in range(2):
                        P.op("dve", lambda e, b2=b2, kvg=kvg, sb0=sb0, koff=koff: e.tensor_reduce(out=sm1[:, kvg * 4 + b2:kvg * 4 + 4:2],
                                                                                      in_=ps[sb0 + b2][:, :].rearrange("p (h s) -> p h s", s=256)[:, :, koff:256],
                                                                                      axis=AX.X, op=ALU.max),
                             reads=[R_ps[sb0 + b2]], writes=[R_smk[kvg]])
                    P.op("dve", lambda e, kvg=kvg: e.tensor_scalar(out=sm2[:, kvg * 4:kvg * 4 + 4], in0=sm1[:, kvg * 4:kvg * 4 + 4], scalar1=-1.0, scalar2=None, op0=ALU.mult),
                         reads=[R_smk[kvg]], writes=[R_smk[kvg]])
                    P.op("dve", lambda e, kvg=kvg: e.tensor_tensor(out=negm[:, kvg * 4:kvg * 4 + 4], in0=sm2[:, kvg * 4:kvg * 4 + 4], in1=nsink_bc[:, kvg * 4:kvg * 4 + 4], op=ALU.min),
                         reads=[R_smk[kvg], R_lp], writes=[R_smk[kvg]])

                def st_exp(kvg):
                    sb0 = 2 if kvg % 2 == 0 else 0
                    ec, pc, Rec, Rpc = e_cs[kvg % 2], p_cs[kvg % 2], R_ecs[kvg % 2], R_pcs[kvg % 2]
                    if kvg % 2 == 0:
                        Rec = [Rec[0], Rec[1], Rec[2], Rec[3]]
                    for hl in range(4):
                        h = kvg * 4 + hl
                        sbk = sb0 + hl % 2
                        P.op("act", lambda e, hl=hl, h=h, sbk=sbk, ec=ec, koff=koff: e.activation(out=ec[:, hl, koff:256], in_=ps[sbk][:, (hl // 2) * 256 + koff:(hl // 2 + 1) * 256],
                                                                                     func=AF.Exp, bias=negm[:, h:h + 1], scale=1.0),
                             reads=[R_ps[sbk], R_smk[kvg]], writes=[Rec[hl]] + ([R_ec] if kvg % 2 == 0 and hl < 2 else []))
                        P.op("dve", lambda e, hl=hl, h=h, ec=ec, pc=pc, koff=koff: e.scalar_tensor_tensor(out=pc[:, hl, koff:256], in0=ec[:, hl, koff:256], scalar=1.0,
                                                                                             in1=Etab[:, h, koff:256], op0=ALU.mult, op1=ALU.mult, accum_out=rs[:, h:h + 1]),
                             reads=[Rec[hl], R_E], writes=[Rpc[hl], R_rsk[kvg]])

                def st_tr(kvg):
                    tbk = 4 if kvg % 2 == 0 else 7
                    pc, Rpc = p_cs[kvg % 2], R_pcs[kvg % 2]
                    srcs = []
                    for hl in range(4):
                        for h2 in range(koff // 128, 2):
                            srcs.append(pc[:, hl, h2 * 128:(h2 + 1) * 128])
                    transposes(tbk, srcs, list(Rpc))

                def st_pv(kvg):
                    tbk = 4 if kvg % 2 == 0 else 7
                    nsl = 4 * nh
                    P.op("act", lambda e, nsl=nsl, tbk=tbk: e.copy(out=pT_c[:, 0:nsl, :], in_=psb[tbk][:, 0:nsl * 128].rearrange("p (c q) -> p c q", q=128)),
                         reads=[R_ps[tbk]], writes=[R_pTc])
                    if is_s:
                        for j in range(4):
                            P.op("dve", lambda e, j=j: e.tensor_tensor(out=pTm[:, j], in0=pT_c[:, 0:8:2, :], in1=colmask[:, j, :].unsqueeze(1).to_broadcast([128, 4, 128]), op=ALU.mult),
                                 reads=[R_pTc, R_masks], writes=[R_pTm])
                    for hl in range(4):
                        h = kvg * 4 + hl
                        ob = 5 + h // 8
                        oap = ps[ob][:, (h % 8) * 64:(h % 8 + 1) * 64]
                        if is_s:
                            P.op("pe", lambda e, oap=oap, hl=hl, kvg=kvg: e.matmul(oap, pT_c[:, hl * 2 + 1, :], v_all[:, 1, kvg * 64:(kvg + 1) * 64], start=True, stop=False),
                                 reads=[R_pTc, R_vall[1]], writes=[R_ps[ob]], inc=False)
                            for Q in range(4):
                                for j in range(4):
                                    last = (Q == 3 and j == 3)
                                    P.op("pe", lambda e, ob=ob, h=h, hl=hl, kvg=kvg, Q=Q, j=j, last=last: e.matmul(
                                        ps[ob][32 * Q:32 * Q + 32, (h % 8) * 64:(h % 8 + 1) * 64], pTm[:, j, hl, 32 * Q:32 * Q + 32], Vc[:, 4 * Q + j, kvg * 64:(kvg + 1) * 64],
                                        start=False, stop=(j == 3), tile_position=(0, 32 * Q)),
                                        reads=[R_pTm, R_Vc], writes=[R_ps[ob]], inc=(last and hl == 3))
                        else:
                            for i2, h2 in enumerate(range(koff // 128, 2)):
                                P.op("pe", lambda e, oap=oap, hl=hl, i2=i2, h2=h2, kvg=kvg, t=t, nh=nh: e.matmul(
                                    oap, pT_c[:, hl * nh + i2, :], v_all[:, t + h2, kvg * 64:(kvg + 1) * 64], start=(i2 == 0), stop=(i2 == nh - 1)),
                                    reads=[R_pTc, R_vall[t + h2]], writes=[R_ps[ob]], inc=(i2 == nh - 1) and (hl == 3))

                st_scores(0)
                for kvg in range(4):
                    if kvg + 1 < 4:
                        st_scores(kvg + 1)
                    st_max(kvg)
                    if kvg >= 1:
                        st_pv(kvg - 1)
                    st_exp(kvg)
                    st_tr(kvg)
                st_pv(3)
                P.op("dve", lambda e: e.tensor_tensor(out=sm3[:, :], in0=sink_bc[:, :], in1=negm[:, :], op=ALU.add), reads=[R_small, R_lp] + R_smk, writes=[R_small])
                P.op("act", lambda e: e.activation(out=sm4[:, :], in_=sm3[:, :], func=AF.Exp), reads=[R_small], writes=[R_small])
                P.op("dve", lambda e: e.tensor_tensor(out=sm3[:, :], in0=sm4[:, :], in1=rs[:, :], op=ALU.add), reads=[R_small] + R_smk + R_rsk, writes=[R_small])
                P.op("dve", lambda e: e.reciprocal(out=sm4[:, :], in_=sm3[:, :]), reads=[R_small], writes=[R_small])
                P.op("dve", lambda e: e.tensor_scalar(out=sm3[:, :], in0=sm4[:, :], scalar1=0.5, scalar2=None, op0=ALU.mult), reads=[R_small], writes=[R_small])
                for b2 in range(2):
                    P.op("dve", lambda e, b2=b2: e.tensor_tensor(out=tmpo.rearrange("p (h d) -> p h d", d=64), in0=ps[5 + b2][:, :].rearrange("p (h d) -> p h d", d=64),
                                                                 in1=sm3[:, b2 * 8:(b2 + 1) * 8].unsqueeze(2).to_broadcast([128, 8, 64]), op=ALU.mult),
                         reads=[R_ps[5 + b2], R_small], writes=[R_tmpo])
                    P.op("pool", lambda e, b2=b2, t=t: e.tensor_tensor(out=zc_tm[:, b2 * 512:(b2 + 1) * 512], in0=tmpo, in1=sgc[:, t, b2 * 512:(b2 + 1) * 512], op=ALU.mult),
                         reads=[R_tmpo, R_sgc[t]], writes=[R_zctm])
                transposes(7, [zc_tm[:, c * 128:(c + 1) * 128] for c in range(8)], [R_zctm])
                P.op("act", lambda e, t=t: e.copy(out=zT[:, 16:24, t * 128:(t + 1) * 128], in_=psb[7][:, :].rearrange("p (c q) -> p c q", q=128)),
                     reads=[R_ps[7]], writes=[R_z[2][t]])

            if STOP_AT == "C":
                raise _Stop()
            P.barrier()
            mT = av(0, 16 * TOK).rearrange("p (c t) -> p c t", t=TOK)
            acc = av(8192, 4 * TOK * 2, F32).rearrange("p (c t) -> p c t", t=TOK)
            t2 = av(12288, TOK * 2, F32)
            R_mT, R_acc, R_t2 = [AR_res(f"mT{t}") for t in range(NT)], AR_res("acc"), AR_res("t2")
            for sc4 in range(4):
                for br, (wo, m0) in enumerate((("w_ret_o", 7680), ("w_pool_o", 9728), ("w_att_o", 11776))):
                    wbo, Rwo = w_get((wo, sc4 * 512))
                    for cl in range(4):
                        mm_group(ps[cl][:, 0:TOK], [(wbo[:, kc, cl * 128:(cl + 1) * 128], zT[:, br * 8 + kc, 0:TOK]) for kc in range(8)],
                                 R_z[br][:NT] + [Rwo], cl)
                    wbm, Rwm = w_get(("w_in", m0 + sc4 * 512))
                    for cl in range(4):
                        c = sc4 * 4 + cl
                        yb = cl
                        mb = next_bank(4, 8)
                        mm_group(ps[mb][:, 0:TOK], [(wbm[:, kc, cl * 128:(cl + 1) * 128], xT[:, kc, 0:TOK]) for kc in range(16)], Rx + [Rwm], mb)
                        thi = cl % 2
                        P.op("act", lambda e, mb=mb, thi=thi: e.activation(out=th[thi][:, 0:TOK], in_=ps[mb][:, 0:TOK], func=AF.Tanh, scale=0.5),
                             reads=[R_ps[mb]], writes=[R_th[thi]])
                        if br == 0:
                            P.op("dve", lambda e, yb=yb, thi=thi, cl=cl: e.scalar_tensor_tensor(out=acc[:, cl, :], in0=th[thi][:, 0:TOK], scalar=1.0, in1=ps[yb][:, 0:TOK],
                                                                                              op0=ALU.add, op1=ALU.mult),
                                 reads=[R_th[thi], R_ps[yb]], writes=[R_acc])
                        else:
                            P.op("dve", lambda e, yb=yb, thi=thi: e.scalar_tensor_tensor(out=t2, in0=th[thi][:, 0:TOK], scalar=1.0, in1=ps[yb][:, 0:TOK],
                                                                                       op0=ALU.add, op1=ALU.mult),
                                 reads=[R_th[thi], R_ps[yb]], writes=[R_t2])
                            if br == 1:
                                P.op("pool", lambda e, cl=cl: e.tensor_tensor(out=acc[:, cl, :], in0=acc[:, cl, :], in1=t2, op=ALU.add),
                                     reads=[R_t2, R_acc], writes=[R_acc])
                            else:
                                P.op("pool", lambda e, cl=cl, c=c: e.tensor_tensor(out=mT[:, c, :], in0=acc[:, cl, :], in1=t2, op=ALU.add),
                                     reads=[R_t2, R_acc], writes=R_mT)

            if STOP_AT == "D":
                raise _Stop()
            P.barrier()
            g_bc = av(8192, 4096, F32)
            b_bc = av(12288, 4096, F32)
            R_gb = AR_res("gbc")
            P.dma("sp", g_bc, din["ln_g"][l, :].partition_broadcast(128), writes=[R_gb])
            P.dma("sp", b_bc, din["ln_b"][l, :].partition_broadcast(128), writes=[R_gb])
            for c in range(4):
                wb, Rw = w_get(("w_out", c * 512))
                for t in range(NT):
                    bk = next_bank(0, 8)
                    mm_group(ps[bk][:, :], [(mT[:, kc, t * 128:(t + 1) * 128], wb[:, kc, :]) for kc in range(16)], [R_mT[t], Rw], bk)
                    P.op("dve", lambda e, bk=bk, t=t, c=c: e.scalar_tensor_tensor(out=xres[:, t, c * 512:(c + 1) * 512], in0=xres[:, t, c * 512:(c + 1) * 512],
                                                                                scalar=2.0 * ALPHA, in1=ps[bk][:, :], op0=ALU.mult, op1=ALU.add),
                         reads=[R_ps[bk], R_xres[t]], writes=[R_xres[t]])
            w_prefetch()
            R_lnst = [AR_res("lnst0"), AR_res("lnst1")]
            for t in range(NT):
                st_, mv_, Rst, c0 = ((stats, mv, R_lnst[0], 0) if t % 2 == 0 else (stats2, mv2, R_lnst[1], 4))
                for c in range(4):
                    P.op("dve", lambda e, t=t, c=c, st_=st_: e.bn_stats(out=st_[:, c, :], in_=xres[:, t, c * 512:(c + 1) * 512]), reads=[R_xres[t]], writes=[Rst])
                P.op("dve", lambda e, st_=st_, mv_=mv_: e.bn_aggr(out=mv_[:, :], in_=st_[:, :, :]), reads=[Rst], writes=[Rst])
                P.op("dve", lambda e, mv_=mv_, c0=c0: e.tensor_scalar(out=sm1[:, c0 + 2:c0 + 3], in0=mv_[:, 1:2], scalar1=4.0 * LN_EPS, scalar2=None, op0=ALU.add),
                     reads=[Rst], writes=[Rst])
                P.op("pool", lambda e, c0=c0: e.tensor_tensor(out=sm1[:, c0:c0 + 1], in0=sm1[:, c0 + 2:c0 + 3], in1=mhalf[:, 0:1], op=ALU.pow),
                     reads=[Rst], writes=[Rst])
                P.op("dve", lambda e, mv_=mv_, c0=c0: e.scalar_tensor_tensor(out=sm1[:, c0 + 1:c0 + 2], in0=mv_[:, 0:1], scalar=-1.0, in1=sm1[:, c0:c0 + 1], op0=ALU.mult, op1=ALU.mult),
                     reads=[Rst], writes=[Rst])
                P.op("act", lambda e, t=t, c0=c0: e.activation(out=xres[:, t, :], in_=xres[:, t, :], func=AF.Identity, bias=sm1[:, c0 + 1:c0 + 2], scale=sm1[:, c0:c0 + 1]),
                     reads=[R_xres[t], Rst], writes=[R_xres[t]])
                P.op("dve", lambda e, t=t: e.tensor_tensor(out=xres[:, t, :], in0=xres[:, t, :], in1=g_bc, op=ALU.mult), reads=[R_xres[t], R_gb], writes=[R_xres[t]])
                P.op("pool", lambda e, t=t: e.tensor_tensor(out=xres[:, t, :], in0=xres[:, t, :], in1=b_bc, op=ALU.add), reads=[R_xres[t], R_gb], writes=[R_xres[t]])
                def finish(t):
                    if l == DEPTH - 1:
                        dst = dout["ys"] if is_s else dout["yp"][tiles[t] * 128:(tiles[t] + 1) * 128, :]
                        out_toks.append(P.dma("sp", dst, xres[:, t, :], reads=[R_xres[t]]))
                    else:
                        make_xT(t)
                if t >= 1:
                    finish(t - 1)
                if t == NT - 1:
                    finish(t)

        def sample_ret(l, scT_a, R_scT, v_a, R_va, qdT_a, R_qdT, k_dec, R_kdec, cd):
            qdTm = av(5120, 2048).rearrange("p (j h i) -> p j h i", j=4, h=4)
            kdm = av(9216, 2048).rearrange("p (j c) -> p j c", j=4)
            R_qdTm, R_kdm = AR_res("qdTm"), AR_res("kdm")
            for j in range(4):
                P.op("pool", lambda e, j=j: e.tensor_tensor(out=qdTm[:, j], in0=qdT_a, in1=colmask[:, j, :].unsqueeze(1).to_broadcast([128, 4, 128]), op=ALU.mult),
                     reads=[R_qdT, R_masks], writes=[R_qdTm])
                P.op("pool", lambda e, j=j: e.tensor_scalar(out=kdm[:, j, :], in0=k_dec, scalar1=rowmask[:, j:j + 1], scalar2=None, op0=ALU.mult),
                     reads=[R_kdec, R_masks], writes=[R_kdm])
            S_old = [xres[:, 1 + i, :].rearrange("p (b v) -> p b v", v=256) for i in range(2)]
            S16f = xres[:, 3, :].bitcast(BF16)
            S16 = [S16f[:, i * 2048:(i + 1) * 2048].rearrange("p (b v) -> p b v", v=256) for i in range(2)]
            R_So = [R_xres[1], R_xres[2]]
            R_S16 = [AR_res("S16_0"), AR_res("S16_1")]
            banks = [6, 7, 0, 1]
            it = 0

            def load_state(i):
                hh_, hf_ = i // 2, i % 2
                P.dma("sp", S_old[i % 2], din["st_ret"][l, hf_ * 8:(hf_ + 1) * 8, hh_].rearrange("b d v -> d b v"), writes=[R_So[i % 2]])

            load_state(0)
            for h in range(4):
                ob = 4 + h // 2
                c0 = (h % 2) * 256
                P.op("pe", lambda e, ob=ob, c0=c0, h=h: e.matmul(ps[ob][:, c0:c0 + 256], scT_a[:, h, :], v_a[:, 0, h * 256:(h + 1) * 256], start=True, stop=False),
                     reads=[R_scT, R_va[0]], writes=[R_ps[ob]], inc=False)
                for half in range(2):
                    bi = it % 2
                    it += 1
                    if it < 8:
                        load_state(it)
                    P.op("act", lambda e, bi=bi: e.copy(out=S16[bi], in_=S_old[bi]), reads=[R_So[bi]], writes=[R_S16[bi]])
                    for b in range(8):
                        seq = half * 8 + b
                        Q, j = seq // 4, seq % 4
                        last = (half == 1 and b == 7)
                        P.op("pe", lambda e, ob=ob, c0=c0, h=h, Q=Q, j=j, b=b, bi=bi, last=last: e.matmul(
                            ps[ob][32 * Q:32 * Q + 32, c0:c0 + 256], qdTm[:, j, h, 32 * Q:32 * Q + 32], S16[bi][:, b, :], start=False, stop=(j == 3), tile_position=(0, 32 * Q)),
                            reads=[R_qdTm, R_S16[bi]], writes=[R_ps[ob]], inc=last)
                    for b in range(8):
                        seq = half * 8 + b
                        Q, j = seq // 4, seq % 4
                        bk = banks[b // 2]
                        P.op("pe", lambda e, bk=bk, b=b, Q=Q, j=j, h=h: e.matmul(
                            ps[bk][:, (b % 2) * 256:(b % 2 + 1) * 256], kdm[32 * Q:32 * Q + 32, j, h * 128:(h + 1) * 128], v_a[32 * Q:32 * Q + 32, 0, h * 256:(h + 1) * 256],
                            start=True, stop=True, tile_position=(32 * Q, 0)),
                            reads=[R_kdm, R_va[0]], writes=[R_ps[bk]], inc=(b % 2 == 1))
                    for i in range(4):
                        bk = banks[i]
                        sv = S_old[bi][:, 2 * i:2 * i + 2, :]
                        P.op("dve", lambda e, bk=bk, sv=sv, h=h: e.scalar_tensor_tensor(out=sv, in0=sv, scalar=cd[h], in1=ps[bk][:, :].rearrange("p (b v) -> p b v", v=256),
                                                                                      op0=ALU.mult, op1=ALU.add),
                             reads=[R_ps[bk], R_So[bi]], writes=[R_So[bi]])
                    out_toks.append(P.dma("sp", dout["rets"][l, half * 8:(half + 1) * 8, h].rearrange("b d v -> d b v"), S_old[bi], reads=[R_So[bi]]))

        def sample_kv_out(l, kv32, R_ec):
            out_toks.append(P.dma("sp", dout["wvs"][l, :, 0:120, :], din["cv"][l, :, 8:128, :]))
            out_toks.append(P.dma("sp", dout["wks"][l, :, 0:120, :], din["ck"][l, :, 8:128, :]))
            for b in range(16):
                out_toks.append(P.dma("sp", dout["wvs"][l, b, 120:128, :], kv32[b * 8:(b + 1) * 8, 0:256], reads=[R_ec]))
                out_toks.append(P.dma("sp", dout["wks"][l, b, 120:128, :], kv32[b * 8:(b + 1) * 8, 256:512], reads=[R_ec]))

        try:
            for gi, tiles in enumerate(GROUPS):
                if STOP_AT == "setup":
                    raise _Stop()
                if tiles == ["s"] and not ENABLE_SAMPLE:
                    continue
                if GROUP_SEL is not None and gi not in GROUP_SEL:
                    continue
                run_group(gi, tiles)
                if STOP_AT == "G0":
                    raise _Stop()
        except _Stop:
            pass

        for tk in out_toks:
            if tk is not None:
                P.wait("sp", tk)
        print("sbuf bytes remaining", nc.sbuf_bytes_remaining, flush=True)
        P.emit()
    return nc


_CACHE = {}


def kernel(x_prompt, x_sample, state_ret, cache_win_k, cache_win_v, state_pool, w_in, w_ret_o, w_pool_map,
           pool_scale, w_pool_o, attn_sinks, w_att_o, w_out, ln_g, ln_b, rel_bias):
    f = lambda a: np.ascontiguousarray(np.asarray(a, dtype=np.float32))
    if "nc" not in _CACHE:
        _CACHE["nc"] = build_program()
        _CACHE["consts"] = make_consts()
    nc = _CACHE["nc"]
    consts = _CACHE["consts"]
    shared = {"w_in": f(w_in), "w_ret_o": f(w_ret_o), "w_pool_map": f(w_pool_map), "pool_scale": f(pool_scale),
              "w_pool_o": f(w_pool_o), "attn_sinks": f(attn_sinks), "w_att_o": f(w_att_o), "w_out": f(w_out),
              "ln_g": f(ln_g), "ln_b": f(ln_b), "rel_bias": f(rel_bias)}
    for k in _CONST_SHAPES:
        shared["c_" + k] = f(consts[k]).reshape(_CONST_SHAPES[k])
    xp, xs = f(x_prompt), f(x_sample)
    sr, ck, cv, sp = f(state_ret), f(cache_win_k), f(cache_win_v), f(state_pool)
    in_maps = []
    for c in range(8):
        m = dict(shared)
        b0, b1 = 16 * c, 16 * (c + 1)
        m["xp"] = xp[c]
        m["xs"] = xs[b0:b1].reshape(128, D)
        m["st_ret"] = np.ascontiguousarray(sr[:, b0:b1])
        m["ck"] = np.ascontiguousarray(ck[:, b0:b1].reshape(2, 16, 128, 256))
        m["cv"] = np.ascontiguousarray(cv[:, b0:b1].reshape(2, 16, 128, 256))
        m["st_pool"] = np.ascontiguousarray(sp[:, b0:b1])
        in_maps.append(m)
    res = run_bass_kernel_spmd(nc, in_maps, core_ids=list(range(8)))
    r = res.results
    cat = lambda k, ax: np.concatenate([r[c][k] for c in range(8)], axis=ax)
    y_prompt = np.stack([r[c]["yp"] for c in range(8)], 0)
    y_sample = cat("ys", 0).reshape(128, 8, D)
    ret_p = np.stack([r[c]["retp"] for c in range(8)], 1)
    ret_s = cat("rets", 1)
    wk_p = np.stack([r[c]["wkp"] for c in range(8)], 1).reshape(2, 8, 128, 4, 64)
    wk_s = cat("wks", 1).reshape(2, 128, 128, 4, 64)
    wv_p = np.stack([r[c]["wvp"] for c in range(8)], 1).reshape(2, 8, 128, 4, 64)
    wv_s = cat("wvs", 1).reshape(2, 128, 128, 4, 64)
    pl_p = np.stack([r[c]["plp"] for c in range(8)], 1)
    pl_s = cat("pls", 1)
    return (y_prompt, y_sample, ret_p, ret_s, wk_p, wk_s, wv_p, wv_s, pl_p, pl_s)
```
